# Optimizing a Trainium2 kernel written in Bass

```python
import math
import jax
import jax.numpy as jnp
from jax import lax
import numpy as np

D_MODEL = 1024
BATCH = 8
SEQ = 2048
DEPTH = 4

CTX_LEN = 256
GRID_W = 64
RMS_EPS = 1e-6
N_MOD = 9
D_FF = 2816

A_HEADS = 4
A_HEAD_DIM = 64
A_WIDTH = A_HEADS * A_HEAD_DIM
GLR_CHUNK = 16
B_Q_HEADS = 8
B_KV_HEADS = 2
B_GROUP = B_Q_HEADS // B_KV_HEADS
B_HEAD_DIM = 64
B_WIDTH = B_Q_HEADS * B_HEAD_DIM
B_KV_WIDTH = B_KV_HEADS * B_HEAD_DIM
WINDOW = 128
ATTN_BLOCK = 128
ROPE_BASE = 10000.0
ROPE_PAIRS = B_HEAD_DIM // 4
MASK_VALUE = -1e9
C_GROUPS = 16
C_GROUP_CH = 16
C_WIDTH = C_GROUPS * C_GROUP_CH
C_STATE = 64

D_MIX = A_WIDTH + B_WIDTH + C_WIDTH
IN_WIDTHS = (A_WIDTH, A_WIDTH, A_WIDTH, A_WIDTH, A_WIDTH, B_WIDTH, B_KV_WIDTH, B_KV_WIDTH, C_WIDTH)
D_IN = sum(IN_WIDTHS)

kernel_name = 'hybrid_hgrn2_swa_s5_macaron_adaln'


def rms_norm(x, g):
    x32 = x.astype(jnp.float32)
    y = x32 * lax.rsqrt(jnp.mean(x32 * x32, axis=-1, keepdims=True) + RMS_EPS)
    return (y * g.astype(jnp.float32)).astype(x.dtype)


def adaln(h, g, m, i):
    return rms_norm(h, g) * (1 + m[:, :, i, 1]) + m[:, :, i, 0]


def swiglu(h, w1, w2):
    gate, up = jnp.split(h @ w1, 2, axis=-1)
    return (jax.nn.silu(gate) * up) @ w2


def split_columns(p):
    points, acc = [], 0
    for w in IN_WIDTHS[:-1]:
        acc += w
        points.append(acc)
    return jnp.split(p, points, axis=-1)


def axial_rope_tables(length):
    rows = length // GRID_W
    row = jnp.repeat(jnp.arange(rows), GRID_W, total_repeat_length=length)
    col = jnp.tile(jnp.arange(GRID_W), rows)
    inv_freq = ROPE_BASE ** (-jnp.arange(ROPE_PAIRS, dtype=jnp.float32) / ROPE_PAIRS)
    ang = jnp.stack([row.astype(jnp.float32)[:, None] * inv_freq,
                     col.astype(jnp.float32)[:, None] * inv_freq], axis=1)
    return jnp.cos(ang), jnp.sin(ang)


def apply_axial_rope(x, cos, sin):
    b, l, h, d = x.shape
    xr = x.reshape(b, l, h, 2, 2, d // 4)
    x1, x2 = xr[..., 0, :], xr[..., 1, :]
    c = cos[None, :, None].astype(x.dtype)
    s = sin[None, :, None].astype(x.dtype)
    out = jnp.stack([x1 * c - x2 * s, x2 * c + x1 * s], axis=-2)
    return out.reshape(b, l, h, d)


def hgrn2_gates(z, lb):
    z32 = z.astype(jnp.float32)
    f = lb + (1.0 - lb) * jax.nn.sigmoid(z32)
    logf = jnp.log(f)
    k = (1.0 - lb) * jax.nn.sigmoid(-z32)
    return logf, k


def to_heads(a):
    bsz, length, _ = a.shape
    return a.astype(jnp.float32).reshape(bsz, length, A_HEADS, A_HEAD_DIM).transpose(0, 2, 1, 3)


def gated_linear_recurrence(q, k, v, logf, s0):
    bsz, h, length, dk = q.shape
    dv = v.shape[-1]
    n = length // GLR_CHUNK
    q, k, logf = (a.reshape(bsz, h, n, GLR_CHUNK, dk) for a in (q, k, logf))
    v = v.reshape(bsz, h, n, GLR_CHUNK, dv)
    b = jnp.cumsum(logf, axis=3)
    b_last = b[:, :, :, -1]
    u = jnp.einsum('bhnsk,bhnsv->bhnkv', k * jnp.exp(b_last[:, :, :, None] - b), v)

    def step(s, inp):
        dec, un = inp
        return dec[..., None] * s + un, s

    s_fin, s_start = lax.scan(step, s0, (jnp.moveaxis(jnp.exp(b_last), 2, 0), jnp.moveaxis(u, 2, 0)))
    s_start = jnp.moveaxis(s_start, 0, 2)
    o_inter = jnp.einsum('bhntk,bhnkv->bhntv', q * jnp.exp(b), s_start)
    lower = jnp.tril(jnp.ones((GLR_CHUNK, GLR_CHUNK), dtype=bool))[..., None]
    diff = b[:, :, :, :, None, :] - b[:, :, :, None, :, :]
    decay = jnp.where(lower, jnp.exp(jnp.where(lower, diff, 0.0)), 0.0)
    scores = jnp.einsum('bhntk,bhnsk,bhntsk->bhnts', q, k, decay)
    o = o_inter + jnp.einsum('bhnts,bhnsv->bhntv', scores, v)
    return o.reshape(bsz, h, length, dv), s_fin


def hgrn2_direction(lat, ctx_side, reverse):
    if reverse:
        lat = [jnp.flip(a, axis=2) for a in lat]
        ctx_side = [jnp.flip(a, axis=2) for a in ctx_side]
    qc, kc, vc, logfc = ctx_side
    s0 = jnp.zeros(qc.shape[:2] + (A_HEAD_DIM, A_HEAD_DIM), jnp.float32)
    oc, s_ctx = gated_linear_recurrence(qc, kc, vc, logfc, s0)
    o, _ = gated_linear_recurrence(lat[0], lat[1], lat[2], lat[3], s_ctx)
    if reverse:
        o, oc = jnp.flip(o, axis=2), jnp.flip(oc, axis=2)
    return o, oc


def hgrn2_readout(o, g, norm_g):
    o = o * lax.rsqrt(jnp.mean(o * o, axis=-1, keepdims=True) + RMS_EPS)
    bsz, _, length, _ = o.shape
    o = o.transpose(0, 2, 1, 3).reshape(bsz, length, A_WIDTH)
    return (o * norm_g.astype(jnp.float32) * jax.nn.silu(g.astype(jnp.float32))).astype(g.dtype)


def hgrn2_mixer(lat, ctx_side, lb, norm_g):
    q, v, zf, zb, g = lat
    qc, vc, zfc, zbc, gc = ctx_side
    q_h, v_h = to_heads(jax.nn.silu(q)), to_heads(v)
    qc_h, vc_h = to_heads(jax.nn.silu(qc)), to_heads(vc)
    logf_f, k_f = hgrn2_gates(zf, lb[0])
    logfc_f, kc_f = hgrn2_gates(zfc, lb[0])
    logf_b, k_b = hgrn2_gates(zb, lb[1])
    logfc_b, kc_b = hgrn2_gates(zbc, lb[1])
    o_f, oc_f = hgrn2_direction([q_h, to_heads(k_f), v_h, to_heads(logf_f)],
                                [qc_h, to_heads(kc_f), vc_h, to_heads(logfc_f)], False)
    o_b, oc_b = hgrn2_direction([q_h, to_heads(k_b), v_h, to_heads(logf_b)],
                                [qc_h, to_heads(kc_b), vc_h, to_heads(logfc_b)], True)
    return hgrn2_readout(o_f + o_b, g, norm_g), hgrn2_readout(oc_f + oc_b, gc, norm_g)


def sink_softmax(logits, sink):
    s = sink.astype(jnp.float32).reshape((1,) + sink.shape + (1,) * (logits.ndim - 3))
    s = jnp.broadcast_to(s, logits.shape[:-1] + (1,))
    p = jax.nn.softmax(jnp.concatenate([logits, s], axis=-1), axis=-1)
    return p[..., :-1]


def window_gqa_mixer(lat, ctx_side, sink, cos, sin):
    q, k, v = lat
    qc, kc, vc = ctx_side
    bsz, length, _ = q.shape
    lc = qc.shape[1]
    nb = length // ATTN_BLOCK
    band = 3 * ATTN_BLOCK
    scale = B_HEAD_DIM ** -0.5
    sink_g = sink.reshape(B_KV_HEADS, B_GROUP)
    q = apply_axial_rope(q.reshape(bsz, length, B_Q_HEADS, B_HEAD_DIM), cos, sin)
    k = apply_axial_rope(k.reshape(bsz, length, B_KV_HEADS, B_HEAD_DIM), cos, sin)
    v = v.reshape(bsz, length, B_KV_HEADS, B_HEAD_DIM)
    qc = qc.reshape(bsz, lc, B_KV_HEADS, B_GROUP, B_HEAD_DIM)
    kc = kc.reshape(bsz, lc, B_KV_HEADS, B_HEAD_DIM)
    vc = vc.reshape(bsz, lc, B_KV_HEADS, B_HEAD_DIM)
    qb = q.reshape(bsz, nb, ATTN_BLOCK, B_KV_HEADS, B_GROUP, B_HEAD_DIM)
    idx = jnp.arange(nb)[:, None] * ATTN_BLOCK + jnp.arange(band)[None, :]
    pad = ((0, 0), (ATTN_BLOCK, ATTN_BLOCK), (0, 0), (0, 0))
    kb = jnp.pad(k, pad)[:, idx]
    vb = jnp.pad(v, pad)[:, idx]
    s_loc = jnp.einsum('bnqhgd,bnkhd->bhgnqk', qb, kb).astype(jnp.float32) * scale
    s_ctx = jnp.einsum('bnqhgd,bkhd->bhgnqk', qb, kc).astype(jnp.float32) * scale
    t_pos = jnp.arange(nb)[:, None, None] * ATTN_BLOCK + jnp.arange(ATTN_BLOCK)[None, :, None]
    s_pos = jnp.arange(nb)[:, None, None] * ATTN_BLOCK - ATTN_BLOCK + jnp.arange(band)[None, None, :]
    valid = (jnp.abs(t_pos - s_pos) <= WINDOW) & (s_pos >= 0) & (s_pos < length)
    s_loc = jnp.where(valid, s_loc, MASK_VALUE)
    p = sink_softmax(jnp.concatenate([s_loc, s_ctx], axis=-1), sink_g).astype(v.dtype)
    o = (jnp.einsum('bhgnqk,bnkhd->bnqhgd', p[..., :band], vb)
         + jnp.einsum('bhgnqk,bkhd->bnqhgd', p[..., band:], vc))
    o = o.reshape(bsz, length, B_WIDTH)
    sc = jnp.einsum('bqhgd,bkhd->bhgqk', qc, kc).astype(jnp.float32) * scale
    pc = sink_softmax(sc, sink_g).astype(vc.dtype)
    oc = jnp.einsum('bhgqk,bkhd->bqhgd', pc, vc).reshape(bsz, lc, B_WIDTH)
    return o, oc


def zoh_discretise(a_re, a_im, log_dt, b_re, b_im):
    dt = jnp.exp(log_dt)[:, None]
    mag = jnp.exp(a_re * dt)
    ang = a_im * dt
    abar_re, abar_im = mag * jnp.cos(ang), mag * jnp.sin(ang)
    den = a_re * a_re + a_im * a_im
    coef_re = ((abar_re - 1.0) * a_re + abar_im * a_im) / den
    coef_im = (abar_im * a_re - (abar_re - 1.0) * a_im) / den
    bbar_re = coef_re[..., None] * b_re - coef_im[..., None] * b_im
    bbar_im = coef_re[..., None] * b_im + coef_im[..., None] * b_re
    return abar_re, abar_im, bbar_re, bbar_im


def diagonal_scan(abar_re, abar_im, bu_re, bu_im, h0_re, h0_im):
    bu_re = bu_re.at[:, 0].add(abar_re * h0_re - abar_im * h0_im)
    bu_im = bu_im.at[:, 0].add(abar_re * h0_im + abar_im * h0_re)
    a_re = jnp.broadcast_to(abar_re, bu_re.shape)
    a_im = jnp.broadcast_to(abar_im, bu_im.shape)

    def combine(e1, e2):
        a1r, a1i, b1r, b1i = e1
        a2r, a2i, b2r, b2i = e2
        return (a2r * a1r - a2i * a1i, a2r * a1i + a2i * a1r,
                a2r * b1r - a2i * b1i + b2r, a2r * b1i + a2i * b1r + b2i)

    _, _, h_re, h_im = lax.associative_scan(combine, (a_re, a_im, bu_re, bu_im), axis=1)
    return h_re, h_im


def s5_direction(ug, ugc, abar_re, abar_im, bbar_re, bbar_im, c_re, c_im, reverse):
    if reverse:
        ug, ugc = jnp.flip(ug, axis=1), jnp.flip(ugc, axis=1)

    def drive(u):
        return (jnp.einsum('blgc,gpc->blgp', u, bbar_re), jnp.einsum('blgc,gpc->blgp', u, bbar_im))

    def readout(h_re, h_im):
        return jnp.einsum('blgp,gcp->blgc', h_re, c_re) - jnp.einsum('blgp,gcp->blgc', h_im, c_im)

    h0 = jnp.zeros((ugc.shape[0], C_GROUPS, C_STATE), jnp.float32)
    hc_re, hc_im = diagonal_scan(abar_re, abar_im, *drive(ugc), h0, h0)
    h_re, h_im = diagonal_scan(abar_re, abar_im, *drive(ug), hc_re[:, -1], hc_im[:, -1])
    y, yc = readout(h_re, h_im), readout(hc_re, hc_im)
    if reverse:
        y, yc = jnp.flip(y, axis=1), jnp.flip(yc, axis=1)
    return y, yc


def s5_mixer(u, uc, a_re, a_im, log_dt, b_re, b_im, c_re, c_im, d, glu_w, glu_b):
    f32 = jnp.float32
    bsz, length, _ = u.shape
    lc = uc.shape[1]
    u32, uc32 = u.astype(f32), uc.astype(f32)
    ug = u32.reshape(bsz, length, C_GROUPS, C_GROUP_CH)
    ugc = uc32.reshape(bsz, lc, C_GROUPS, C_GROUP_CH)
    y = d.astype(f32) * u32
    yc = d.astype(f32) * uc32
    for k, rev in ((0, False), (1, True)):
        disc = zoh_discretise(a_re[k].astype(f32), a_im[k].astype(f32), log_dt[k].astype(f32),
                              b_re.astype(f32), b_im.astype(f32))
        yk, yck = s5_direction(ug, ugc, *disc, c_re[k].astype(f32), c_im[k].astype(f32), rev)
        y = y + yk.reshape(bsz, length, C_WIDTH)
        yc = yc + yck.reshape(bsz, lc, C_WIDTH)

    def glu(h):
        a, b = jnp.split(jax.nn.gelu(h) @ glu_w.astype(f32) + glu_b.astype(f32), 2, axis=-1)
        return (a * jax.nn.sigmoid(b)).astype(u.dtype)

    return glu(y), glu(yc)


def setup_inputs(seed: int = 0) -> dict:
    key = jax.random.key(seed)
    ks = jax.random.split(key, 25)
    f32 = jnp.float32

    def nrm(k, shape, scale):
        return scale * jax.random.normal(k, shape, f32)

    n_idx = jnp.arange(C_STATE, dtype=f32)
    return {
        'x': nrm(ks[0], (BATCH, SEQ, D_MODEL), 1.0),
        'c': nrm(ks[1], (BATCH, D_MODEL), 1.0),
        'ctx': nrm(ks[2], (BATCH, CTX_LEN, D_MODEL), 1.0),
        'c_ctx': nrm(ks[3], (D_MODEL,), 1.0),
        'ada_w': nrm(ks[4], (DEPTH, D_MODEL, N_MOD * D_MODEL), 0.5 * D_MODEL ** -0.5),
        'ada_b': nrm(ks[5], (DEPTH, N_MOD * D_MODEL), 0.02),
        'norm_g': 1.0 + nrm(ks[6], (DEPTH, 3, D_MODEL), 0.01),
        'ffn_w1': nrm(ks[7], (DEPTH, 2, D_MODEL, 2 * D_FF), D_MODEL ** -0.5),
        'ffn_w2': nrm(ks[8], (DEPTH, 2, D_FF, D_MODEL), D_FF ** -0.5),
        'w_in': nrm(ks[9], (DEPTH, D_MODEL, D_IN), D_MODEL ** -0.5),
        'w_out': nrm(ks[10], (DEPTH, D_MIX, D_MODEL), D_MIX ** -0.5),
        'hgrn_lower_bounds': nrm(ks[11], (DEPTH, 2, A_WIDTH), 0.1),
        'hgrn_norm_g': 1.0 + nrm(ks[12], (DEPTH, A_WIDTH), 0.01),
        'attn_sink': nrm(ks[13], (DEPTH, B_Q_HEADS), 0.1),
        's5_a_re': -0.5 + nrm(ks[14], (DEPTH, 2, C_GROUPS, C_STATE), 0.01),
        's5_a_im': math.pi * n_idx + nrm(ks[15], (DEPTH, 2, C_GROUPS, C_STATE), 0.01),
        's5_log_dt': jax.random.uniform(ks[16], (DEPTH, 2, C_GROUPS), f32, math.log(1e-3), math.log(1e-1)),
        's5_b_re': nrm(ks[17], (DEPTH, C_GROUPS, C_STATE, C_GROUP_CH), (2 * C_GROUP_CH) ** -0.5),
        's5_b_im': nrm(ks[18], (DEPTH, C_GROUPS, C_STATE, C_GROUP_CH), (2 * C_GROUP_CH) ** -0.5),
        's5_c_re': nrm(ks[19], (DEPTH, 2, C_GROUPS, C_GROUP_CH, C_STATE), C_STATE ** -0.5),
        's5_c_im': nrm(ks[20], (DEPTH, 2, C_GROUPS, C_GROUP_CH, C_STATE), C_STATE ** -0.5),
        's5_d': nrm(ks[21], (DEPTH, C_WIDTH), 1.0),
        's5_glu_w': nrm(ks[22], (DEPTH, C_WIDTH, 2 * C_WIDTH), C_WIDTH ** -0.5),
        's5_glu_b': nrm(ks[23], (DEPTH, 2 * C_WIDTH), 0.02),
        'final_norm_g': 1.0 + nrm(ks[24], (D_MODEL,), 0.01),
    }


def reference(x, c, ctx, c_ctx, ada_w, ada_b, norm_g, ffn_w1, ffn_w2, w_in, w_out,
              hgrn_lower_bounds, hgrn_norm_g, attn_sink, s5_a_re, s5_a_im, s5_log_dt,
              s5_b_re, s5_b_im, s5_c_re, s5_c_im, s5_d, s5_glu_w, s5_glu_b, final_norm_g):
    bsz, seq_len, d = x.shape
    cos, sin = axial_rope_tables(seq_len)
    lb_soft = jax.nn.softmax(hgrn_lower_bounds.astype(jnp.float32), axis=0)
    lower_bound = jnp.cumsum(lb_soft, axis=0) - lb_soft[0]
    act_c = jax.nn.silu(c)
    act_cc = jax.nn.silu(c_ctx)
    xc = ctx
    for l in range(DEPTH):
        mod_x = (act_c @ ada_w[l] + ada_b[l]).reshape(bsz, 1, 3, 3, d)
        mod_c = (act_cc @ ada_w[l] + ada_b[l]).reshape(1, 1, 3, 3, d)
        x = x + 0.5 * mod_x[:, :, 0, 2] * swiglu(adaln(x, norm_g[l, 0], mod_x, 0), ffn_w1[l, 0], ffn_w2[l, 0])
        xc = xc + 0.5 * mod_c[:, :, 0, 2] * swiglu(adaln(xc, norm_g[l, 0], mod_c, 0), ffn_w1[l, 0], ffn_w2[l, 0])
        px = split_columns(adaln(x, norm_g[l, 1], mod_x, 1) @ w_in[l])
        pc = split_columns(adaln(xc, norm_g[l, 1], mod_c, 1) @ w_in[l])
        a_x, a_c = hgrn2_mixer(px[0:5], pc[0:5], lower_bound[l], hgrn_norm_g[l])
        b_x, b_c = window_gqa_mixer(px[5:8], pc[5:8], attn_sink[l], cos, sin)
        c_x, c_c = s5_mixer(px[8], pc[8], s5_a_re[l], s5_a_im[l], s5_log_dt[l], s5_b_re[l], s5_b_im[l],
                            s5_c_re[l], s5_c_im[l], s5_d[l], s5_glu_w[l], s5_glu_b[l])
        x = x + mod_x[:, :, 1, 2] * (jnp.concatenate([a_x, b_x, c_x], axis=-1) @ w_out[l])
        if l < DEPTH - 1:
            xc = xc + mod_c[:, :, 1, 2] * (jnp.concatenate([a_c, b_c, c_c], axis=-1) @ w_out[l])
            xc = xc + 0.5 * mod_c[:, :, 2, 2] * swiglu(adaln(xc, norm_g[l, 2], mod_c, 2), ffn_w1[l, 1], ffn_w2[l, 1])
        x = x + 0.5 * mod_x[:, :, 2, 2] * swiglu(adaln(x, norm_g[l, 2], mod_x, 2), ffn_w1[l, 1], ffn_w2[l, 1])
    return rms_norm(x, final_norm_g)
```

```python
import os
import numpy as np
import concourse.bass as bass
import concourse.mybir as mybir
from concourse.bass_utils import run_bass_kernel_spmd

F32 = mybir.dt.float32
BF16 = mybir.dt.bfloat16
AF = mybir.ActivationFunctionType
ALU = mybir.AluOpType

NTOK = 2304
NCTX = 256
NLAT = 2048
DM = 1024
DFF = 2816
DEPTH = 4
TILES = [(0, 256), (256, 512), (768, 512), (1280, 512), (1792, 512)]
EPS = 1e-6
INTERLEAVE = True


class Reg:
    __slots__ = ("base", "lo", "hi")

    def __init__(self, base, lo, hi):
        self.base, self.lo, self.hi = base, lo, hi


class Buf:
    _n = 0

    def __init__(self, t, size, base=None, off=0):
        self.t = t
        self.size = size
        if base is None:
            Buf._n += 1
            base = "b%d" % Buf._n
        self.base = base
        self.off = off

    def r(self, lo=0, hi=None):
        if hi is None:
            hi = self.size
        return Reg(self.base, self.off + lo, self.off + hi)

    def __getitem__(self, k):
        return self.t[k]


class Alias(Buf):
    def __init__(self, view, orig):
        self.t = view
        self.size = orig.size
        self.base = orig.base
        self.off = orig.off

    def r(self, lo=0, hi=None):
        return Reg(self.base, self.off, self.off + self.size)


class View(Buf):
    def __init__(self, view, orig, off, size, scale=1):
        self.t = view
        self.size = size
        self.base = orig.base
        self.off = orig.off + off
        self.scale = scale

    def r(self, lo=0, hi=None):
        if hi is None:
            hi = self.size
        return Reg(self.base, self.off + lo * self.scale, self.off + hi * self.scale)


class Q:
    def __init__(self, name, sem):
        self.name, self.sem = name, sem
        self.count = 0
        self.prog = []
        self.waited = {}


class FW:
    def __init__(self, nc):
        self.nc = nc
        self.q = {}
        for n in ("pe", "act", "dve", "pool", "sp"):
            self.q[n] = Q(n, nc.alloc_semaphore("s_" + n))
        self.NDS = 8
        self.dsems = {n: [nc.alloc_semaphore("d_%s%d" % (n, i)) for i in range(self.NDS)] for n in ("sp", "pool")}
        self.dcount = {n: 0 for n in self.dsems}
        self.acc = {}
        self.n_instr = 0

    def _need(self, ev, waits):
        sem, val, key = ev
        cur = waits.get(key)
        if cur is None or cur[1] < val:
            waits[key] = (sem, val)

    def _deps(self, qn, reads, writes, is_dma):
        waits = {}
        for regs, w in ((reads, False), (writes, True)):
            for rg in regs:
                lst = self.acc.get(rg.base)
                if not lst:
                    continue
                for a in lst:
                    if a[1] <= rg.lo or a[0] >= rg.hi:
                        continue
                    if not (a[2] or w):
                        continue
                    if a[3] == qn and not a[5] and not is_dma and not (a[2] and not w):
                        continue
                    self._need(a[4], waits)
        return waits

    def _record(self, qn, reads, writes, ev, is_dma):
        for rg in writes:
            lst = self.acc.setdefault(rg.base, [])
            lst[:] = [a for a in lst if not (a[0] >= rg.lo and a[1] <= rg.hi)]
            lst.append([rg.lo, rg.hi, True, qn, ev, is_dma])
        for rg in reads:
            lst = self.acc.setdefault(rg.base, [])
            lst[:] = [a for a in lst if not (not a[2] and a[3] == qn and a[5] == is_dma and a[0] == rg.lo and a[1] == rg.hi)]
            lst.append([rg.lo, rg.hi, False, qn, ev, is_dma])

    def _emit_waits(self, q, waits):
        for key, (sem, val) in waits.items():
            if q.waited.get(key, 0) >= val:
                continue
            q.waited[key] = val
            q.prog.append(("w", sem, val))

    def op(self, qn, fn, reads=(), writes=()):
        q = self.q[qn]
        waits = self._deps(qn, reads, writes, False)
        self._emit_waits(q, waits)
        q.count += 1
        ev = (q.sem, q.count, qn)
        q.prog.append(("o", fn, q.sem, 1))
        self._record(qn, reads, writes, ev, False)
        self.n_instr += 1
        return ev

    def dma(self, qn, fn, reads=(), writes=()):
        q = self.q[qn]
        waits = self._deps(qn, reads, writes, True)
        j = self.dcount[qn]
        self.dcount[qn] += 1
        s = j % self.NDS
        sem = self.dsems[qn][s]
        key = "d_%s%d" % (qn, s)
        prev = 16 * (j // self.NDS)
        if prev > 0:
            self._need((sem, prev, key), waits)
        self._emit_waits(q, waits)
        ev = (sem, prev + 16, key)
        q.prog.append(("o", fn, sem, 16))
        self._record(qn, reads, writes, ev, True)
        self.n_instr += 1
        return ev

    def wait_all(self, qn):
        q = self.q[qn]
        waits = {}
        for lst in self.acc.values():
            for a in lst:
                self._need(a[4], waits)
        self._emit_waits(q, waits)

    def emit(self):
        nc = self.nc
        me = self

        def run(qn, eng):
            for it in me.q[qn].prog:
                if it[0] == "w":
                    eng.wait_ge(it[1], it[2])
                else:
                    getattr(eng, it[1][0])(**it[1][1]).then_inc(it[2], it[3])

        with nc.Block() as block:
            @block.tensor
            def _(e):
                run("pe", e)

            @block.scalar
            def _(e):
                run("act", e)

            @block.vector
            def _(e):
                run("dve", e)

            @block.gpsimd
            def _(e):
                run("pool", e)

            @block.sync
            def _(e):
                run("sp", e)


class K:
    def __init__(self, stage=99):
        self.stage = stage
        nc = self.nc = bass.Bass("TRN2", target_bir_lowering=False)
        fw = self.fw = FW(nc)
        self.din = {}
        self.rot = {}

        def dram_in(name, shape, dt=F32):
            t = nc.dram_tensor(name, list(shape), dt, kind="ExternalInput").ap()
            n = int(np.prod(shape[1:])) if len(shape) > 1 else 1
            self.din[name] = Buf(t, max(n, 1))
            return self.din[name]

        self.x_in = dram_in("x_in", [NTOK, DM])
        self.cvec = dram_in("cvec", [128, 8, 2])
        self.ada_w = dram_in("ada_w", [DEPTH, DM, 9 * DM])
        self.ada_b = dram_in("ada_b_t", [128, DEPTH, 72])
        self.norm_g = dram_in("norm_g_t", [128, DEPTH, 3, 8])
        self.fng = dram_in("fng_t", [128, 8])
        self.w1 = dram_in("ffn_w1", [DEPTH, 2, DM, 2 * DFF])
        self.w2 = dram_in("ffn_w2", [DEPTH, 2, DFF, DM])
        self.ident_d = dram_in("ident", [128, 128])
        self.win = dram_in("win_ext", [DEPTH, DM, 3200])
        self.wout = dram_in("w_out", [DEPTH, DM, DM])
        self.ropeC = dram_in("ropeC", [128, NTOK])
        self.ropeS = dram_in("ropeS", [128, NTOK])
        self.mprev_d = dram_in("mprev", [128, 128])
        self.mnext_d = dram_in("mnext", [128, 128])
        self.sink_d = dram_in("sink_t", [64, DEPTH, 8])

        self.lb_d = dram_in("lb_t", [64, DEPTH, 8])
        self.hng_d = dram_in("hng_t", [64, DEPTH, 4])
        self.rm_d = dram_in("rmask", [64, 1024])
        self.hm_d = dram_in("hmask", [64, 2, 64])

        self.s5A = dram_in("s5_As", [128, DEPTH, 3, 16])
        self.s5Q = dram_in("s5_Aq", [128, DEPTH, 2, 4, 64])
        self.s5LDq = dram_in("s5_LDq", [128, DEPTH, 4])
        self.s5B = dram_in("s5_Bq", [128, DEPTH, 2, 2, 64])
        self.s5C = dram_in("s5_Cblk", [128, DEPTH, 2, 16, 32])
        self.s5d_d = dram_in("s5_d_t", [128, DEPTH, 2])
        self.glub_d = dram_in("glu_b_t", [128, DEPTH, 4])
        self.gluw_d = dram_in("s5_glu_w", [DEPTH, 256, 512])
        self.rowm_d = dram_in("rowmask", [128, 8])
        self.J_d = dram_in("Jmat", [128, 128])

        def scr(name, shape, dt):
            return Buf(nc.dram_tensor(name, list(shape), dt, kind="Internal").ap(), 24)

        self.mix = scr("mix", [DM, NTOK], BF16)
        self.sQ = scr("sQ", [512, NTOK], BF16)
        self.sK = scr("sK", [128, NTOK], BF16)
        self.sV = scr("sV", [NTOK, 128], BF16)
        self.sA = scr("sA", [4, 256, NTOK], F32)
        self.sAv = scr("sAv", [NTOK, 256], BF16)
        self.sO = scr("sO", [256, NTOK], F32)
        self.sU = scr("sU", [256, NTOK], F32)
        self.sUt = scr("sUt", [NTOK, 256], BF16)
        y = nc.dram_tensor("y", [NLAT, DM], F32, kind="ExternalOutput").ap()
        self.y = Buf(y, DM)

        def sb(name, shape, dt=F32):
            return Buf(nc.alloc_sbuf_tensor(name, list(shape), dt), int(np.prod(shape[1:])))

        self.xT = sb("xT", [128, 8, NTOK])
        self.hT = sb("hT", [128, 8, NTOK], BF16)
        self.wA = [sb("wA%d" % i, [128, 8, 512], BF16) for i in range(2)]
        self.wB = [sb("wB%d" % i, [128, 2, 1024], BF16) for i in range(2)]
        self.sg = [sb("sg%d" % i, [128, 512]) for i in range(2)]
        self.tmp = [sb("tmp%d" % i, [128, 512]) for i in range(2)]
        self.rstd = sb("rstd", [128, 512])
        self.hid = [sb("hid%d" % i, [128, 2, 512], BF16) for i in range(2)]
        self.sq = [sb("sq%d" % i, [128, 512], BF16) for i in range(2)]
        self.xin = [sb("xin%d" % i, [128, 1024]) for i in range(2)]
        self.ident = sb("identf", [128, 128])
        self.onesb = sb("onesb", [128, 128], BF16)
        self.cv = sb("cv", [128, 8, 2])
        self.csb = sb("csb", [128, 8, 2], BF16)
        self.modr = sb("modr", [128, DEPTH, 72, 2])
        self.adab = sb("adab", [128, DEPTH, 72])
        self.ng = sb("ng", [128, DEPTH, 3, 8])
        self.fngs = sb("fngs", [128, 8])
        self.GS = sb("GS", [128, DEPTH, 3, 8, 2])
        self.GT = sb("GT", [128, DEPTH, 3, 8, 2])
        self.mprev = sb("mprev_s", [128, 128], BF16)
        self.mnext = sb("mnext_s", [128, 128], BF16)
        self.esink = sb("esink", [64, DEPTH, 8])
        self.rc = sb("rc", [128, 512])
        self.rs = sb("rs", [128, 512])
        self.kTall = sb("kTall", [128, NTOK], BF16)
        self.vt = sb("vt", [128, 18, 128], BF16)
        self.qt = [sb("qt%d" % i, [128, 512], BF16) for i in range(2)]
        self.sq2 = [sb("sq2_%d" % i, [128, 512], BF16) for i in range(2)]
        self.pT = [sb("pT%d" % i, [128, 512], BF16) for i in range(3)]
        self.ob = [sb("ob%d" % i, [128, 512], BF16) for i in range(2)]
        self.zt = Alias(self.ob[0].t, self.ob[0])
        self.LB = sb("LB", [64, DEPTH, 8])
        self.OML = sb("OML", [64, DEPTH, 8])
        self.lbw = sb("lbw", [64, DEPTH, 8])
        self.lbs = sb("lbs", [64, 8])
        self.hng = sb("hng", [64, DEPTH, 4])
        self.RM = sb("RM", [64, 1024])
        self.hmask = sb("hmask_s", [64, 2, 64], BF16)
        self.identb = sb("identb", [64, 64], BF16)
        self.HW = [sb("HW%d" % i, [64, 1024]) for i in range(2)] + [
            Alias(self.wB[i].t[:].rearrange("p f n -> p (f n)").bitcast(F32)[0:64, :], self.wB[i]) for i in range(2)]
        self.HQE = Alias(self.hid[0].t[:].rearrange("p f n -> p (f n)")[0:64, :], self.hid[0])
        self.HKE = Alias(self.hid[1].t[:].rearrange("p f n -> p (f n)")[0:64, :], self.hid[1])
        self.HV = sb("HV", [32, 8, 256], BF16)
        self.HS = Alias(self.sg[0].t[0:64, 0:256], self.sg[0])
        self.HT1 = Alias(self.sg[1].t[0:64, 0:256], self.sg[1])
        self.HSP = sb("HSP", [64, 256], BF16)
        self.HKT3 = [sb("HKT3_%d" % i, [32, 256], BF16) for i in range(3)]
        self.HSC3 = [sb("HSC3_%d" % i, [32, 128], BF16) for i in range(3)]
        self.HKT = self.HKT3[0]
        self.HSC = self.HSC3[0]
        self.HM1 = sb("HM1", [64, 32])
        self.HEM = sb("HEM", [64, 32])
        self.HET = sb("HET", [64, 32])
        self.HE2 = sb("HE2", [64, 32])
        self.sAs = sb("s5As_s", [128, 3, 16])
        self.sDT = sb("s5DT", [128, 16])
        self.sR = sb("s5R", [128, 16])
        self.sTH = sb("s5TH", [128, 16])
        self.sC1 = sb("s5C1", [128, 16])
        self.sS1 = sb("s5S1", [128, 16])
        self.sY16 = sb("s5Y16", [128, 16])
        self.sN16 = sb("s5N16", [128, 16])
        self.sT16 = sb("s5T16", [128, 16])
        self.ldq = sb("s5ldq", [128, DEPTH, 4])
        self.dtq = sb("s5dtq", [128, 4])
        a0 = self.wA[0].t[:].rearrange("p k n -> p (k n)")
        a1 = self.wA[1].t[:].rearrange("p k n -> p (k n)").bitcast(F32)
        self.Cblk = View(a0[:, 0:1024].rearrange("p (a b c) -> p a b c", a=2, b=16), self.wA[0], 0, 1024)
        self.gluwS = View(a0[:, 1024:2048].rearrange("p (a b) -> p a b", a=2), self.wA[0], 1024, 1024)
        self.Bblk = View(a0[:, 2048:2304].rearrange("p (a b) -> p a b", a=2), self.wA[0], 2048, 256)
        self.s5gy = [View(a0[:, 2304 + i * 512:2816 + i * 512], self.wA[0], 2304 + i * 512, 512) for i in range(2)]
        self.s5ub = [View(a0[:, 3328 + i * 256:3584 + i * 256], self.wA[0], 3328 + i * 256, 256) for i in range(2)]
        self.s5ob = View(a0[:, 3328:3840], self.wA[0], 3328, 512)
        self.s5w = [View(a1[:, i * 512:(i + 1) * 512], self.wA[1], i * 1024, 512, 2) for i in range(4)]
        self.Jb = sb("Jb", [128, 128], BF16)
        self.Ib = sb("Ib", [128, 128], BF16)
        self.rowm = sb("rowm", [128, 8])
        self.s5dS = sb("s5dS", [128, DEPTH, 2])
        self.glubS = sb("glubS", [128, DEPTH, 4])
        self.hprev = sb("hprev", [128, 2])
        hflat = self.hT.t[:].rearrange("p k t -> p (k t)")
        self.ytF = View(hflat[:, 0:4608], self.hT, 0, 4608)
        self.ytB = View(hflat[:, 4608:9216], self.hT, 4608, 4608)
        self.uTb = View(hflat[:, 9216:13824], self.hT, 9216, 4608)
        self.s5w2 = [View(hflat[:, 14848 + i * 1024:15872 + i * 1024].bitcast(F32), self.hT, 14848 + i * 1024, 512, 2) for i in range(3)] + [self.rstd]
        self.hpt = sb("s5hpt", [128, 2])
        self.BB = View(hflat[:, 13824:14848].bitcast(F32).rearrange("p (a b c) -> p a b c", a=2, b=4), self.hT, 13824, 512, 2)
        self.epsc = sb("epsc", [128, 1])
        self.onec = sb("onec", [128, 1])
        self.P = [Buf(nc.alloc_psum_tensor("ps%d" % i, [128, 512], F32), 512) for i in range(8)]

    def nxt(self, key, n):
        v = self.rot.get(key, 0)
        self.rot[key] = v + 1
        return v % n

    def prologue(self):
        fw = self.fw
        ld = lambda dst, src: fw.dma("sp", ("dma_start", dict(out=dst[:], in_=src[:])), reads=[src.r()], writes=[dst.r()])
        ld(self.ident, self.ident_d)
        ld(self.cv, self.cvec)
        ld(self.adab, self.ada_b)
        ld(self.ng, self.norm_g)
        ld(self.fngs, self.fng)
        fw.op("dve", ("memset", dict(ap=self.onesb[:], constant=1.0)), writes=[self.onesb.r()])
        fw.op("dve", ("memset", dict(ap=self.epsc[:], constant=EPS)), writes=[self.epsc.r()])
        fw.op("dve", ("memset", dict(ap=self.onec[:], constant=1.0)), writes=[self.onec.r()])
        fw.dma("pool", ("dma_start", dict(out=self.mprev[:], in_=self.mprev_d[:])), reads=[self.mprev_d.r()], writes=[self.mprev.r()])
        fw.dma("pool", ("dma_start", dict(out=self.mnext[:], in_=self.mnext_d[:])), reads=[self.mnext_d.r()], writes=[self.mnext.r()])
        ld(self.lbw, self.lb_d)
        ld(self.hng, self.hng_d)
        ld(self.RM, self.rm_d)
        fw.dma("pool", ("dma_start", dict(out=self.hmask[:], in_=self.hm_d[:])), reads=[self.hm_d.r()], writes=[self.hmask.r()])
        fw.dma("pool", ("dma_start", dict(out=self.identb[:], in_=self.ident_d[0:64, 0:64])), reads=[self.ident_d.r()], writes=[self.identb.r()])
        lbw, lbs, LB = self.lbw, self.lbs, self.LB
        fw.op("act", ("activation", dict(out=lbw[:], in_=lbw[:], func=AF.Exp)), reads=[lbw.r()], writes=[lbw.r()])
        fw.op("dve", ("tensor_tensor", dict(out=lbs[:], in0=lbw[:, 0, :], in1=lbw[:, 1, :], op=ALU.add)), reads=[lbw.r()], writes=[lbs.r()])
        fw.op("dve", ("tensor_tensor", dict(out=lbs[:], in0=lbs[:], in1=lbw[:, 2, :], op=ALU.add)), reads=[lbw.r(), lbs.r()], writes=[lbs.r()])
        fw.op("dve", ("tensor_tensor", dict(out=lbs[:], in0=lbs[:], in1=lbw[:, 3, :], op=ALU.add)), reads=[lbw.r(), lbs.r()], writes=[lbs.r()])
        fw.op("dve", ("reciprocal", dict(out=lbs[:], in_=lbs[:])), reads=[lbs.r()], writes=[lbs.r()])
        fw.op("dve", ("tensor_tensor", dict(out=lbw[:], in0=lbw[:], in1=lbs[:].unsqueeze(1).to_broadcast([64, DEPTH, 8]), op=ALU.mult)), reads=[lbw.r(), lbs.r()], writes=[lbw.r()])
        fw.op("dve", ("memset", dict(ap=LB[:, 0, :], constant=0.0)), writes=[LB.r()])
        fw.op("dve", ("tensor_copy", dict(out=LB[:, 1, :], in_=lbw[:, 1, :])), reads=[lbw.r()], writes=[LB.r()])
        fw.op("dve", ("tensor_tensor", dict(out=LB[:, 2, :], in0=LB[:, 1, :], in1=lbw[:, 2, :], op=ALU.add)), reads=[lbw.r(), LB.r()], writes=[LB.r()])
        fw.op("dve", ("tensor_tensor", dict(out=LB[:, 3, :], in0=LB[:, 2, :], in1=lbw[:, 3, :], op=ALU.add)), reads=[lbw.r(), LB.r()], writes=[LB.r()])
        fw.op("dve", ("tensor_scalar", dict(out=self.OML[:], in0=LB[:], scalar1=-1.0, scalar2=1.0, op0=ALU.mult, op1=ALU.add)), reads=[LB.r()], writes=[self.OML.r()])
        ld(self.ldq, self.s5LDq)
        ld(self.rowm, self.rowm_d)
        ld(self.s5dS, self.s5d_d)
        ld(self.glubS, self.glub_d)
        fw.dma("pool", ("dma_start", dict(out=self.Jb[:], in_=self.J_d[:])), reads=[self.J_d.r()], writes=[self.Jb.r()])
        fw.dma("pool", ("dma_start", dict(out=self.Ib[:], in_=self.ident_d[:])), reads=[self.ident_d.r()], writes=[self.Ib.r()])
        ld(self.esink, self.sink_d)
        fw.op("act", ("activation", dict(out=self.esink[:], in_=self.esink[:], func=AF.Exp)), reads=[self.esink.r()], writes=[self.esink.r()])
        xT, ident = self.xT, self.ident
        for blk in range(NTOK // 128):
            xi = self.xin[blk % 2]
            fw.dma("sp", ("dma_start", dict(out=xi[:], in_=self.x_in[blk * 128:(blk + 1) * 128, :])),
                   reads=[self.x_in.r()], writes=[xi.r()])
            for k4 in range(2):
                ps = self.P[self.nxt("pa", 4)]
                for kk in range(4):
                    k = k4 * 4 + kk
                    fw.op("pe", ("transpose", dict(out=ps[:, kk * 128:(kk + 1) * 128], in_=xi[:, k * 128:(k + 1) * 128], identity=ident[:])),
                          reads=[xi.r(k * 128, (k + 1) * 128), ident.r()], writes=[ps.r(kk * 128, (kk + 1) * 128)])
                fw.op("dve", ("tensor_copy", dict(
                    out=xT[:, k4 * 4:(k4 + 1) * 4, blk * 128:(blk + 1) * 128], in_=ps[:].rearrange("p (a b) -> p a b", b=128))),
                    reads=[ps.r()], writes=[xT.r((k4 * 4 + kk) * NTOK + blk * 128, (k4 * 4 + kk) * NTOK + (blk + 1) * 128) for kk in range(4)])
        fw.op("act", ("activation", dict(out=self.csb[:], in_=self.cv[:], func=AF.Silu)), reads=[self.cv.r()], writes=[self.csb.r()])
        csb, modr = self.csb, self.modr
        for l in range(DEPTH):
            awl = self.ada_w[l].rearrange("(k p) n -> p k n", p=128)
            for pc in range(18):
                wa = self.wA[self.nxt("wA", 2)]
                fw.dma("pool", ("dma_start", dict(out=wa[:], in_=awl[:, :, pc * 512:(pc + 1) * 512])),
                       reads=[self.ada_w.r()], writes=[wa.r()])
                ps = self.P[self.nxt("pa", 4)]
                for m4 in range(4):
                    for k in range(8):
                        fw.op("pe", ("matmul", dict(out=ps[:, m4 * 2:(m4 + 1) * 2], lhsT=wa[:, k, m4 * 128:(m4 + 1) * 128], rhs=csb[:, k, :], start=(k == 0), stop=(k == 7))),
                              reads=[wa.r(), csb.r()], writes=[ps.r(m4 * 2, m4 * 2 + 2)])
                lo = (l * 72 + pc * 4) * 2
                fw.op("dve", ("tensor_tensor", dict(
                    out=modr[:, l, pc * 4:(pc + 1) * 4, :], in0=ps[:, 0:8].rearrange("p (a b) -> p a b", b=2),
                    in1=self.adab[:, l, pc * 4:(pc + 1) * 4].unsqueeze(2).to_broadcast([128, 4, 2]), op=ALU.add)),
                    reads=[ps.r(0, 8), self.adab.r()], writes=[modr.r(lo, lo + 8)])
        GS, GT, ng = self.GS, self.GT, self.ng
        for l in range(DEPTH):
            for i in range(3):
                sc = modr[:, l, (i * 3 + 1) * 8:(i * 3 + 2) * 8, :]
                gt = modr[:, l, (i * 3 + 2) * 8:(i * 3 + 3) * 8, :]
                lo = ((l * 3 + i) * 8) * 2
                fw.op("dve", ("scalar_tensor_tensor", dict(
                    out=GS[:, l, i, :, :], in0=sc, scalar=1.0, in1=ng[:, l, i, :].unsqueeze(2).to_broadcast([128, 8, 2]), op0=ALU.add, op1=ALU.mult)),
                    reads=[modr.r(), ng.r()], writes=[GS.r(lo, lo + 16)])
                fw.op("dve", ("tensor_scalar", dict(out=GT[:, l, i, :, :], in0=gt, scalar1=(1.0 if i == 1 else 0.5), scalar2=None, op0=ALU.mult)),
                      reads=[modr.r()], writes=[GT.r(lo, lo + 16)])

    def rms_stats(self, t0, n):
        fw, xT = self.fw, self.xT
        ps = self.P[self.nxt("pa", 4)]
        for k in range(8):
            sq = self.sq[self.nxt("sq", 2)]
            fw.op("act", ("activation", dict(out=sq[:, :n], in_=xT[:, k, t0:t0 + n], func=AF.Square)),
                  reads=[xT.r(k * NTOK + t0, k * NTOK + t0 + n)], writes=[sq.r(0, n)])
            fw.op("pe", ("matmul", dict(out=ps[:, :n], lhsT=self.onesb[:], rhs=sq[:, :n], start=(k == 0), stop=(k == 7))),
                  reads=[sq.r(0, n), self.onesb.r()], writes=[ps.r(0, n)])
        rstd = self.rstd
        fw.op("act", ("activation", dict(out=rstd[:, :n], in_=ps[:, :n], func=AF.Ln, bias=self.epsc[:, 0:1], scale=1.0 / DM)),
              reads=[ps.r(0, n), self.epsc.r()], writes=[rstd.r(0, n)])
        fw.op("act", ("activation", dict(out=rstd[:, :n], in_=rstd[:, :n], func=AF.Exp, scale=-0.5)), reads=[rstd.r(0, n)], writes=[rstd.r(0, n)])

    def norm_h(self, l, i, tiles):
        fw, xT, hT = self.fw, self.xT, self.hT
        for (t0, n) in tiles:
            var = 1 if t0 == 0 else 0
            self.rms_stats(t0, n)
            for k in range(8):
                tmp = self.tmp[self.nxt("tmp", 2)]
                fw.op("dve", ("scalar_tensor_tensor", dict(
                    out=tmp[:, :n], in0=xT[:, k, t0:t0 + n], scalar=self.GS[:, l, i, k, var:var + 1], in1=self.rstd[:, :n], op0=ALU.mult, op1=ALU.mult)),
                    reads=[xT.r(k * NTOK + t0, k * NTOK + t0 + n), self.GS.r(), self.rstd.r(0, n)], writes=[tmp.r(0, n)])
                fw.op("act", ("activation", dict(
                    out=hT[:, k, t0:t0 + n], in_=tmp[:, :n], func=AF.Identity, bias=self.modr[:, l, (i * 3) * 8 + k, var:var + 1], scale=1.0)),
                    reads=[tmp.r(0, n), self.modr.r()], writes=[hT.r(k * NTOK + t0, k * NTOK + t0 + n)])

    def ffn(self, l, i, tiles):
        fw, xT, hT = self.fw, self.xT, self.hT
        fi = 0 if i == 0 else 1
        w1v = self.w1[l, fi].rearrange("(k p) n -> p k n", p=128)

        def out_pair(prev, o):
            wb, hid, t0, n, var = prev
            po = self.P[4 + self.nxt("pb", 4)]
            for f in range(2):
                fw.op("pe", ("matmul", dict(out=po[:, :n], lhsT=wb[:, f, o * 128:(o + 1) * 128], rhs=hid[:, f, :n], start=(f == 0), stop=(f == 1))),
                      reads=[wb.r(f * 1024 + o * 128, f * 1024 + (o + 1) * 128), hid.r(f * 512, f * 512 + n)], writes=[po.r(0, n)])
            fw.op("dve", ("scalar_tensor_tensor", dict(
                out=xT[:, o, t0:t0 + n], in0=po[:, :n], scalar=self.GT[:, l, i, o, var:var + 1], in1=xT[:, o, t0:t0 + n], op0=ALU.mult, op1=ALU.add)),
                reads=[po.r(0, n), self.GT.r(), xT.r(o * NTOK + t0, o * NTOK + t0 + n)], writes=[xT.r(o * NTOK + t0, o * NTOK + t0 + n)])

        prev = None
        for g in range(11):
            wa = self.wA[self.nxt("wA", 2)]
            wb = self.wB[self.nxt("wB", 2)]
            fw.dma("pool", ("dma_start", dict(out=wa[:, :, 0:256], in_=w1v[:, :, g * 256:(g + 1) * 256])),
                   reads=[self.w1.r()], writes=[wa.r(kk * 512, kk * 512 + 256) for kk in range(8)])
            fw.dma("pool", ("dma_start", dict(out=wa[:, :, 256:512], in_=w1v[:, :, DFF + g * 256:DFF + (g + 1) * 256])),
                   reads=[self.w1.r()], writes=[wa.r(kk * 512 + 256, kk * 512 + 512) for kk in range(8)])
            fw.dma("pool", ("dma_start", dict(out=wb[:], in_=self.w2[l, fi, g * 256:(g + 1) * 256, :].rearrange("(f p) n -> p f n", p=128))),
                   reads=[self.w2.r()], writes=[wb.r()])
            for (t0, n) in tiles:
                var = 1 if t0 == 0 else 0
                hid = self.hid[self.nxt("hid", 2)]
                q = 0
                for f in range(2):
                    pg = self.P[self.nxt("pa", 4)]
                    pu = self.P[self.nxt("pa", 4)]
                    for (pp, c0, isg) in ((pg, f * 128, True), (pu, 256 + f * 128, False)):
                        for k in range(8):
                            fw.op("pe", ("matmul", dict(out=pp[:, :n], lhsT=wa[:, k, c0:c0 + 128], rhs=hT[:, k, t0:t0 + n], start=(k == 0), stop=(k == 7))),
                                  reads=[wa.r(k * 512 + c0, k * 512 + c0 + 128), hT.r(k * NTOK + t0, k * NTOK + t0 + n)], writes=[pp.r(0, n)])
                            if k % 4 == 3:
                                if prev is not None:
                                    out_pair(prev, q)
                                q += 1
                        if isg:
                            sg = self.sg[self.nxt("sg", 2)]
                            fw.op("act", ("activation", dict(out=sg[:, :n], in_=pg[:, :n], func=AF.Silu)), reads=[pg.r(0, n)], writes=[sg.r(0, n)])
                    fw.op("dve", ("tensor_tensor", dict(out=hid[:, f, :n], in0=sg[:, :n], in1=pu[:, :n], op=ALU.mult)),
                          reads=[sg.r(0, n), pu.r(0, n)], writes=[hid.r(f * 512, f * 512 + n)])
                prev = (wb, hid, t0, n, var)
        for o in range(8):
            out_pair(prev, o)

    def mm8(self, ps, n, wa, c0, t0, ncol=128):
        fw, hT = self.fw, self.hT
        for k in range(8):
            fw.op("pe", ("matmul", dict(out=ps[0:ncol, :n], lhsT=wa[:, k, c0:c0 + ncol], rhs=hT[:, k, t0:t0 + n], start=(k == 0), stop=(k == 7))),
                  reads=[wa.r(k * 512 + c0, k * 512 + c0 + ncol), hT.r(k * NTOK + t0, k * NTOK + t0 + n)], writes=[ps.r(0, n)])

    def inproj(self, l):
        fw = self.fw
        GR = [(0, 512, "A0"), (512, 512, "A1"), (1024, 512, "B0"), (1536, 512, "B1"), (2048, 512, "BK"), (2560, 384, "TV"), (2944, 256, "TU")]
        winl = self.win[l].rearrange("(k p) n -> p k n", p=128)
        for (c0, ncl, kind) in GR:
            wa = self.wA[self.nxt("wA", 2)]
            fw.dma("pool", ("dma_start", dict(out=wa[:, :, 0:ncl], in_=winl[:, :, c0:c0 + ncl])), reads=[self.win.r()], writes=[wa.r()])
            for ti, (t0, n) in enumerate(TILES):
                if kind in ("A0", "A1"):
                    for j in range(4):
                        ps = self.P[self.nxt("pa", 4)]
                        self.mm8(ps, n, wa, j * 128, t0)
                        tmp = self.tmp[self.nxt("tmp", 2)]
                        fw.op("act", ("activation", dict(out=tmp[:, :n], in_=ps[:, :n], func=AF.Identity)), reads=[ps.r(0, n)], writes=[tmp.r(0, n)])
                        qi = (0 if kind == "A0" else 2) + j // 2
                        r0 = (j % 2) * 128
                        fw.dma("sp", ("dma_start", dict(out=self.sA[qi, r0:r0 + 128, t0:t0 + n], in_=tmp[:, :n])), reads=[tmp.r(0, n)], writes=[self.sA.r(ti, ti + 1)])
                elif kind in ("B0", "B1", "BK"):
                    fw.dma("sp", ("dma_start", dict(out=self.rc[:, :n], in_=self.ropeC[:, t0:t0 + n])), reads=[self.ropeC.r()], writes=[self.rc.r(0, n)])
                    fw.dma("sp", ("dma_start", dict(out=self.rs[:, :n], in_=self.ropeS[:, t0:t0 + n])), reads=[self.ropeS.r()], writes=[self.rs.r(0, n)])
                    for j in range(2 if kind != "BK" else 1):
                        pq = self.P[self.nxt("pa", 4)]
                        pr = self.P[self.nxt("pa", 4)]
                        self.mm8(pq, n, wa, j * 256, t0)
                        self.mm8(pr, n, wa, j * 256 + 128, t0)
                        t1 = self.tmp[0]
                        t2 = self.tmp[1]
                        fw.op("dve", ("tensor_tensor", dict(out=t1[:, :n], in0=pq[:, :n], in1=self.rc[:, :n], op=ALU.mult)), reads=[pq.r(0, n), self.rc.r(0, n)], writes=[t1.r(0, n)])
                        fw.op("dve", ("tensor_tensor", dict(out=t2[:, :n], in0=pr[:, :n], in1=self.rs[:, :n], op=ALU.mult)), reads=[pr.r(0, n), self.rs.r(0, n)], writes=[t2.r(0, n)])
                        qb = self.ob[self.nxt("ob", 2)]
                        fw.op("dve", ("tensor_tensor", dict(out=qb[:, :n], in0=t1[:, :n], in1=t2[:, :n], op=ALU.add)), reads=[t1.r(0, n), t2.r(0, n)], writes=[qb.r(0, n)])
                        if kind == "BK":
                            fw.dma("sp", ("dma_start", dict(out=self.sK[:, t0:t0 + n], in_=qb[:, :n])), reads=[qb.r(0, n)], writes=[self.sK.r(ti, ti + 1)])
                        else:
                            ch = (0 if kind == "B0" else 2) + j
                            fw.dma("sp", ("dma_start", dict(out=self.sQ[ch * 128:(ch + 1) * 128, t0:t0 + n], in_=qb[:, :n])), reads=[qb.r(0, n)], writes=[self.sQ.r(ti, ti + 1)])
                    if kind == "BK":
                        for j in range(2):
                            ps = self.P[self.nxt("pa", 4)]
                            self.mm8(ps, n, wa, 256 + j * 128, t0)
                            tmp = self.tmp[self.nxt("tmp", 2)]
                            fw.op("act", ("activation", dict(out=tmp[:, :n], in_=ps[:, :n], func=AF.Identity)), reads=[ps.r(0, n)], writes=[tmp.r(0, n)])
                            fw.dma("sp", ("dma_start", dict(out=self.sU[j * 128:(j + 1) * 128, t0:t0 + n], in_=tmp[:, :n])), reads=[tmp.r(0, n)], writes=[self.sU.r(ti, ti + 1)])
                else:
                    hT = self.hT
                    for blk in range(n // 128):
                        tb = t0 + blk * 128
                        ps = self.P[self.nxt("pa", 4)]
                        for k in range(8):
                            fw.op("pe", ("matmul", dict(out=ps[:, :ncl], lhsT=hT[:, k, tb:tb + 128], rhs=wa[:, k, 0:ncl], start=(k == 0), stop=(k == 7))),
                                  reads=[wa.r(k * 512, k * 512 + ncl), hT.r(k * NTOK + tb, k * NTOK + tb + 128)], writes=[ps.r(0, ncl)])
                        ob = self.ob[self.nxt("ob", 2)]
                        fw.op("act", ("activation", dict(out=ob[:, :ncl], in_=ps[:, :ncl], func=AF.Identity)), reads=[ps.r(0, ncl)], writes=[ob.r(0, ncl)])
                        if kind == "TV":
                            fw.dma("sp", ("dma_start", dict(out=self.sAv[tb:tb + 128, :], in_=ob[:, 0:256])), reads=[ob.r(0, 256)], writes=[self.sAv.r(ti, ti + 1)])
                            fw.dma("sp", ("dma_start", dict(out=self.sV[tb:tb + 128, :], in_=ob[:, 256:384])), reads=[ob.r(256, 384)], writes=[self.sV.r(ti, ti + 1)])
                        else:
                            fw.dma("sp", ("dma_start", dict(out=self.sUt[tb:tb + 128, :], in_=ob[:, 0:256])), reads=[ob.r(0, 256)], writes=[self.sUt.r(ti, ti + 1)])


    def hgrn(self, l):
        fw = self.fw
        W1, W2, W3, W4 = self.HW
        W5, Qb = self.xin[0], self.xin[1]
        QE, KE, Vt, S, T1, SP, KT, SC = self.HQE, self.HKE, self.HV, self.HS, self.HT1, self.HSP, self.HKT, self.HSC
        M1, EM, ET, E2 = self.HM1, self.HEM, self.HET, self.HE2

        def v3(b):
            return b[0:64, :].rearrange("p (a j) -> p a j", j=32)

        def o(q, name, reads, writes, **kw):
            fw.op(q, (name, kw), reads=reads, writes=writes)

        for dirn in range(2):
            o("dve", "memset", [], [S.r()], ap=S[:, :], constant=0.0)
            segs = list(range(9)) if dirn == 0 else [0] + list(range(8, 0, -1))
            lbb = self.LB[:, l, dirn * 4:(dirn + 1) * 4].unsqueeze(2).to_broadcast([64, 4, 256])
            omb = self.OML[:, l, dirn * 4:(dirn + 1) * 4].unsqueeze(2).to_broadcast([64, 4, 256])
            for sgi in segs:
                t0 = sgi * 256
                ti = 0 if sgi == 0 else 1 + (sgi - 1) // 2
                fw.dma("sp", ("dma_start", dict(out=W1[:, :].rearrange("p (h t) -> p h t", h=4), in_=self.sA[1 + dirn, :, t0:t0 + 256].rearrange("(h d) t -> d h t", d=64))),
                       reads=[self.sA.r()], writes=[W1.r()])
                fw.dma("sp", ("dma_start", dict(out=Qb[0:64, :].rearrange("p (h t) -> p h t", h=4), in_=self.sA[0, :, t0:t0 + 256].rearrange("(h d) t -> d h t", d=64))),
                       reads=[self.sA.r()], writes=[Qb.r()])
                fw.dma("sp", ("dma_start", dict(out=Vt[:, :, :], in_=self.sAv[t0:t0 + 256, :].rearrange("(c s) v -> s c v", s=32))),
                       reads=[self.sAv.r()], writes=[Vt.r()])
                w1h = W1[:, :].rearrange("p (h t) -> p h t", h=4)
                o("act", "activation", [W1.r()], [W1.r()], out=W1[:, :], in_=W1[:, :], func=AF.Sigmoid)
                for h in range(4):
                    o("act", "activation", [W1.r(), self.OML.r(), self.LB.r()], [W1.r()], out=W1[:, h * 256:(h + 1) * 256], in_=W1[:, h * 256:(h + 1) * 256], func=AF.Identity,
                      scale=self.OML[:, l, dirn * 4 + h:dirn * 4 + h + 1], bias=self.LB[:, l, dirn * 4 + h:dirn * 4 + h + 1])
                o("act", "activation", [W1.r()], [W2.r()], out=W2[:, :], in_=W1[:, :], func=AF.Ln)
                o("act", "activation", [W1.r(), self.onec.r()], [W3.r()], out=W3[:, :], in_=W1[:, :], func=AF.Identity, scale=-1.0, bias=self.onec[0:64, 0:1])
                o("dve", "tensor_tensor_scan", [self.RM.r(), W2.r()], [W4.r()], out=W4[:, :], data0=self.RM[:, :], data1=W2[:, :], initial=0.0, op0=ALU.mult, op1=ALU.add)
                if dirn == 0:
                    o("dve", "tensor_copy", [W4.r()], [M1.r()], out=M1[:, :].unsqueeze(2), in_=v3(W4)[:, :, 15:16])
                    o("dve", "scalar_tensor_tensor", [M1.r(), W4.r()], [W5.r()], out=v3(W5), in0=v3(W4), scalar=1.0, in1=M1[:, :].unsqueeze(2).to_broadcast([64, 32, 32]), op0=ALU.mult, op1=ALU.subtract)
                    o("act", "activation", [M1.r()], [EM.r()], out=EM[:, :], in_=M1[:, :], func=AF.Exp)
                    o("act", "activation", [W4.r()], [ET.r()], out=ET[:, :].unsqueeze(2), in_=v3(W4)[:, :, 31:32], func=AF.Exp)
                    o("act", "activation", [W5.r()], [E2.r()], out=E2[:, :].unsqueeze(2), in_=v3(W5)[:, :, 31:32], func=AF.Exp)
                else:
                    o("dve", "tensor_tensor", [W4.r(), W2.r()], [W2.r()], out=W2[:, :], in0=W4[:, :], in1=W2[:, :], op=ALU.subtract)
                    o("dve", "tensor_copy", [W2.r()], [M1.r()], out=M1[:, :].unsqueeze(2), in_=v3(W2)[:, :, 16:17])
                    o("dve", "scalar_tensor_tensor", [M1.r(), W2.r()], [W5.r()], out=v3(W5), in0=v3(W2), scalar=-1.0, in1=M1[:, :].unsqueeze(2).to_broadcast([64, 32, 32]), op0=ALU.mult, op1=ALU.add)
                    o("act", "activation", [M1.r()], [E2.r()], out=E2[:, :], in_=M1[:, :], func=AF.Exp)
                    o("act", "activation", [W4.r()], [ET.r()], out=ET[:, :].unsqueeze(2), in_=v3(W4)[:, :, 31:32], func=AF.Exp)
                    o("dve", "tensor_tensor", [W4.r(), M1.r()], [EM.r()], out=EM[:, :].unsqueeze(2), in0=v3(W4)[:, :, 31:32], in1=M1[:, :].unsqueeze(2), op=ALU.subtract)
                    o("act", "activation", [EM.r()], [EM.r()], out=EM[:, :], in_=EM[:, :], func=AF.Exp)
                o("act", "activation", [W5.r()], [W1.r()], out=W1[:, :], in_=W5[0:64, :], func=AF.Exp)
                o("act", "activation", [W5.r()], [W2.r()], out=W2[:, :], in_=W5[0:64, :], func=AF.Exp, scale=-1.0)
                o("act", "activation", [Qb.r()], [Qb.r()], out=Qb[0:64, :], in_=Qb[0:64, :], func=AF.Silu)
                o("dve", "tensor_tensor", [Qb.r(), W1.r()], [QE.r()], out=QE[:, :], in0=Qb[0:64, :], in1=W1[:, :], op=ALU.mult)
                o("dve", "tensor_tensor", [W3.r(), W2.r()], [KE.r()], out=KE[:, :], in0=W3[:, :], in1=W2[:, :], op=ALU.mult)
                yield
                OF = W4
                if dirn == 1:
                    fw.dma("sp", ("dma_start", dict(out=OF[:, :].rearrange("p (h t) -> p h t", h=4), in_=self.sO[:, t0:t0 + 256].rearrange("(h d) t -> d h t", d=64))),
                           reads=[self.sO.r()], writes=[OF.r()])
                corder = list(range(8)) if dirn == 0 else list(range(7, -1, -1))
                OB = W5 if dirn == 1 else None
                bufs = {}
                for it in range(10):
                    cA = corder[it] if it < 8 else None
                    cU = corder[it - 1] if 1 <= it <= 8 else None
                    cB = corder[it - 2] if it >= 2 else None
                    if cA is not None:
                        par3 = self.nxt("hpar3", 3)
                        KTa, SCa = self.HKT3[par3], self.HSC3[par3]
                        bufs[cA] = (KTa, SCa)
                        psK = self.P[self.nxt("pa", 4)]
                        psS = self.P[self.nxt("pa", 4)]
                        for h in range(4):
                            cb = h * 256 + cA * 32
                            o("pe", "matmul", [KE.r(cb, cb + 32), self.identb.r()], [psK.r(h * 64, h * 64 + 64)], out=psK[0:32, h * 64:(h + 1) * 64], lhsT=KE[:, cb:cb + 32], rhs=self.identb[:, :], start=True, stop=True)
                            o("pe", "matmul", [KE.r(cb, cb + 32), QE.r(cb, cb + 32)], [psS.r(h * 32, h * 32 + 32)], out=psS[0:32, h * 32:(h + 1) * 32], lhsT=KE[:, cb:cb + 32], rhs=QE[:, cb:cb + 32], start=True, stop=True)
                    if cU is not None:
                        KTu = bufs[cU][0]
                        psU = self.P[4 + self.nxt("pb", 4)]
                        for h in range(4):
                            o("pe", "matmul", [KTu.r(), Vt.r()], [psU.r(h * 64, h * 64 + 64)], out=psU[0:64, h * 64:(h + 1) * 64], lhsT=KTu[0:32, h * 64:(h + 1) * 64], rhs=Vt[0:32, cU, h * 64:(h + 1) * 64], start=True, stop=True)
                    if cA is not None:
                        o("act", "activation", [psK.r(0, 256)], [KTa.r()], out=KTa[0:32, :], in_=psK[0:32, 0:256], func=AF.Identity)
                    if cB is not None:
                        c = cB
                        SCb = bufs.pop(c)[1]
                        emb = EM[:, :].rearrange("p (h c) -> p h c", c=8)[:, :, c:c + 1].to_broadcast([64, 4, 64])
                        etb = ET[:, :].rearrange("p (h c) -> p h c", c=8)[:, :, c:c + 1].to_broadcast([64, 4, 64])
                        s3 = S[:, :].rearrange("p (h v) -> p h v", h=4)
                        o("dve", "tensor_tensor", [S.r(), EM.r()], [SP.r()], out=SP[:, :].rearrange("p (h v) -> p h v", h=4), in0=s3, in1=emb, op=ALU.mult)
                        psO = self.P[4 + self.nxt("pb", 4)]
                        for h in range(4):
                            cb = h * 256 + c * 32
                            o("pe", "matmul", [Vt.r(), SCb.r()], [psO.r(h * 32, h * 32 + 32)], out=psO[0:64, h * 32:(h + 1) * 32], lhsT=Vt[0:32, c, h * 64:(h + 1) * 64], rhs=SCb[0:32, h * 32:(h + 1) * 32], start=True, stop=False)
                            o("pe", "matmul", [SP.r(), QE.r(cb, cb + 32)], [psO.r(h * 32, h * 32 + 32)], out=psO[0:64, h * 32:(h + 1) * 32], lhsT=SP[:, h * 64:(h + 1) * 64], rhs=QE[:, cb:cb + 32], start=False, stop=True)
                        dst = OF if dirn == 0 else OB
                        ofv = dst[0:64, :].rearrange("p (h t) -> p h t", h=4)[:, :, c * 32:(c + 1) * 32]
                        pov = psO[0:64, 0:128].rearrange("p (h t) -> p h t", h=4)
                        o("act", "activation", [psO.r(0, 128)], [dst.r()], out=ofv, in_=pov, func=AF.Identity)
                        o("dve", "tensor_tensor", [S.r(), ET.r()], [S.r()], out=s3, in0=s3, in1=etb, op=ALU.mult)
                        o("dve", "tensor_tensor", [S.r(), T1.r()], [S.r()], out=S[:, :], in0=S[:, :], in1=T1[:, :], op=ALU.add)
                    if cA is not None:
                        o("dve", "tensor_tensor", [psS.r(0, 128), self.hmask.r()], [SCa.r()], out=SCa[0:32, 0:128].rearrange("p (h t) -> p h t", h=4), in0=psS[0:32, 0:128].rearrange("p (h t) -> p h t", h=4),
                          in1=self.hmask[0:32, dirn, 0:32].unsqueeze(1).to_broadcast([32, 4, 32]), op=ALU.mult)
                    if cU is not None:
                        e2b = E2[:, :].rearrange("p (h c) -> p h c", c=8)[:, :, cU:cU + 1].to_broadcast([64, 4, 64])
                        o("dve", "tensor_tensor", [psU.r(0, 256), E2.r()], [T1.r()], out=T1[:, :].rearrange("p (h v) -> p h v", h=4), in0=psU[0:64, 0:256].rearrange("p (h v) -> p h v", h=4), in1=e2b, op=ALU.mult)
                    yield
                if dirn == 1:
                    o("dve", "tensor_tensor", [OF.r(), OB.r()], [OF.r()], out=OF[:, :], in0=OF[:, :], in1=OB[0:64, :], op=ALU.add)
                if dirn == 0:
                    fw.dma("sp", ("dma_start", dict(out=self.sO[:, t0:t0 + 256].rearrange("(h d) t -> d h t", d=64), in_=OF[:, :].rearrange("p (h t) -> p h t", h=4))),
                           reads=[OF.r()], writes=[self.sO.r(ti, ti + 1)])
                else:
                    o("act", "activation", [OF.r()], [QE.r()], out=QE[:, :], in_=OF[:, :], func=AF.Square)
                    for hf in range(2):
                        psR = self.P[self.nxt("pa", 4)]
                        o("pe", "matmul", [self.onesb.r(), QE.r()], [psR.r()], out=psR[0:64, :], lhsT=self.onesb[0:64, 0:64], rhs=QE[:, hf * 512:(hf + 1) * 512], start=True, stop=True)
                        o("act", "activation", [psR.r(), self.epsc.r()], [W1.r(hf * 512, hf * 512 + 512)], out=W1[:, hf * 512:(hf + 1) * 512], in_=psR[0:64, :], func=AF.Ln, bias=self.epsc[0:64, 0:1], scale=1.0 / 64)
                    o("act", "activation", [W1.r()], [W1.r()], out=W1[:, :], in_=W1[:, :], func=AF.Exp, scale=-0.5)
                    o("dve", "tensor_tensor", [OF.r(), W1.r()], [OF.r()], out=OF[:, :], in0=OF[:, :], in1=W1[:, :], op=ALU.mult)
                    ofh = OF[:, :].rearrange("p (h t) -> p h t", h=4)
                    o("dve", "tensor_tensor", [OF.r(), self.hng.r()], [OF.r()], out=ofh, in0=ofh, in1=self.hng[:, l, :].unsqueeze(2).to_broadcast([64, 4, 256]), op=ALU.mult)
                    fw.dma("sp", ("dma_start", dict(out=W3[:, :].rearrange("p (h t) -> p h t", h=4), in_=self.sA[3, :, t0:t0 + 256].rearrange("(h d) t -> d h t", d=64))),
                           reads=[self.sA.r()], writes=[W3.r()])
                    o("act", "activation", [W3.r()], [W3.r()], out=W3[:, :], in_=W3[:, :], func=AF.Silu)
                    o("dve", "tensor_tensor", [OF.r(), W3.r()], [KE.r()], out=KE[:, :], in0=OF[:, :], in1=W3[:, :], op=ALU.mult)
                    fw.dma("sp", ("dma_start", dict(out=self.mix[0:256, t0:t0 + 256].rearrange("(h d) t -> d h t", d=64), in_=KE[:, :].rearrange("p (h t) -> p h t", h=4))),
                           reads=[KE.r()], writes=[self.mix.r(ti, ti + 1)])
                yield


    def s5(self, l):
        fw = self.fw
        PI = float(np.pi)

        def o(q, name, reads, writes, **kw):
            fw.op(q, (name, kw), reads=reads, writes=writes)

        class Sl:
            def __init__(s_, buf, lo, n=256):
                s_.buf, s_.lo, s_.n = buf, lo, n
                s_.ap = buf[:, lo:lo + n]
                s_.rg = buf.r(lo, lo + n)

            def v4(s_):
                return s_.ap.rearrange("p (a b) -> p a b", b=64)

        pool = []
        for b, n in ((self.s5w[0], 512), (self.s5w[1], 512), (self.s5w[2], 512), (self.s5w[3], 512),
                     (self.rc, 512), (self.rs, 512), (self.rstd, 512)):
            for lo in range(0, n, 256):
                pool.append(Sl(b, lo))
        AR, AI, BQ, AD, ANG, Y, CN, T, U, RDEN, CR, CI, T2, MAG = pool[:14]

        def tt(out, a, b_, op, extra_r=()):
            o("dve", "tensor_tensor", [a.rg, b_.rg] + list(extra_r), [out.rg], out=out.ap, in0=a.ap, in1=b_.ap, op=op)

        def reduce_angle(xap, xregs, shift, out_ap, out_regs, y_ap, y_regs, c_ap, c_regs, t_ap, t_regs):
            o("dve", "tensor_scalar", xregs, y_regs, out=y_ap, in0=xap, scalar1=shift, scalar2=None, op0=ALU.add)
            o("dve", "tensor_scalar", y_regs, c_regs, out=c_ap, in0=y_ap, scalar1=PI, scalar2=None, op0=ALU.is_gt)
            for thr in (3 * PI, 5 * PI, 7 * PI):
                o("dve", "tensor_scalar", y_regs, t_regs, out=t_ap, in0=y_ap, scalar1=thr, scalar2=None, op0=ALU.is_gt)
                o("dve", "tensor_tensor", c_regs + t_regs, c_regs, out=c_ap, in0=c_ap, in1=t_ap, op=ALU.add)
            o("dve", "scalar_tensor_tensor", c_regs + y_regs, out_regs, out=out_ap, in0=c_ap, scalar=-2.0 * PI, in1=y_ap, op0=ALU.mult, op1=ALU.add)

        fw.dma("sp", ("dma_start", dict(out=AR.v4(), in_=self.s5Q[:, l, 0])), reads=[self.s5Q.r()], writes=[AR.rg])
        fw.dma("sp", ("dma_start", dict(out=AI.v4(), in_=self.s5Q[:, l, 1])), reads=[self.s5Q.r()], writes=[AI.rg])
        fw.dma("sp", ("dma_start", dict(out=BQ.v4(), in_=self.s5B[:, l].rearrange("p a h d -> p (a h) d"))), reads=[self.s5B.r()], writes=[BQ.rg])
        fw.dma("sp", ("dma_start", dict(out=self.sAs[:], in_=self.s5A[:, l])), reads=[self.s5A.r()], writes=[self.sAs.r()])
        fw.dma("pool", ("dma_start", dict(out=self.Cblk[:], in_=self.s5C[:, l])), reads=[self.s5C.r()], writes=[self.Cblk.r()])
        fw.dma("pool", ("dma_start", dict(out=self.gluwS[:], in_=self.gluw_d[l].rearrange("(h p) n -> p h n", p=128))), reads=[self.gluw_d.r()], writes=[self.gluwS.r()])
        o("dve", "tensor_scalar", [self.Cblk.r()], [self.Cblk.r()], out=self.Cblk[:, 1], in0=self.Cblk[:, 1], scalar1=-1.0, scalar2=None, op0=ALU.mult)

        dtq = self.dtq
        o("act", "activation", [self.ldq.r()], [dtq.r()], out=dtq[:, :], in_=self.ldq[:, l, :], func=AF.Exp)
        dtb = dtq[:, :].unsqueeze(2).to_broadcast([128, 4, 64])
        o("dve", "tensor_tensor", [AR.rg, dtq.r()], [AD.rg], out=AD.v4(), in0=AR.v4(), in1=dtb, op=ALU.mult)
        o("act", "activation", [AD.rg], [MAG.rg], out=MAG.ap, in_=AD.ap, func=AF.Exp)
        o("dve", "tensor_tensor", [AI.rg, dtq.r()], [ANG.rg], out=ANG.v4(), in0=AI.v4(), in1=dtb, op=ALU.mult)
        SN, CS = AD, U
        reduce_angle(ANG.ap, [ANG.rg], 0.0, SN.ap, [SN.rg], Y.ap, [Y.rg], CN.ap, [CN.rg], T.ap, [T.rg])
        o("act", "activation", [SN.rg], [SN.rg], out=SN.ap, in_=SN.ap, func=AF.Sin)
        reduce_angle(ANG.ap, [ANG.rg], PI / 2, CS.ap, [CS.rg], Y.ap, [Y.rg], CN.ap, [CN.rg], T.ap, [T.rg])
        o("act", "activation", [CS.rg], [CS.rg], out=CS.ap, in_=CS.ap, func=AF.Sin)
        ABR, ABI = CS, SN
        tt(ABR, MAG, CS, ALU.mult)
        tt(ABI, MAG, SN, ALU.mult)
        tt(T, AR, AR, ALU.mult)
        tt(Y, AI, AI, ALU.mult)
        tt(T, T, Y, ALU.add)
        o("dve", "reciprocal", [T.rg], [RDEN.rg], out=RDEN.ap, in_=T.ap)
        o("dve", "tensor_scalar", [ABR.rg], [ABR.rg], out=ABR.ap, in0=ABR.ap, scalar1=-1.0, scalar2=None, op0=ALU.add)
        tt(T, ABR, AR, ALU.mult)
        tt(Y, ABI, AI, ALU.mult)
        tt(T, T, Y, ALU.add)
        tt(CR, T, RDEN, ALU.mult)
        tt(T2, ABI, AR, ALU.mult)
        tt(Y, ABR, AI, ALU.mult)
        tt(T2, T2, Y, ALU.subtract)
        tt(CI, T2, RDEN, ALU.mult)
        BB = self.BB
        cr3 = CR.ap.rearrange("p (d x) -> p d x", d=2)
        ci3 = CI.ap.rearrange("p (d x) -> p d x", d=2)
        bre = BQ.ap[:, 0:128].unsqueeze(1).to_broadcast([128, 2, 128])
        bim = BQ.ap[:, 128:256].unsqueeze(1).to_broadcast([128, 2, 128])
        t3 = T.ap.rearrange("p (d x) -> p d x", d=2)
        y3 = Y.ap.rearrange("p (d x) -> p d x", d=2)
        bbr = BB[:, 0].rearrange("p a b -> p (a b)").rearrange("p (d x) -> p d x", d=2)
        bbi = BB[:, 1].rearrange("p a b -> p (a b)").rearrange("p (d x) -> p d x", d=2)
        o("dve", "tensor_tensor", [CR.rg, BQ.rg], [T.rg], out=t3, in0=cr3, in1=bre, op=ALU.mult)
        o("dve", "tensor_tensor", [CI.rg, BQ.rg], [Y.rg], out=y3, in0=ci3, in1=bim, op=ALU.mult)
        o("dve", "tensor_tensor", [T.rg, Y.rg], [BB.r(0, 256)], out=bbr, in0=t3, in1=y3, op=ALU.subtract)
        o("dve", "tensor_tensor", [CR.rg, BQ.rg], [T.rg], out=t3, in0=cr3, in1=bim, op=ALU.mult)
        o("dve", "tensor_tensor", [CI.rg, BQ.rg], [Y.rg], out=y3, in0=ci3, in1=bre, op=ALU.mult)
        o("dve", "tensor_tensor", [T.rg, Y.rg], [BB.r(256, 512)], out=bbi, in0=t3, in1=y3, op=ALU.add)

        As, DT, R, TH, C1, S1, Y16, N16, T16 = self.sAs, self.sDT, self.sR, self.sTH, self.sC1, self.sS1, self.sY16, self.sN16, self.sT16
        o("act", "activation", [As.r()], [DT.r()], out=DT[:, :], in_=As[:, 2, :], func=AF.Exp)
        o("dve", "tensor_tensor", [As.r(), DT.r()], [TH.r()], out=TH[:, :], in0=As[:, 0, :], in1=DT[:, :], op=ALU.mult)
        o("act", "activation", [TH.r()], [R.r()], out=R[:, :], in_=TH[:, :], func=AF.Exp)
        o("dve", "tensor_tensor", [As.r(), DT.r()], [TH.r()], out=TH[:, :], in0=As[:, 1, :], in1=DT[:, :], op=ALU.mult)
        reduce_angle(TH[:, :], [TH.r()], 0.0, S1[:, :], [S1.r()], Y16[:, :], [Y16.r()], N16[:, :], [N16.r()], T16[:, :], [T16.r()])
        o("act", "activation", [S1.r()], [S1.r()], out=S1[:, :], in_=S1[:, :], func=AF.Sin)
        reduce_angle(TH[:, :], [TH.r()], PI / 2, C1[:, :], [C1.r()], Y16[:, :], [Y16.r()], N16[:, :], [N16.r()], T16[:, :], [T16.r()])
        o("act", "activation", [C1.r()], [C1.r()], out=C1[:, :], in_=C1[:, :], func=AF.Sin)

        uTb, ytF, ytB = self.uTb, self.ytF, self.ytB
        u3 = uTb[:, :].rearrange("p (h t) -> p h t", h=2)
        fw.dma("pool", ("dma_start", dict(out=u3, in_=self.sU[:, :].rearrange("(h p) t -> p h t", p=128))), reads=[self.sU.r()], writes=[uTb.r()])
        Ct, St = self.rc, self.rs
        wsets = [self.s5w, self.s5w2]
        hsets = [(self.sq[0], self.sq[1]), (self.sq2[0], self.sq2[1])]
        tcnt = 0
        pend = None
        yield
        hp = self.hprev
        for dirn in range(2):
            yt = ytF if dirn == 0 else ytB
            yt3 = yt[:, :].rearrange("p (b c) -> p b c", c=256)
            if dirn == 1:
                for bt in range(18):
                    tb = (1 - bt) if bt < 2 else (19 - bt)
                    ub = self.s5ub[self.nxt("s5ub", 2)]
                    fw.dma("sp", ("dma_start", dict(out=ub[:, 0:256], in_=self.sUt[tb * 128:(tb + 1) * 128, :])), reads=[self.sUt.r()], writes=[ub.r(0, 256)])
                    ps = self.P[self.nxt("pa", 4)]
                    for half in range(2):
                        o("pe", "matmul", [ub.r(0, 256), self.Jb.r()], [ps.r(half * 128, half * 128 + 128)], out=ps[:, half * 128:(half + 1) * 128],
                          lhsT=ub[:, half * 128:(half + 1) * 128], rhs=self.Jb[:, :], start=True, stop=True)
                    o("act", "activation", [ps.r(0, 256)], [uTb.r(bt * 128, bt * 128 + 128), uTb.r(2304 + bt * 128, 2304 + bt * 128 + 128)],
                      out=u3[:, :, bt * 128:(bt + 1) * 128], in_=ps[:, 0:256].rearrange("p (h t) -> p h t", h=2), func=AF.Identity)
                    if bt % 3 == 2:
                        yield
            for st in range(8):
                ds = dirn * 8 + st
                half = st // 4
                hd = dirn * 2 + half
                o("act", "activation", [C1.r()], [Ct.r(0, 1)], out=Ct[:, 0:1], in_=C1[:, ds:ds + 1], func=AF.Identity)
                o("act", "activation", [S1.r()], [St.r(0, 1)], out=St[:, 0:1], in_=S1[:, ds:ds + 1], func=AF.Identity)
                gr, gi = wsets[0][2], wsets[0][3]
                m = 1
                while m < 512:
                    cm, sm = Ct[:, m - 1:m], St[:, m - 1:m]
                    o("dve", "tensor_scalar", [St.r(0, m)], [gr.r(0, m)], out=gr[:, 0:m], in0=St[:, 0:m], scalar1=sm, scalar2=None, op0=ALU.mult)
                    o("dve", "scalar_tensor_tensor", [Ct.r(0, m), gr.r(0, m)], [Ct.r(m, 2 * m)], out=Ct[:, m:2 * m], in0=Ct[:, 0:m], scalar=cm, in1=gr[:, 0:m], op0=ALU.mult, op1=ALU.subtract)
                    o("dve", "tensor_scalar", [Ct.r(0, m), St.r(0, m)], [gi.r(0, m)], out=gi[:, 0:m], in0=Ct[:, 0:m], scalar1=sm, scalar2=None, op0=ALU.mult)
                    o("dve", "scalar_tensor_tensor", [St.r(0, m), Ct.r(0, m), gi.r(0, m)], [St.r(m, 2 * m)], out=St[:, m:2 * m], in0=St[:, 0:m], scalar=cm, in1=gi[:, 0:m], op0=ALU.mult, op1=ALU.add)
                    m *= 2
                yield
                for ri in range(2):
                    for gl in range(2):
                        j = 2 * (st % 4) + gl
                        o("dve", "tensor_scalar", [BB.r(), self.rowm.r()], [self.Bblk.r(ri * 128 + gl * 64, ri * 128 + gl * 64 + 64)],
                          out=self.Bblk[:, ri, gl * 64:(gl + 1) * 64], in0=BB[:, ri, hd, :], scalar1=self.rowm[:, j:j + 1], scalar2=None, op0=ALU.mult)
                rb = R[:, ds:ds + 1]
                for ti, (t0, n) in enumerate(TILES):
                    dr, di, gr, gi = wsets[tcnt % 2]
                    hr, hi = hsets[tcnt % 2]
                    tcnt += 1
                    pr = self.P[self.nxt("pa", 4)]
                    pi_ = self.P[self.nxt("pa", 4)]
                    c0 = half * NTOK + t0
                    o("pe", "matmul", [self.Bblk.r(0, 128), uTb.r(c0, c0 + n)], [pr.r(0, n)], out=pr[:, :n], lhsT=self.Bblk[:, 0, :], rhs=uTb[:, c0:c0 + n], start=True, stop=True)
                    o("pe", "matmul", [self.Bblk.r(128, 256), uTb.r(c0, c0 + n)], [pi_.r(0, n)], out=pi_[:, :n], lhsT=self.Bblk[:, 1, :], rhs=uTb[:, c0:c0 + n], start=True, stop=True)
                    o("dve", "tensor_tensor", [pi_.r(0, n), St.r(0, n)], [gr.r(0, n)], out=gr[:, :n], in0=pi_[:, :n], in1=St[:, :n], op=ALU.mult)
                    o("dve", "tensor_tensor", [pr.r(0, n), Ct.r(0, n)], [dr.r(0, n)], out=dr[:, :n], in0=pr[:, :n], in1=Ct[:, :n], op=ALU.mult)
                    o("dve", "tensor_tensor", [dr.r(0, n), gr.r(0, n)], [dr.r(0, n)], out=dr[:, :n], in0=dr[:, :n], in1=gr[:, :n], op=ALU.add)
                    o("dve", "tensor_tensor", [pr.r(0, n), St.r(0, n)], [gi.r(0, n)], out=gi[:, :n], in0=pr[:, :n], in1=St[:, :n], op=ALU.mult)
                    o("dve", "tensor_tensor", [pi_.r(0, n), Ct.r(0, n)], [di.r(0, n)], out=di[:, :n], in0=pi_[:, :n], in1=Ct[:, :n], op=ALU.mult)
                    o("dve", "tensor_tensor", [di.r(0, n), gi.r(0, n)], [di.r(0, n)], out=di[:, :n], in0=di[:, :n], in1=gi[:, :n], op=ALU.subtract)
                    ini_r = 0.0 if ti == 0 else hp[:, 0:1]
                    ini_i = 0.0 if ti == 0 else hp[:, 1:2]
                    o("dve", "tensor_tensor_scan", [R.r(), dr.r(0, n), hp.r()], [gr.r(0, n)], out=gr[:, :n], data0=rb.to_broadcast([128, n]), data1=dr[:, :n], initial=ini_r, op0=ALU.mult, op1=ALU.add)
                    o("dve", "tensor_tensor_scan", [R.r(), di.r(0, n), hp.r()], [gi.r(0, n)], out=gi[:, :n], data0=rb.to_broadcast([128, n]), data1=di[:, :n], initial=ini_i, op0=ALU.mult, op1=ALU.add)
                    if ti < len(TILES) - 1:
                        hpt = self.hpt
                        cl, sl = Ct[:, n - 1:n], St[:, n - 1:n]
                        o("dve", "tensor_scalar", [gi.r(0, n), St.r(0, n)], [hpt.r(0, 1)], out=hpt[:, 0:1], in0=gi[:, n - 1:n], scalar1=sl, scalar2=None, op0=ALU.mult)
                        o("dve", "tensor_scalar", [gr.r(0, n), St.r(0, n)], [hpt.r(1, 2)], out=hpt[:, 1:2], in0=gr[:, n - 1:n], scalar1=sl, scalar2=None, op0=ALU.mult)
                        o("dve", "scalar_tensor_tensor", [gr.r(0, n), Ct.r(0, n), hpt.r(0, 1)], [hp.r(0, 1)], out=hp[:, 0:1], in0=gr[:, n - 1:n], scalar=cl, in1=hpt[:, 0:1], op0=ALU.mult, op1=ALU.subtract)
                        o("dve", "scalar_tensor_tensor", [gi.r(0, n), Ct.r(0, n), hpt.r(1, 2)], [hp.r(1, 2)], out=hp[:, 1:2], in0=gi[:, n - 1:n], scalar=cl, in1=hpt[:, 1:2], op0=ALU.mult, op1=ALU.add)
                    o("pool", "tensor_tensor", [gi.r(0, n), St.r(0, n)], [dr.r(0, n)], out=dr[:, :n], in0=gi[:, :n], in1=St[:, :n], op=ALU.mult)
                    o("pool", "tensor_tensor", [gr.r(0, n), St.r(0, n)], [di.r(0, n)], out=di[:, :n], in0=gr[:, :n], in1=St[:, :n], op=ALU.mult)
                    o("pool", "tensor_tensor", [gr.r(0, n), Ct.r(0, n)], [gr.r(0, n)], out=gr[:, :n], in0=gr[:, :n], in1=Ct[:, :n], op=ALU.mult)
                    o("pool", "tensor_tensor", [gi.r(0, n), Ct.r(0, n)], [gi.r(0, n)], out=gi[:, :n], in0=gi[:, :n], in1=Ct[:, :n], op=ALU.mult)
                    o("pool", "tensor_tensor", [gr.r(0, n), dr.r(0, n)], [hr.r(0, n)], out=hr[:, :n], in0=gr[:, :n], in1=dr[:, :n], op=ALU.subtract)
                    o("pool", "tensor_tensor", [gi.r(0, n), di.r(0, n)], [hi.r(0, n)], out=hi[:, :n], in0=gi[:, :n], in1=di[:, :n], op=ALU.add)
                    def readout(hr=hr, hi=hi, n=n, t0=t0, ds=ds, st=st, yt=yt, yt3=yt3):
                        py = self.P[4 + self.nxt("pb", 4)]
                        nb, b0 = n // 128, t0 // 128
                        for j in range(nb):
                            o("pe", "matmul", [hr.r(j * 128, j * 128 + 128), self.Cblk.r()], [py.r(j * 32, j * 32 + 32)], out=py[:, j * 32:(j + 1) * 32],
                              lhsT=hr[:, j * 128:(j + 1) * 128], rhs=self.Cblk[:, 0, ds, :], start=True, stop=False)
                            o("pe", "matmul", [hi.r(j * 128, j * 128 + 128), self.Cblk.r()], [py.r(j * 32, j * 32 + 32)], out=py[:, j * 32:(j + 1) * 32],
                              lhsT=hi[:, j * 128:(j + 1) * 128], rhs=self.Cblk[:, 1, ds, :], start=False, stop=True)
                        o("act", "activation", [py.r(0, nb * 32)], [yt.r((b0 + j) * 256 + st * 32, (b0 + j) * 256 + st * 32 + 32) for j in range(nb)],
                          out=yt3[:, b0:b0 + nb, st * 32:(st + 1) * 32], in_=py[:, 0:nb * 32].rearrange("p (b c) -> p b c", c=32), func=AF.Identity)

                    if pend is not None:
                        pend()
                    pend = readout
                    yield
        pend()
        ytF3 = ytF[:, :].rearrange("p (b c) -> p b c", c=256)
        ytB3 = ytB[:, :].rearrange("p (b c) -> p b c", c=256)
        for ti, (t0, n) in enumerate(TILES):
            nb, b0 = n // 128, t0 // 128
            gys = []
            for half in range(2):
                pc = self.P[self.nxt("pa", 4)]
                for j in range(nb):
                    tb = b0 + j
                    bt = (1 - tb) if tb < 2 else (19 - tb)
                    o("pe", "matmul", [ytF.r(tb * 256, tb * 256 + 256), self.Ib.r()], [pc.r(j * 128, j * 128 + 128)], out=pc[:, j * 128:(j + 1) * 128],
                      lhsT=ytF3[:, tb, half * 128:(half + 1) * 128], rhs=self.Ib[:, :], start=True, stop=False)
                    o("pe", "matmul", [ytB.r(bt * 256, bt * 256 + 256), self.Jb.r()], [pc.r(j * 128, j * 128 + 128)], out=pc[:, j * 128:(j + 1) * 128],
                      lhsT=ytB3[:, bt, half * 128:(half + 1) * 128], rhs=self.Jb[:, :], start=False, stop=True)
                ut = self.s5w[half]
                fw.dma("sp", ("dma_start", dict(out=ut[:, :n], in_=self.sU[half * 128:(half + 1) * 128, t0:t0 + n])), reads=[self.sU.r()], writes=[ut.r(0, n)])
                yv = self.s5w[2 + half]
                o("dve", "scalar_tensor_tensor", [ut.r(0, n), self.s5dS.r(), pc.r(0, n)], [yv.r(0, n)], out=yv[:, :n], in0=ut[:, :n], scalar=self.s5dS[:, l, half:half + 1], in1=pc[:, :n], op0=ALU.mult, op1=ALU.add)
                o("dve", "tensor_tensor", [yv.r(0, n)], [ut.r(0, n)], out=ut[:, :n], in0=yv[:, :n], in1=yv[:, :n], op=ALU.mult)
                o("dve", "tensor_scalar", [ut.r(0, n)], [ut.r(0, n)], out=ut[:, :n], in0=ut[:, :n], scalar1=0.044715, scalar2=1.0, op0=ALU.mult, op1=ALU.add)
                o("dve", "tensor_tensor", [ut.r(0, n), yv.r(0, n)], [ut.r(0, n)], out=ut[:, :n], in0=ut[:, :n], in1=yv[:, :n], op=ALU.mult)
                o("act", "activation", [ut.r(0, n)], [ut.r(0, n)], out=ut[:, :n], in_=ut[:, :n], func=AF.Sigmoid, scale=1.5957691216057308)
                gy = self.s5gy[half]
                o("dve", "tensor_tensor", [ut.r(0, n), yv.r(0, n)], [gy.r(0, n)], out=gy[:, :n], in0=ut[:, :n], in1=yv[:, :n], op=ALU.mult)
                gys.append(gy)
            pm = [self.P[4 + m_] for m_ in range(4)]
            for m_ in range(4):
                for half in range(2):
                    o("pe", "matmul", [self.gluwS.r(), gys[half].r(0, n)], [pm[m_].r(0, n)], out=pm[m_][:, :n],
                      lhsT=self.gluwS[:, half, m_ * 128:(m_ + 1) * 128], rhs=gys[half][:, :n], start=(half == 0), stop=(half == 1))
            for mm in range(2):
                sgm = self.rstd
                o("act", "activation", [pm[2 + mm].r(0, n), self.glubS.r()], [sgm.r(0, n)], out=sgm[:, :n], in_=pm[2 + mm][:, :n], func=AF.Sigmoid, bias=self.glubS[:, l, 2 + mm:3 + mm], scale=1.0)
                ob = self.s5ob
                o("dve", "scalar_tensor_tensor", [pm[mm].r(0, n), self.glubS.r(), sgm.r(0, n)], [ob.r(0, n)], out=ob[:, :n], in0=pm[mm][:, :n], scalar=self.glubS[:, l, mm:mm + 1], in1=sgm[:, :n], op0=ALU.add, op1=ALU.mult)
                fw.dma("sp", ("dma_start", dict(out=self.mix[768 + mm * 128:768 + (mm + 1) * 128, t0:t0 + n], in_=ob[:, :n])), reads=[ob.r(0, n)], writes=[self.mix.r(16 + ti, 17 + ti)])
            yield

    def zero_mix(self, r0, r1):
        fw = self.fw
        fw.op("dve", ("memset", dict(ap=self.zt[:], constant=0.0)), writes=[self.zt.r()])
        for rr in range(r0, r1, 128):
            for ti, (t0, n) in enumerate(TILES):
                fw.dma("sp", ("dma_start", dict(out=self.mix[rr:rr + 128, t0:t0 + n], in_=self.zt[:, :n])), reads=[self.zt.r()], writes=[self.mix.r(ti, ti + 1)])

    def attn(self, l):
        fw = self.fw
        fw.dma("sp", ("dma_start", dict(out=self.kTall[:], in_=self.sK[:, :])), reads=[self.sK.r()], writes=[self.kTall.r()])
        fw.dma("sp", ("dma_start", dict(out=self.vt[:], in_=self.sV[:, :].rearrange("(b p) c -> p b c", p=128))), reads=[self.sV.r()], writes=[self.vt.r()])
        for hk in range(2):
            for qb in range(18):
                tq = qb * 128
                qt = self.qt[self.nxt("qt", 2)]
                pb_ = hk * 64
                fw.dma("sp", ("dma_start", dict(out=qt[pb_:pb_ + 64, :].rearrange("d (h t) -> d h t", h=4),
                                                 in_=self.sQ[hk * 256:(hk + 1) * 256, tq:tq + 128].rearrange("(h d) t -> d h t", d=64))),
                       reads=[self.sQ.r()], writes=[qt.r()])
                kbs = [(0, None), (1, None)]
                if qb >= 2:
                    for dl, mk in ((-1, self.mprev), (0, None), (1, self.mnext)):
                        kb = qb + dl
                        if 2 <= kb <= 17:
                            kbs.append((kb, mk))
                pp = self.nxt("pb", 2)
                po, pd = self.P[4 + 2 * pp], self.P[5 + 2 * pp]
                for idx, (kb, mk) in enumerate(kbs):
                    ps = self.P[self.nxt("pa", 4)]
                    fw.op("pe", ("matmul", dict(out=ps[:, :], lhsT=self.kTall[pb_:pb_ + 64, kb * 128:(kb + 1) * 128], rhs=qt[pb_:pb_ + 64, :], start=True, stop=True)),
                          reads=[self.kTall.r(kb * 128, (kb + 1) * 128), qt.r()], writes=[ps.r()])
                    pT = self.pT[self.nxt("pT", 3)]
                    fw.op("act", ("activation", dict(out=pT[:, :], in_=ps[:, :], func=AF.Exp, scale=0.125)), reads=[ps.r()], writes=[pT.r()])
                    if mk is not None:
                        fw.op("dve", ("tensor_tensor", dict(out=pT[:, :].rearrange("p (h t) -> p h t", h=4), in0=pT[:, :].rearrange("p (h t) -> p h t", h=4),
                                                             in1=mk[:, :].unsqueeze(1).to_broadcast([128, 4, 128]), op=ALU.mult)),
                              reads=[pT.r(), mk.r()], writes=[pT.r()])
                    st, sp_ = (idx == 0), (idx == len(kbs) - 1)
                    fw.op("pe", ("matmul", dict(out=po[0:64, :], lhsT=self.vt[:, kb, hk * 64:(hk + 1) * 64], rhs=pT[:, :], start=st, stop=sp_)),
                          reads=[self.vt.r(), pT.r()], writes=[po.r()])
                    fw.op("pe", ("matmul", dict(out=pd[0:64, :], lhsT=self.onesb[:, 0:64], rhs=pT[:, :], start=st, stop=sp_)),
                          reads=[self.onesb.r(), pT.r()], writes=[pd.r()])
                den = self.tmp[self.nxt("tmp", 2)]
                fw.op("dve", ("tensor_tensor", dict(out=den[0:64, :].rearrange("p (h t) -> p h t", h=4), in0=pd[0:64, :].rearrange("p (h t) -> p h t", h=4),
                                                     in1=self.esink[:, l, hk * 4:(hk + 1) * 4].unsqueeze(2).to_broadcast([64, 4, 128]), op=ALU.add)),
                      reads=[pd.r(), self.esink.r()], writes=[den.r()])
                fw.op("act", ("activation", dict(out=den[0:64, :], in_=den[0:64, :], func=AF.Ln)), reads=[den.r()], writes=[den.r()])
                fw.op("act", ("activation", dict(out=den[0:64, :], in_=den[0:64, :], func=AF.Exp, scale=-1.0)), reads=[den.r()], writes=[den.r()])
                ob = self.ob[self.nxt("ob", 2)]
                fw.op("dve", ("tensor_tensor", dict(out=ob[0:64, :], in0=po[0:64, :], in1=den[0:64, :], op=ALU.mult)), reads=[po.r(), den.r()], writes=[ob.r()])
                ti = 0 if qb < 2 else 1 + (qb - 2) // 4
                fw.dma("sp", ("dma_start", dict(out=self.mix[256 + hk * 256:256 + (hk + 1) * 256, tq:tq + 128].rearrange("(h d) t -> d h t", d=64),
                                                 in_=ob[0:64, :].rearrange("d (h t) -> d h t", h=4))),
                       reads=[ob.r()], writes=[self.mix.r(8 + ti, 9 + ti)])
                yield

    def outproj(self, l, tiles):
        fw, xT, hT = self.fw, self.xT, self.hT
        wv = self.wout[l].rearrange("(k p) n -> p k n", p=128)
        for c in range(2):
            fw.dma("pool", ("dma_start", dict(out=self.wA[c][:], in_=wv[:, :, c * 512:(c + 1) * 512])), reads=[self.wout.r()], writes=[self.wA[c].r()])
        for (t0, n) in tiles:
            var = 1 if t0 == 0 else 0
            fw.dma("sp", ("dma_start", dict(out=hT[:, :, t0:t0 + n], in_=self.mix[:, t0:t0 + n].rearrange("(k p) t -> p k t", p=128))),
                   reads=[self.mix.r()], writes=[hT.r(k * NTOK + t0, k * NTOK + t0 + n) for k in range(8)])
            for o in range(8):
                po = self.P[4 + self.nxt("pb", 4)]
                self.mm8(po, n, self.wA[o // 4], (o % 4) * 128, t0)
                fw.op("dve", ("scalar_tensor_tensor", dict(
                    out=xT[:, o, t0:t0 + n], in0=po[:, :n], scalar=self.GT[:, l, 1, o, var:var + 1], in1=xT[:, o, t0:t0 + n], op0=ALU.mult, op1=ALU.add)),
                    reads=[po.r(0, n), self.GT.r(), xT.r(o * NTOK + t0, o * NTOK + t0 + n)], writes=[xT.r(o * NTOK + t0, o * NTOK + t0 + n)])

    def final(self):
        fw, xT = self.fw, self.xT
        for (t0, n) in TILES[1:]:
            self.rms_stats(t0, n)
            for sb4 in range(n // 128):
                ot = self.xin[self.nxt("xin", 2)]
                tb = t0 + sb4 * 128
                for k4 in range(2):
                    ps = self.P[self.nxt("pa", 4)]
                    for kk in range(4):
                        k = k4 * 4 + kk
                        tmp = self.tmp[self.nxt("tmp", 2)]
                        fw.op("dve", ("scalar_tensor_tensor", dict(
                            out=tmp[:, :128], in0=xT[:, k, tb:tb + 128], scalar=self.fngs[:, k:k + 1], in1=self.rstd[:, sb4 * 128:(sb4 + 1) * 128], op0=ALU.mult, op1=ALU.mult)),
                            reads=[xT.r(k * NTOK + tb, k * NTOK + tb + 128), self.fngs.r(), self.rstd.r(0, n)], writes=[tmp.r(0, 128)])
                        fw.op("pe", ("transpose", dict(out=ps[:, kk * 128:(kk + 1) * 128], in_=tmp[:, :128], identity=self.ident[:])),
                              reads=[tmp.r(0, 128), self.ident.r()], writes=[ps.r(kk * 128, (kk + 1) * 128)])
                    fw.op("act", ("activation", dict(out=ot[:, k4 * 512:(k4 + 1) * 512], in_=ps[:], func=AF.Identity)),
                          reads=[ps.r()], writes=[ot.r(k4 * 512, (k4 + 1) * 512)])
                r0 = tb - NCTX
                fw.dma("sp", ("dma_start", dict(out=self.y[r0:r0 + 128, :], in_=ot[:])), reads=[ot.r()], writes=[self.y.r()])

    def build(self):
        self.prologue()
        last = DEPTH - 1
        for l in range(DEPTH):
            if self.stage < 1:
                break
            self.norm_h(l, 0, TILES)
            self.ffn(l, 0, TILES)
            if self.stage < 2:
                break
            tl = TILES if l < last else TILES[1:]
            if self.stage >= 3:
                self.norm_h(l, 1, TILES)
                self.inproj(l)
                gens = [self.attn(l)]
                wts = {}
                if self.stage < 4:
                    self.zero_mix(0, 256)
                else:
                    hg = self.hgrn(l)
                    gens.insert(0, hg)
                    wts[id(hg)] = 2
                if self.stage < 5:
                    self.zero_mix(768, 1024)
                else:
                    gens.append(self.s5(l))
                if not INTERLEAVE:
                    for g in gens:
                        for _ in g:
                            pass
                else:
                    while gens:
                        for g in list(gens):
                            try:
                                for _ in range(wts.get(id(g), 1)):
                                    next(g)
                            except StopIteration:
                                gens.remove(g)
                self.outproj(l, tl)
                if self.stage == 3 and l == 0:
                    break
            self.norm_h(l, 2, tl)
            self.ffn(l, 2, tl)
        self.final()
        self.fw.wait_all("sp")
        self.fw.emit()
        return self.nc


def host_prep(inp, b):
    f32 = np.float32
    d = {}
    d["x_in"] = np.ascontiguousarray(np.concatenate([inp["ctx"][b], inp["x"][b]], axis=0), dtype=f32)
    cv = np.stack([inp["c"][b], inp["c_ctx"]], axis=-1)
    d["cvec"] = np.ascontiguousarray(cv.reshape(8, 128, 2).transpose(1, 0, 2), dtype=f32)
    d["ada_w"] = inp["ada_w"]
    d["ada_b_t"] = np.ascontiguousarray(inp["ada_b"].reshape(DEPTH, 72, 128).transpose(2, 0, 1), dtype=f32)
    d["norm_g_t"] = np.ascontiguousarray(inp["norm_g"].reshape(DEPTH, 3, 8, 128).transpose(3, 0, 1, 2), dtype=f32)
    d["fng_t"] = np.ascontiguousarray(inp["final_norm_g"].reshape(8, 128).T, dtype=f32)
    d["ffn_w1"] = inp["ffn_w1"]
    d["ffn_w2"] = inp["ffn_w2"]
    d["ident"] = np.eye(128, dtype=f32)
    d["win_ext"] = _WIN_EXT(inp)
    d["w_out"] = inp["w_out"]
    rc, rs = _ROPE()
    d["ropeC"], d["ropeS"] = rc, rs
    ii = np.arange(128)[:, None]
    jj = np.arange(128)[None, :]
    d["mprev"] = (jj <= ii).astype(f32)
    d["mnext"] = (ii <= jj).astype(f32)
    d["lb_t"] = np.ascontiguousarray(inp["hgrn_lower_bounds"].reshape(DEPTH, 2, 4, 64).transpose(3, 0, 1, 2).reshape(64, DEPTH, 8), dtype=f32)
    d["hng_t"] = np.ascontiguousarray(inp["hgrn_norm_g"].reshape(DEPTH, 4, 64).transpose(2, 0, 1), dtype=f32)
    rm = np.ones((64, 1024), f32)
    rm[:, ::32] = 0.0
    d["rmask"] = rm
    si = np.arange(64)[:, None]
    tj = np.arange(64)[None, :]
    d["hmask"] = np.ascontiguousarray(np.stack([(si <= tj), (si >= tj)], axis=1).astype(f32))
    d.update(_S5(inp))
    d["sink_t"] = np.ascontiguousarray(np.broadcast_to(inp["attn_sink"][None], (64, DEPTH, 8)), dtype=f32)
    return d


_HC = {}


def _WIN_EXT(inp):
    if "win" in _HC:
        return _HC["win"]
    w = inp["w_in"]
    aq, ai, af, ab, ag = (w[:, :, i * 256:(i + 1) * 256] for i in range(5))
    bq = w[:, :, 1280:1792]
    bk = w[:, :, 1792:1920]
    bv = w[:, :, 1920:2048]
    cu = w[:, :, 2048:2304]
    perm = np.arange(64).reshape(2, 2, 16)[:, ::-1, :].reshape(64)

    def partner(m):
        nh = m.shape[-1] // 64
        idx = (np.arange(nh)[:, None] * 64 + perm[None, :]).reshape(-1)
        return m[:, :, idx]

    bqp, bkp = partner(bq), partner(bk)
    cols = [aq, af, ab, ag]
    for j in range(4):
        cols += [bq[:, :, j * 128:(j + 1) * 128], bqp[:, :, j * 128:(j + 1) * 128]]
    cols += [bk, bkp, cu, ai, bv, cu]
    _HC["win"] = np.ascontiguousarray(np.concatenate(cols, axis=-1), dtype=np.float32)
    assert _HC["win"].shape[-1] == 3200
    return _HC["win"]


def _S5(inp):
    if "s5" in _HC:
        return _HC["s5"]
    f32 = np.float32
    L = DEPTH
    are, aim, ldt = inp["s5_a_re"], inp["s5_a_im"], inp["s5_log_dt"]

    def st_layout(a):
        return a.reshape(L, 2, 8, 2, 64).transpose(3, 4, 0, 1, 2).reshape(128, L, 16)

    ld_s = np.broadcast_to(ldt.reshape(L, 2, 8, 2, 1), (L, 2, 8, 2, 64)).transpose(3, 4, 0, 1, 2).reshape(128, L, 16)
    As = np.stack([st_layout(are), st_layout(aim), ld_s], axis=2)

    def q_layout(a):
        t = a.reshape(L, 2, 2, 8, 64).transpose(3, 0, 1, 2, 4)
        t = np.repeat(t[:, None], 16, axis=1)
        return t.reshape(128, L, 4, 64)

    Aq = np.stack([q_layout(are), q_layout(aim)], axis=2)
    ldq = np.repeat(ldt.reshape(L, 2, 2, 8).transpose(3, 0, 1, 2)[:, None], 16, axis=1).reshape(128, L, 4)

    def b_layout(b):
        return b.reshape(L, 2, 8, 64, 16).transpose(2, 4, 0, 1, 3).reshape(128, L, 2, 64)

    Bq = np.stack([b_layout(inp["s5_b_re"]), b_layout(inp["s5_b_im"])], axis=2)

    def c_layout(c):
        t = c.reshape(L, 2, 8, 2, 16, 64)
        out = np.zeros((2, 64, L, 2, 8, 2, 16), f32)
        for gl in range(2):
            out[gl, :, :, :, :, gl, :] = t[:, :, :, gl, :, :].transpose(4, 0, 1, 2, 3)
        return out.reshape(128, L, 16, 32)

    Cb = np.stack([c_layout(inp["s5_c_re"]), c_layout(inp["s5_c_im"])], axis=2)
    c = lambda a: np.ascontiguousarray(a, dtype=f32)
    _HC["s5"] = {
        "s5_As": c(As), "s5_Aq": c(Aq), "s5_LDq": c(ldq), "s5_Bq": c(Bq), "s5_Cblk": c(Cb),
        "s5_d_t": c(inp["s5_d"].reshape(L, 2, 128).transpose(2, 0, 1)),
        "glu_b_t": c(inp["s5_glu_b"].reshape(L, 4, 128).transpose(2, 0, 1)),
        "s5_glu_w": inp["s5_glu_w"],
        "rowmask": c(np.arange(128)[:, None] // 16 == np.arange(8)[None, :]),
        "Jmat": c(np.eye(128)[::-1]),
    }
    return _HC["s5"]


def _ROPE():
    if "rope" in _HC:
        return _HC["rope"]
    f32 = np.float32
    t = np.arange(NLAT)
    row = (t // 64).astype(f32)
    col = (t % 64).astype(f32)
    inv = (f32(10000.0) ** (-np.arange(16, dtype=f32) / f32(16))).astype(f32)
    ang = np.stack([row[:, None] * inv, col[:, None] * inv], axis=1).astype(f32)
    cos, sin = np.cos(ang).astype(f32), np.sin(ang).astype(f32)
    C = np.ones((64, NTOK), f32)
    S = np.zeros((64, NTOK), f32)
    for ax in range(2):
        for two in range(2):
            d0 = ax * 32 + two * 16
            C[d0:d0 + 16, NCTX:] = cos[:, ax, :].T
            S[d0:d0 + 16, NCTX:] = (-sin[:, ax, :].T if two == 0 else sin[:, ax, :].T)
    _HC["rope"] = (np.ascontiguousarray(np.tile(C, (2, 1))), np.ascontiguousarray(np.tile(S, (2, 1))))
    return _HC["rope"]


_NC_CACHE = {}


def kernel(**inputs):
    inp = {k: np.asarray(v) for k, v in inputs.items()}
    stage = int(os.environ.get("KSTAGE", "99"))
    ncores = int(os.environ.get("KCORES", "8"))
    if stage not in _NC_CACHE:
        _NC_CACHE[stage] = K(stage).build()
    nc = _NC_CACHE[stage]
    in_maps = [host_prep(inp, b) for b in range(ncores)]
    res = run_bass_kernel_spmd(nc, in_maps, core_ids=list(range(ncores)))
    out = np.stack([np.asarray(r["y"]) for r in res.results], axis=0)
    return out.astype(np.float32)
```

```python
import os
import numpy as np
import concourse.bass as bass
import concourse.mybir as mybir
from concourse.bass_utils import run_bass_kernel_spmd

F32 = mybir.dt.float32
BF16 = mybir.dt.bfloat16
AF = mybir.ActivationFunctionType
ALU = mybir.AluOpType

NTOK = 2304
NCTX = 256
NLAT = 2048
DM = 1024
DFF = 2816
DEPTH = 4
TILES = [(0, 256), (256, 512), (768, 512), (1280, 512), (1792, 512)]
EPS = 1e-6
INTERLEAVE = True


class Reg:
    __slots__ = ("base", "lo", "hi")

    def __init__(self, base, lo, hi):
        self.base, self.lo, self.hi = base, lo, hi


class Buf:
    _n = 0

    def __init__(self, t, size, base=None, off=0):
        self.t = t
        self.size = size
        if base is None:
            Buf._n += 1
            base = "b%d" % Buf._n
        self.base = base
        self.off = off

    def r(self, lo=0, hi=None):
        if hi is None:
            hi = self.size
        return Reg(self.base, self.off + lo, self.off + hi)

    def __getitem__(self, k):
        return self.t[k]


class Alias(Buf):
    def __init__(self, view, orig):
        self.t = view
        self.size = orig.size
        self.base = orig.base
        self.off = orig.off

    def r(self, lo=0, hi=None):
        return Reg(self.base, self.off, self.off + self.size)


class View(Buf):
    def __init__(self, view, orig, off, size, scale=1):
        self.t = view
        self.size = size
        self.base = orig.base
        self.off = orig.off + off
        self.scale = scale

    def r(self, lo=0, hi=None):
        if hi is None:
            hi = self.size
        return Reg(self.base, self.off + lo * self.scale, self.off + hi * self.scale)


class Q:
    def __init__(self, name, sem):
        self.name, self.sem = name, sem
        self.count = 0
        self.prog = []
        self.waited = {}


class FW:
    def __init__(self, nc):
        self.nc = nc
        self.q = {}
        for n in ("pe", "act", "dve", "pool", "sp"):
            self.q[n] = Q(n, nc.alloc_semaphore("s_" + n))
        self.NDS = 8
        self.dsems = {n: [nc.alloc_semaphore("d_%s%d" % (n, i)) for i in range(self.NDS)] for n in ("sp", "pool")}
        self.dcount = {n: 0 for n in self.dsems}
        self.acc = {}
        self.n_instr = 0

    def _need(self, ev, waits):
        sem, val, key = ev
        cur = waits.get(key)
        if cur is None or cur[1] < val:
            waits[key] = (sem, val)

    def _deps(self, qn, reads, writes, is_dma):
        waits = {}
        for regs, w in ((reads, False), (writes, True)):
            for rg in regs:
                lst = self.acc.get(rg.base)
                if not lst:
                    continue
                for a in lst:
                    if a[1] <= rg.lo or a[0] >= rg.hi:
                        continue
                    if not (a[2] or w):
                        continue
                    if a[3] == qn and not a[5] and not is_dma and not (a[2] and not w):
                        continue
                    self._need(a[4], waits)
        return waits

    def _record(self, qn, reads, writes, ev, is_dma):
        for rg in writes:
            lst = self.acc.setdefault(rg.base, [])
            lst[:] = [a for a in lst if not (a[0] >= rg.lo and a[1] <= rg.hi)]
            lst.append([rg.lo, rg.hi, True, qn, ev, is_dma])
        for rg in reads:
            lst = self.acc.setdefault(rg.base, [])
            lst[:] = [a for a in lst if not (not a[2] and a[3] == qn and a[5] == is_dma and a[0] == rg.lo and a[1] == rg.hi)]
            lst.append([rg.lo, rg.hi, False, qn, ev, is_dma])

    def _emit_waits(self, q, waits):
        for key, (sem, val) in waits.items():
            if q.waited.get(key, 0) >= val:
                continue
            q.waited[key] = val
            q.prog.append(("w", sem, val))

    def op(self, qn, fn, reads=(), writes=()):
        q = self.q[qn]
        waits = self._deps(qn, reads, writes, False)
        self._emit_waits(q, waits)
        q.count += 1
        ev = (q.sem, q.count, qn)
        q.prog.append(("o", fn, q.sem, 1))
        self._record(qn, reads, writes, ev, False)
        self.n_instr += 1
        return ev

    def dma(self, qn, fn, reads=(), writes=()):
        q = self.q[qn]
        waits = self._deps(qn, reads, writes, True)
        j = self.dcount[qn]
        self.dcount[qn] += 1
        s = j % self.NDS
        sem = self.dsems[qn][s]
        key = "d_%s%d" % (qn, s)
        prev = 16 * (j // self.NDS)
        if prev > 0:
            self._need((sem, prev, key), waits)
        self._emit_waits(q, waits)
        ev = (sem, prev + 16, key)
        q.prog.append(("o", fn, sem, 16))
        self._record(qn, reads, writes, ev, True)
        self.n_instr += 1
        return ev

    def wait_all(self, qn):
        q = self.q[qn]
        waits = {}
        for lst in self.acc.values():
            for a in lst:
                self._need(a[4], waits)
        self._emit_waits(q, waits)

    def emit(self):
        nc = self.nc
        me = self

        def run(qn, eng):
            for it in me.q[qn].prog:
                if it[0] == "w":
                    eng.wait_ge(it[1], it[2])
                else:
                    getattr(eng, it[1][0])(**it[1][1]).then_inc(it[2], it[3])

        with nc.Block() as block:
            @block.tensor
            def _(e):
                run("pe", e)

            @block.scalar
            def _(e):
                run("act", e)

            @block.vector
            def _(e):
                run("dve", e)

            @block.gpsimd
            def _(e):
                run("pool", e)

            @block.sync
            def _(e):
                run("sp", e)


class K:
    def __init__(self, stage=99):
        self.stage = stage
        nc = self.nc = bass.Bass("TRN2", target_bir_lowering=False)
        fw = self.fw = FW(nc)
        self.din = {}
        self.rot = {}

        def dram_in(name, shape, dt=F32):
            t = nc.dram_tensor(name, list(shape), dt, kind="ExternalInput").ap()
            n = int(np.prod(shape[1:])) if len(shape) > 1 else 1
            self.din[name] = Buf(t, max(n, 1))
            return self.din[name]

        self.x_in = dram_in("x_in", [NTOK, DM])
        self.cvec = dram_in("cvec", [128, 8, 2])
        self.ada_w = dram_in("ada_w", [DEPTH, DM, 9 * DM])
        self.ada_b = dram_in("ada_b_t", [128, DEPTH, 72])
        self.norm_g = dram_in("norm_g_t", [128, DEPTH, 3, 8])
        self.fng = dram_in("fng_t", [128, 8])
        self.w1 = dram_in("ffn_w1", [DEPTH, 2, DM, 2 * DFF])
        self.w2 = dram_in("ffn_w2", [DEPTH, 2, DFF, DM])
        self.ident_d = dram_in("ident", [128, 128])
        self.win = dram_in("win_ext", [DEPTH, DM, 3200])
        self.wout = dram_in("w_out", [DEPTH, DM, DM])
        self.ropeC = dram_in("ropeC", [128, NTOK])
        self.ropeS = dram_in("ropeS", [128, NTOK])
        self.mprev_d = dram_in("mprev", [128, 512])
        self.mnext_d = dram_in("mnext", [128, 512])
        self.sink_d = dram_in("sink_t", [64, DEPTH, 8])

        self.lb_d = dram_in("lb_t", [64, DEPTH, 8])
        self.hng_d = dram_in("hng_t", [64, DEPTH, 4])
        self.rm_d = dram_in("rmask", [64, 1024])
        self.hm_d = dram_in("hmask", [64, 2, 64])

        self.s5A = dram_in("s5_As", [128, DEPTH, 3, 16])
        self.s5Q = dram_in("s5_Aq", [128, DEPTH, 2, 4, 64])
        self.s5LDq = dram_in("s5_LDq", [128, DEPTH, 4])
        self.s5B = dram_in("s5_Bq", [128, DEPTH, 2, 2, 64])
        self.s5C = dram_in("s5_Cblk", [128, DEPTH, 2, 16, 32])
        self.s5d_d = dram_in("s5_d_t", [128, DEPTH, 2])
        self.glub_d = dram_in("glu_b_t", [128, DEPTH, 4])
        self.gluw_d = dram_in("s5_glu_w", [DEPTH, 256, 512])
        self.rowm_d = dram_in("rowmask", [128, 8])
        self.J_d = dram_in("Jmat", [128, 128])

        def scr(name, shape, dt):
            return Buf(nc.dram_tensor(name, list(shape), dt, kind="Internal").ap(), 24)

        self.mix = scr("mix", [DM, NTOK], BF16)
        self.sQ = scr("sQ", [512, NTOK], BF16)
        self.sK = scr("sK", [128, NTOK], BF16)
        self.sV = scr("sV", [NTOK, 128], BF16)
        self.sA = scr("sA", [4, 256, NTOK], F32)
        self.sAv = scr("sAv", [NTOK, 256], BF16)
        self.sO = scr("sO", [256, NTOK], F32)
        self.sU = scr("sU", [256, NTOK], F32)
        self.sUt = scr("sUt", [NTOK, 256], BF16)
        y = nc.dram_tensor("y", [NLAT, DM], F32, kind="ExternalOutput").ap()
        self.y = Buf(y, DM)

        def sb(name, shape, dt=F32):
            return Buf(nc.alloc_sbuf_tensor(name, list(shape), dt), int(np.prod(shape[1:])))

        self.xT = sb("xT", [128, 8, NTOK])
        self.hT = sb("hT", [128, 8, NTOK], BF16)
        self.wA = [sb("wA%d" % i, [128, 8, 512], BF16) for i in range(2)]
        self.wB = [sb("wB%d" % i, [128, 2, 1024], BF16) for i in range(2)]
        self.sg = [sb("sg%d" % i, [128, 512]) for i in range(2)]
        self.tmp = [sb("tmp%d" % i, [128, 512]) for i in range(2)]
        self.rstd = sb("rstd", [128, 512])
        self.hid = [sb("hid%d" % i, [128, 2, 512], BF16) for i in range(2)]
        self.sq = [sb("sq%d" % i, [128, 512], BF16) for i in range(2)]
        self.xin = [sb("xin%d" % i, [128, 1024]) for i in range(2)]
        self.ident = sb("identf", [128, 128])
        self.onesb = sb("onesb", [128, 128], BF16)
        self.cv = sb("cv", [128, 8, 2])
        self.csb = sb("csb", [128, 8, 2], BF16)
        self.modr = sb("modr", [128, DEPTH, 72, 2])
        self.adab = sb("adab", [128, DEPTH, 72])
        self.ng = sb("ng", [128, DEPTH, 3, 8])
        self.fngs = sb("fngs", [128, 8])
        self.GS = sb("GS", [128, DEPTH, 3, 8, 2])
        self.GT = sb("GT", [128, DEPTH, 3, 8, 2])

        self.esink = sb("esink", [64, DEPTH, 8])
        self.rc = sb("rc", [128, 512])
        self.rs = sb("rs", [128, 512])
        self.kTall = sb("kTall", [128, NTOK], BF16)
        self.vt = sb("vt", [128, 18, 128], BF16)
        self.qt = [sb("qt%d" % i, [128, 512], BF16) for i in range(2)]
        self.sq2 = [sb("sq2_%d" % i, [128, 512], BF16) for i in range(2)]
        self.pT = [sb("pT%d" % i, [128, 512], BF16) for i in range(3)]
        self.ob = [sb("ob%d" % i, [128, 512], BF16) for i in range(2)]
        self.zt = Alias(self.ob[0].t, self.ob[0])
        self.LB = sb("LB", [64, DEPTH, 8])
        self.OML = sb("OML", [64, DEPTH, 8])
        self.lbw = sb("lbw", [64, DEPTH, 8])
        self.lbs = sb("lbs", [64, 8])
        self.hng = sb("hng", [64, DEPTH, 4])
        self.RM = sb("RM", [64, 1024])
        self.hmask = sb("hmask_s", [64, 2, 64], BF16)
        self.identb = sb("identb", [64, 64], BF16)
        self.HW = [sb("HW%d" % i, [64, 1024]) for i in range(2)] + [
            Alias(self.wB[i].t[:].rearrange("p f n -> p (f n)").bitcast(F32)[0:64, :], self.wB[i]) for i in range(2)]
        self.HQE = Alias(self.hid[0].t[:].rearrange("p f n -> p (f n)")[0:64, :], self.hid[0])
        self.HKE = Alias(self.hid[1].t[:].rearrange("p f n -> p (f n)")[0:64, :], self.hid[1])
        self.HV = sb("HV", [32, 8, 256], BF16)
        self.HS = Alias(self.sg[0].t[0:64, 0:256], self.sg[0])
        self.HT1 = Alias(self.sg[1].t[0:64, 0:256], self.sg[1])
        self.HSP = sb("HSP", [64, 256], BF16)
        self.HKT3 = [sb("HKT3_%d" % i, [32, 256], BF16) for i in range(3)]
        self.HSC3 = [sb("HSC3_%d" % i, [32, 128], BF16) for i in range(3)]
        self.HKT = self.HKT3[0]
        self.HSC = self.HSC3[0]
        self.HM1 = sb("HM1", [64, 32])
        self.HEM = sb("HEM", [64, 32])
        self.HET = sb("HET", [64, 32])
        self.HE2 = sb("HE2", [64, 32])
        self.sAs = sb("s5As_s", [128, 3, 16])
        self.sDT = sb("s5DT", [128, 16])
        self.sR = sb("s5R", [128, 16])
        self.sTH = sb("s5TH", [128, 16])
        self.sC1 = sb("s5C1", [128, 16])
        self.sS1 = sb("s5S1", [128, 16])
        self.sY16 = sb("s5Y16", [128, 16])
        self.sN16 = sb("s5N16", [128, 16])
        self.sT16 = sb("s5T16", [128, 16])
        self.ldq = sb("s5ldq", [128, DEPTH, 4])
        self.dtq = sb("s5dtq", [128, 4])
        a0 = self.wA[0].t[:].rearrange("p k n -> p (k n)")
        a1 = self.wA[1].t[:].rearrange("p k n -> p (k n)").bitcast(F32)
        self.Cblk = View(a0[:, 0:1024].rearrange("p (a b c) -> p a b c", a=2, b=16), self.wA[0], 0, 1024)
        self.gluwS = View(a0[:, 1024:2048].rearrange("p (a b) -> p a b", a=2), self.wA[0], 1024, 1024)
        self.Bblk = View(a0[:, 2048:2304].rearrange("p (a b) -> p a b", a=2), self.wA[0], 2048, 256)
        self.s5gy = [View(a0[:, 2304 + i * 512:2816 + i * 512], self.wA[0], 2304 + i * 512, 512) for i in range(2)]
        self.s5ub = [View(a0[:, 3328 + i * 256:3584 + i * 256], self.wA[0], 3328 + i * 256, 256) for i in range(2)]
        self.s5ob = View(a0[:, 3328:3840], self.wA[0], 3328, 512)
        self.s5w = [View(a1[:, i * 512:(i + 1) * 512], self.wA[1], i * 1024, 512, 2) for i in range(4)]
        self.Jb = sb("Jb", [128, 128], BF16)
        self.Ib = sb("Ib", [128, 128], BF16)
        self.rowm = sb("rowm", [128, 8])
        self.s5dS = sb("s5dS", [128, DEPTH, 2])
        self.glubS = sb("glubS", [128, DEPTH, 4])
        self.hprev = sb("hprev", [128, 2])
        hflat = self.hT.t[:].rearrange("p k t -> p (k t)")
        self.ytF = View(hflat[:, 0:4608], self.hT, 0, 4608)
        self.ytB = View(hflat[:, 4608:9216], self.hT, 4608, 4608)
        self.uTb = View(hflat[:, 9216:13824], self.hT, 9216, 4608)
        self.mbnext = View(hflat[:, 17920:18432], self.hT, 17920, 512)
        self.mbprev = Alias(self.adab.t[:].rearrange("p a b -> p (a b)").bitcast(BF16)[:, 0:512], self.adab)
        self.s5w2 = [View(hflat[:, 14848 + i * 1024:15872 + i * 1024].bitcast(F32), self.hT, 14848 + i * 1024, 512, 2) for i in range(3)] + [self.rstd]
        self.hpt = sb("s5hpt", [128, 2])
        self.BB = View(hflat[:, 13824:14848].bitcast(F32).rearrange("p (a b c) -> p a b c", a=2, b=4), self.hT, 13824, 512, 2)
        self.epsc = sb("epsc", [128, 1])
        self.onec = sb("onec", [128, 1])
        self.P = [Buf(nc.alloc_psum_tensor("ps%d" % i, [128, 512], F32), 512) for i in range(8)]

    def nxt(self, key, n):
        v = self.rot.get(key, 0)
        self.rot[key] = v + 1
        return v % n

    def prologue(self):
        fw = self.fw
        ld = lambda dst, src: fw.dma("sp", ("dma_start", dict(out=dst[:], in_=src[:])), reads=[src.r()], writes=[dst.r()])
        ld(self.ident, self.ident_d)
        ld(self.cv, self.cvec)
        ld(self.adab, self.ada_b)
        ld(self.ng, self.norm_g)
        ld(self.fngs, self.fng)
        fw.op("dve", ("memset", dict(ap=self.onesb[:], constant=1.0)), writes=[self.onesb.r()])
        fw.op("dve", ("memset", dict(ap=self.epsc[:], constant=EPS)), writes=[self.epsc.r()])
        fw.op("dve", ("memset", dict(ap=self.onec[:], constant=1.0)), writes=[self.onec.r()])

        ld(self.lbw, self.lb_d)
        ld(self.hng, self.hng_d)
        ld(self.RM, self.rm_d)
        fw.dma("pool", ("dma_start", dict(out=self.hmask[:], in_=self.hm_d[:])), reads=[self.hm_d.r()], writes=[self.hmask.r()])
        fw.dma("pool", ("dma_start", dict(out=self.identb[:], in_=self.ident_d[0:64, 0:64])), reads=[self.ident_d.r()], writes=[self.identb.r()])
        lbw, lbs, LB = self.lbw, self.lbs, self.LB
        fw.op("act", ("activation", dict(out=lbw[:], in_=lbw[:], func=AF.Exp)), reads=[lbw.r()], writes=[lbw.r()])
        fw.op("dve", ("tensor_tensor", dict(out=lbs[:], in0=lbw[:, 0, :], in1=lbw[:, 1, :], op=ALU.add)), reads=[lbw.r()], writes=[lbs.r()])
        fw.op("dve", ("tensor_tensor", dict(out=lbs[:], in0=lbs[:], in1=lbw[:, 2, :], op=ALU.add)), reads=[lbw.r(), lbs.r()], writes=[lbs.r()])
        fw.op("dve", ("tensor_tensor", dict(out=lbs[:], in0=lbs[:], in1=lbw[:, 3, :], op=ALU.add)), reads=[lbw.r(), lbs.r()], writes=[lbs.r()])
        fw.op("dve", ("reciprocal", dict(out=lbs[:], in_=lbs[:])), reads=[lbs.r()], writes=[lbs.r()])
        fw.op("dve", ("tensor_tensor", dict(out=lbw[:], in0=lbw[:], in1=lbs[:].unsqueeze(1).to_broadcast([64, DEPTH, 8]), op=ALU.mult)), reads=[lbw.r(), lbs.r()], writes=[lbw.r()])
        fw.op("dve", ("memset", dict(ap=LB[:, 0, :], constant=0.0)), writes=[LB.r()])
        fw.op("dve", ("tensor_copy", dict(out=LB[:, 1, :], in_=lbw[:, 1, :])), reads=[lbw.r()], writes=[LB.r()])
        fw.op("dve", ("tensor_tensor", dict(out=LB[:, 2, :], in0=LB[:, 1, :], in1=lbw[:, 2, :], op=ALU.add)), reads=[lbw.r(), LB.r()], writes=[LB.r()])
        fw.op("dve", ("tensor_tensor", dict(out=LB[:, 3, :], in0=LB[:, 2, :], in1=lbw[:, 3, :], op=ALU.add)), reads=[lbw.r(), LB.r()], writes=[LB.r()])
        fw.op("dve", ("tensor_scalar", dict(out=self.OML[:], in0=LB[:], scalar1=-1.0, scalar2=1.0, op0=ALU.mult, op1=ALU.add)), reads=[LB.r()], writes=[self.OML.r()])
        ld(self.ldq, self.s5LDq)
        ld(self.rowm, self.rowm_d)
        ld(self.s5dS, self.s5d_d)
        ld(self.glubS, self.glub_d)
        fw.dma("pool", ("dma_start", dict(out=self.Jb[:], in_=self.J_d[:])), reads=[self.J_d.r()], writes=[self.Jb.r()])
        fw.dma("pool", ("dma_start", dict(out=self.Ib[:], in_=self.ident_d[:])), reads=[self.ident_d.r()], writes=[self.Ib.r()])
        ld(self.esink, self.sink_d)
        fw.op("act", ("activation", dict(out=self.esink[:], in_=self.esink[:], func=AF.Exp)), reads=[self.esink.r()], writes=[self.esink.r()])
        xT, ident = self.xT, self.ident
        for blk in range(NTOK // 128):
            xi = self.xin[blk % 2]
            fw.dma("sp", ("dma_start", dict(out=xi[:], in_=self.x_in[blk * 128:(blk + 1) * 128, :])),
                   reads=[self.x_in.r()], writes=[xi.r()])
            for k4 in range(2):
                ps = self.P[self.nxt("pa", 4)]
                for kk in range(4):
                    k = k4 * 4 + kk
                    fw.op("pe", ("transpose", dict(out=ps[:, kk * 128:(kk + 1) * 128], in_=xi[:, k * 128:(k + 1) * 128], identity=ident[:])),
                          reads=[xi.r(k * 128, (k + 1) * 128), ident.r()], writes=[ps.r(kk * 128, (kk + 1) * 128)])
                fw.op("dve", ("tensor_copy", dict(
                    out=xT[:, k4 * 4:(k4 + 1) * 4, blk * 128:(blk + 1) * 128], in_=ps[:].rearrange("p (a b) -> p a b", b=128))),
                    reads=[ps.r()], writes=[xT.r((k4 * 4 + kk) * NTOK + blk * 128, (k4 * 4 + kk) * NTOK + (blk + 1) * 128) for kk in range(4)])
        fw.op("act", ("activation", dict(out=self.csb[:], in_=self.cv[:], func=AF.Silu)), reads=[self.cv.r()], writes=[self.csb.r()])
        csb, modr = self.csb, self.modr
        for l in range(DEPTH):
            awl = self.ada_w[l].rearrange("(k p) n -> p k n", p=128)
            for pc in range(18):
                wa = self.wA[self.nxt("wA", 2)]
                fw.dma("pool", ("dma_start", dict(out=wa[:], in_=awl[:, :, pc * 512:(pc + 1) * 512])),
                       reads=[self.ada_w.r()], writes=[wa.r()])
                ps = self.P[self.nxt("pa", 4)]
                for m4 in range(4):
                    for k in range(8):
                        fw.op("pe", ("matmul", dict(out=ps[:, m4 * 2:(m4 + 1) * 2], lhsT=wa[:, k, m4 * 128:(m4 + 1) * 128], rhs=csb[:, k, :], start=(k == 0), stop=(k == 7))),
                              reads=[wa.r(), csb.r()], writes=[ps.r(m4 * 2, m4 * 2 + 2)])
                lo = (l * 72 + pc * 4) * 2
                fw.op("dve", ("tensor_tensor", dict(
                    out=modr[:, l, pc * 4:(pc + 1) * 4, :], in0=ps[:, 0:8].rearrange("p (a b) -> p a b", b=2),
                    in1=self.adab[:, l, pc * 4:(pc + 1) * 4].unsqueeze(2).to_broadcast([128, 4, 2]), op=ALU.add)),
                    reads=[ps.r(0, 8), self.adab.r()], writes=[modr.r(lo, lo + 8)])
        GS, GT, ng = self.GS, self.GT, self.ng
        for l in range(DEPTH):
            for i in range(3):
                sc = modr[:, l, (i * 3 + 1) * 8:(i * 3 + 2) * 8, :]
                gt = modr[:, l, (i * 3 + 2) * 8:(i * 3 + 3) * 8, :]
                lo = ((l * 3 + i) * 8) * 2
                fw.op("dve", ("scalar_tensor_tensor", dict(
                    out=GS[:, l, i, :, :], in0=sc, scalar=1.0, in1=ng[:, l, i, :].unsqueeze(2).to_broadcast([128, 8, 2]), op0=ALU.add, op1=ALU.mult)),
                    reads=[modr.r(), ng.r()], writes=[GS.r(lo, lo + 16)])
                fw.op("dve", ("tensor_scalar", dict(out=GT[:, l, i, :, :], in0=gt, scalar1=(1.0 if i == 1 else 0.5), scalar2=None, op0=ALU.mult)),
                      reads=[modr.r()], writes=[GT.r(lo, lo + 16)])
        fw.dma("pool", ("dma_start", dict(out=self.mbprev[:, :], in_=self.mprev_d[:])), reads=[self.mprev_d.r()], writes=[self.mbprev.r()])

    def rms_stats(self, t0, n):
        fw, xT = self.fw, self.xT
        ps = self.P[self.nxt("pa", 4)]
        for k in range(8):
            sq = self.sq[self.nxt("sq", 2)]
            fw.op("act", ("activation", dict(out=sq[:, :n], in_=xT[:, k, t0:t0 + n], func=AF.Square)),
                  reads=[xT.r(k * NTOK + t0, k * NTOK + t0 + n)], writes=[sq.r(0, n)])
            fw.op("pe", ("matmul", dict(out=ps[:, :n], lhsT=self.onesb[:], rhs=sq[:, :n], start=(k == 0), stop=(k == 7))),
                  reads=[sq.r(0, n), self.onesb.r()], writes=[ps.r(0, n)])
        rstd = self.rstd
        fw.op("act", ("activation", dict(out=rstd[:, :n], in_=ps[:, :n], func=AF.Ln, bias=self.epsc[:, 0:1], scale=1.0 / DM)),
              reads=[ps.r(0, n), self.epsc.r()], writes=[rstd.r(0, n)])
        fw.op("act", ("activation", dict(out=rstd[:, :n], in_=rstd[:, :n], func=AF.Exp, scale=-0.5)), reads=[rstd.r(0, n)], writes=[rstd.r(0, n)])

    def norm_h(self, l, i, tiles):
        fw, xT, hT = self.fw, self.xT, self.hT
        for (t0, n) in tiles:
            var = 1 if t0 == 0 else 0
            self.rms_stats(t0, n)
            for k in range(8):
                tmp = self.tmp[self.nxt("tmp", 2)]
                fw.op("dve", ("scalar_tensor_tensor", dict(
                    out=tmp[:, :n], in0=xT[:, k, t0:t0 + n], scalar=self.GS[:, l, i, k, var:var + 1], in1=self.rstd[:, :n], op0=ALU.mult, op1=ALU.mult)),
                    reads=[xT.r(k * NTOK + t0, k * NTOK + t0 + n), self.GS.r(), self.rstd.r(0, n)], writes=[tmp.r(0, n)])
                fw.op("act", ("activation", dict(
                    out=hT[:, k, t0:t0 + n], in_=tmp[:, :n], func=AF.Identity, bias=self.modr[:, l, (i * 3) * 8 + k, var:var + 1], scale=1.0)),
                    reads=[tmp.r(0, n), self.modr.r()], writes=[hT.r(k * NTOK + t0, k * NTOK + t0 + n)])

    def ffn(self, l, i, tiles):
        fw, xT, hT = self.fw, self.xT, self.hT
        fi = 0 if i == 0 else 1
        w1v = self.w1[l, fi].rearrange("(k p) n -> p k n", p=128)

        def out_pair(prev, o):
            wb, hid, t0, n, var = prev
            po = self.P[4 + self.nxt("pb", 4)]
            for f in range(2):
                fw.op("pe", ("matmul", dict(out=po[:, :n], lhsT=wb[:, f, o * 128:(o + 1) * 128], rhs=hid[:, f, :n], start=(f == 0), stop=(f == 1))),
                      reads=[wb.r(f * 1024 + o * 128, f * 1024 + (o + 1) * 128), hid.r(f * 512, f * 512 + n)], writes=[po.r(0, n)])
            fw.op("dve", ("scalar_tensor_tensor", dict(
                out=xT[:, o, t0:t0 + n], in0=po[:, :n], scalar=self.GT[:, l, i, o, var:var + 1], in1=xT[:, o, t0:t0 + n], op0=ALU.mult, op1=ALU.add)),
                reads=[po.r(0, n), self.GT.r(), xT.r(o * NTOK + t0, o * NTOK + t0 + n)], writes=[xT.r(o * NTOK + t0, o * NTOK + t0 + n)])

        prev = None
        for g in range(11):
            wa = self.wA[self.nxt("wA", 2)]
            wb = self.wB[self.nxt("wB", 2)]
            fw.dma("pool", ("dma_start", dict(out=wa[:, :, 0:256], in_=w1v[:, :, g * 256:(g + 1) * 256])),
                   reads=[self.w1.r()], writes=[wa.r(kk * 512, kk * 512 + 256) for kk in range(8)])
            fw.dma("pool", ("dma_start", dict(out=wa[:, :, 256:512], in_=w1v[:, :, DFF + g * 256:DFF + (g + 1) * 256])),
                   reads=[self.w1.r()], writes=[wa.r(kk * 512 + 256, kk * 512 + 512) for kk in range(8)])
            fw.dma("pool", ("dma_start", dict(out=wb[:], in_=self.w2[l, fi, g * 256:(g + 1) * 256, :].rearrange("(f p) n -> p f n", p=128))),
                   reads=[self.w2.r()], writes=[wb.r()])
            for (t0, n) in tiles:
                var = 1 if t0 == 0 else 0
                hid = self.hid[self.nxt("hid", 2)]
                q = 0
                for f in range(2):
                    pg = self.P[self.nxt("pa", 4)]
                    pu = self.P[self.nxt("pa", 4)]
                    for (pp, c0, isg) in ((pg, f * 128, True), (pu, 256 + f * 128, False)):
                        for k in range(8):
                            fw.op("pe", ("matmul", dict(out=pp[:, :n], lhsT=wa[:, k, c0:c0 + 128], rhs=hT[:, k, t0:t0 + n], start=(k == 0), stop=(k == 7))),
                                  reads=[wa.r(k * 512 + c0, k * 512 + c0 + 128), hT.r(k * NTOK + t0, k * NTOK + t0 + n)], writes=[pp.r(0, n)])
                            if k % 4 == 3:
                                if prev is not None:
                                    out_pair(prev, q)
                                q += 1
                        if isg:
                            sg = self.sg[self.nxt("sg", 2)]
                            fw.op("act", ("activation", dict(out=sg[:, :n], in_=pg[:, :n], func=AF.Silu)), reads=[pg.r(0, n)], writes=[sg.r(0, n)])
                    fw.op("dve", ("tensor_tensor", dict(out=hid[:, f, :n], in0=sg[:, :n], in1=pu[:, :n], op=ALU.mult)),
                          reads=[sg.r(0, n), pu.r(0, n)], writes=[hid.r(f * 512, f * 512 + n)])
                prev = (wb, hid, t0, n, var)
        for o in range(8):
            out_pair(prev, o)

    def mm8(self, ps, n, wa, c0, t0, ncol=128):
        fw, hT = self.fw, self.hT
        for k in range(8):
            fw.op("pe", ("matmul", dict(out=ps[0:ncol, :n], lhsT=wa[:, k, c0:c0 + ncol], rhs=hT[:, k, t0:t0 + n], start=(k == 0), stop=(k == 7))),
                  reads=[wa.r(k * 512 + c0, k * 512 + c0 + ncol), hT.r(k * NTOK + t0, k * NTOK + t0 + n)], writes=[ps.r(0, n)])

    def inproj(self, l):
        fw = self.fw
        GR = [(0, 512, "A0"), (512, 512, "A1"), (1024, 512, "B0"), (1536, 512, "B1"), (2048, 512, "BK"), (2560, 384, "TV"), (2944, 256, "TU")]
        winl = self.win[l].rearrange("(k p) n -> p k n", p=128)
        for (c0, ncl, kind) in GR:
            wa = self.wA[self.nxt("wA", 2)]
            fw.dma("pool", ("dma_start", dict(out=wa[:, :, 0:ncl], in_=winl[:, :, c0:c0 + ncl])), reads=[self.win.r()], writes=[wa.r()])
            for ti, (t0, n) in enumerate(TILES):
                if kind in ("A0", "A1"):
                    for j in range(4):
                        ps = self.P[self.nxt("pa", 4)]
                        self.mm8(ps, n, wa, j * 128, t0)
                        tmp = self.tmp[self.nxt("tmp", 2)]
                        fw.op("act", ("activation", dict(out=tmp[:, :n], in_=ps[:, :n], func=AF.Identity)), reads=[ps.r(0, n)], writes=[tmp.r(0, n)])
                        qi = (0 if kind == "A0" else 2) + j // 2
                        r0 = (j % 2) * 128
                        fw.dma("sp", ("dma_start", dict(out=self.sA[qi, r0:r0 + 128, t0:t0 + n], in_=tmp[:, :n])), reads=[tmp.r(0, n)], writes=[self.sA.r(ti, ti + 1)])
                elif kind in ("B0", "B1", "BK"):
                    fw.dma("sp", ("dma_start", dict(out=self.rc[:, :n], in_=self.ropeC[:, t0:t0 + n])), reads=[self.ropeC.r()], writes=[self.rc.r(0, n)])
                    fw.dma("sp", ("dma_start", dict(out=self.rs[:, :n], in_=self.ropeS[:, t0:t0 + n])), reads=[self.ropeS.r()], writes=[self.rs.r(0, n)])
                    for j in range(2 if kind != "BK" else 1):
                        pq = self.P[self.nxt("pa", 4)]
                        pr = self.P[self.nxt("pa", 4)]
                        self.mm8(pq, n, wa, j * 256, t0)
                        self.mm8(pr, n, wa, j * 256 + 128, t0)
                        t1 = self.tmp[0]
                        t2 = self.tmp[1]
                        fw.op("dve", ("tensor_tensor", dict(out=t1[:, :n], in0=pq[:, :n], in1=self.rc[:, :n], op=ALU.mult)), reads=[pq.r(0, n), self.rc.r(0, n)], writes=[t1.r(0, n)])
                        fw.op("dve", ("tensor_tensor", dict(out=t2[:, :n], in0=pr[:, :n], in1=self.rs[:, :n], op=ALU.mult)), reads=[pr.r(0, n), self.rs.r(0, n)], writes=[t2.r(0, n)])
                        qb = self.ob[self.nxt("ob", 2)]
                        fw.op("dve", ("tensor_tensor", dict(out=qb[:, :n], in0=t1[:, :n], in1=t2[:, :n], op=ALU.add)), reads=[t1.r(0, n), t2.r(0, n)], writes=[qb.r(0, n)])
                        if kind == "BK":
                            fw.dma("sp", ("dma_start", dict(out=self.sK[:, t0:t0 + n], in_=qb[:, :n])), reads=[qb.r(0, n)], writes=[self.sK.r(ti, ti + 1)])
                        else:
                            ch = (0 if kind == "B0" else 2) + j
                            fw.dma("sp", ("dma_start", dict(out=self.sQ[ch * 128:(ch + 1) * 128, t0:t0 + n], in_=qb[:, :n])), reads=[qb.r(0, n)], writes=[self.sQ.r(ti, ti + 1)])
                    if kind == "BK":
                        for j in range(2):
                            ps = self.P[self.nxt("pa", 4)]
                            self.mm8(ps, n, wa, 256 + j * 128, t0)
                            tmp = self.tmp[self.nxt("tmp", 2)]
                            fw.op("act", ("activation", dict(out=tmp[:, :n], in_=ps[:, :n], func=AF.Identity)), reads=[ps.r(0, n)], writes=[tmp.r(0, n)])
                            fw.dma("sp", ("dma_start", dict(out=self.sU[j * 128:(j + 1) * 128, t0:t0 + n], in_=tmp[:, :n])), reads=[tmp.r(0, n)], writes=[self.sU.r(ti, ti + 1)])
                else:
                    hT = self.hT
                    for blk in range(n // 128):
                        tb = t0 + blk * 128
                        ps = self.P[self.nxt("pa", 4)]
                        for k in range(8):
                            fw.op("pe", ("matmul", dict(out=ps[:, :ncl], lhsT=hT[:, k, tb:tb + 128], rhs=wa[:, k, 0:ncl], start=(k == 0), stop=(k == 7))),
                                  reads=[wa.r(k * 512, k * 512 + ncl), hT.r(k * NTOK + tb, k * NTOK + tb + 128)], writes=[ps.r(0, ncl)])
                        ob = self.ob[self.nxt("ob", 2)]
                        fw.op("act", ("activation", dict(out=ob[:, :ncl], in_=ps[:, :ncl], func=AF.Identity)), reads=[ps.r(0, ncl)], writes=[ob.r(0, ncl)])
                        if kind == "TV":
                            fw.dma("sp", ("dma_start", dict(out=self.sAv[tb:tb + 128, :], in_=ob[:, 0:256])), reads=[ob.r(0, 256)], writes=[self.sAv.r(ti, ti + 1)])
                            fw.dma("sp", ("dma_start", dict(out=self.sV[tb:tb + 128, :], in_=ob[:, 256:384])), reads=[ob.r(256, 384)], writes=[self.sV.r(ti, ti + 1)])
                        else:
                            fw.dma("sp", ("dma_start", dict(out=self.sUt[tb:tb + 128, :], in_=ob[:, 0:256])), reads=[ob.r(0, 256)], writes=[self.sUt.r(ti, ti + 1)])


    def hgrn(self, l):
        fw = self.fw
        W1, W2, W3, W4 = self.HW
        W5, Qb = self.xin[0], self.xin[1]
        QE, KE, Vt, S, T1, SP, KT, SC = self.HQE, self.HKE, self.HV, self.HS, self.HT1, self.HSP, self.HKT, self.HSC
        M1, EM, ET, E2 = self.HM1, self.HEM, self.HET, self.HE2

        def v3(b):
            return b[0:64, :].rearrange("p (a j) -> p a j", j=32)

        def o(q, name, reads, writes, **kw):
            fw.op(q, (name, kw), reads=reads, writes=writes)

        for dirn in range(2):
            o("dve", "memset", [], [S.r()], ap=S[:, :], constant=0.0)
            segs = list(range(9)) if dirn == 0 else [0] + list(range(8, 0, -1))
            lbb = self.LB[:, l, dirn * 4:(dirn + 1) * 4].unsqueeze(2).to_broadcast([64, 4, 256])
            omb = self.OML[:, l, dirn * 4:(dirn + 1) * 4].unsqueeze(2).to_broadcast([64, 4, 256])
            for sgi in segs:
                t0 = sgi * 256
                ti = 0 if sgi == 0 else 1 + (sgi - 1) // 2
                fw.dma("sp", ("dma_start", dict(out=W1[:, :].rearrange("p (h t) -> p h t", h=4), in_=self.sA[1 + dirn, :, t0:t0 + 256].rearrange("(h d) t -> d h t", d=64))),
                       reads=[self.sA.r()], writes=[W1.r()])
                fw.dma("sp", ("dma_start", dict(out=Qb[0:64, :].rearrange("p (h t) -> p h t", h=4), in_=self.sA[0, :, t0:t0 + 256].rearrange("(h d) t -> d h t", d=64))),
                       reads=[self.sA.r()], writes=[Qb.r()])
                fw.dma("sp", ("dma_start", dict(out=Vt[:, :, :], in_=self.sAv[t0:t0 + 256, :].rearrange("(c s) v -> s c v", s=32))),
                       reads=[self.sAv.r()], writes=[Vt.r()])
                w1h = W1[:, :].rearrange("p (h t) -> p h t", h=4)
                o("act", "activation", [W1.r()], [W1.r()], out=W1[:, :], in_=W1[:, :], func=AF.Sigmoid)
                for h in range(4):
                    o("act", "activation", [W1.r(), self.OML.r(), self.LB.r()], [W1.r()], out=W1[:, h * 256:(h + 1) * 256], in_=W1[:, h * 256:(h + 1) * 256], func=AF.Identity,
                      scale=self.OML[:, l, dirn * 4 + h:dirn * 4 + h + 1], bias=self.LB[:, l, dirn * 4 + h:dirn * 4 + h + 1])
                o("act", "activation", [W1.r()], [W2.r()], out=W2[:, :], in_=W1[:, :], func=AF.Ln)
                o("act", "activation", [W1.r(), self.onec.r()], [W3.r()], out=W3[:, :], in_=W1[:, :], func=AF.Identity, scale=-1.0, bias=self.onec[0:64, 0:1])
                o("dve", "tensor_tensor_scan", [self.RM.r(), W2.r()], [W4.r()], out=W4[:, :], data0=self.RM[:, :], data1=W2[:, :], initial=0.0, op0=ALU.mult, op1=ALU.add)
                if dirn == 0:
                    o("dve", "tensor_copy", [W4.r()], [M1.r()], out=M1[:, :].unsqueeze(2), in_=v3(W4)[:, :, 15:16])
                    o("dve", "scalar_tensor_tensor", [M1.r(), W4.r()], [W5.r()], out=v3(W5), in0=v3(W4), scalar=1.0, in1=M1[:, :].unsqueeze(2).to_broadcast([64, 32, 32]), op0=ALU.mult, op1=ALU.subtract)
                    o("act", "activation", [M1.r()], [EM.r()], out=EM[:, :], in_=M1[:, :], func=AF.Exp)
                    o("act", "activation", [W4.r()], [ET.r()], out=ET[:, :].unsqueeze(2), in_=v3(W4)[:, :, 31:32], func=AF.Exp)
                    o("act", "activation", [W5.r()], [E2.r()], out=E2[:, :].unsqueeze(2), in_=v3(W5)[:, :, 31:32], func=AF.Exp)
                else:
                    o("dve", "tensor_tensor", [W4.r(), W2.r()], [W2.r()], out=W2[:, :], in0=W4[:, :], in1=W2[:, :], op=ALU.subtract)
                    o("dve", "tensor_copy", [W2.r()], [M1.r()], out=M1[:, :].unsqueeze(2), in_=v3(W2)[:, :, 16:17])
                    o("dve", "scalar_tensor_tensor", [M1.r(), W2.r()], [W5.r()], out=v3(W5), in0=v3(W2), scalar=-1.0, in1=M1[:, :].unsqueeze(2).to_broadcast([64, 32, 32]), op0=ALU.mult, op1=ALU.add)
                    o("act", "activation", [M1.r()], [E2.r()], out=E2[:, :], in_=M1[:, :], func=AF.Exp)
                    o("act", "activation", [W4.r()], [ET.r()], out=ET[:, :].unsqueeze(2), in_=v3(W4)[:, :, 31:32], func=AF.Exp)
                    o("dve", "tensor_tensor", [W4.r(), M1.r()], [EM.r()], out=EM[:, :].unsqueeze(2), in0=v3(W4)[:, :, 31:32], in1=M1[:, :].unsqueeze(2), op=ALU.subtract)
                    o("act", "activation", [EM.r()], [EM.r()], out=EM[:, :], in_=EM[:, :], func=AF.Exp)
                o("act", "activation", [W5.r()], [W1.r()], out=W1[:, :], in_=W5[0:64, :], func=AF.Exp)
                o("act", "activation", [W5.r()], [W2.r()], out=W2[:, :], in_=W5[0:64, :], func=AF.Exp, scale=-1.0)
                o("act", "activation", [Qb.r()], [Qb.r()], out=Qb[0:64, :], in_=Qb[0:64, :], func=AF.Silu)
                o("dve", "tensor_tensor", [Qb.r(), W1.r()], [QE.r()], out=QE[:, :], in0=Qb[0:64, :], in1=W1[:, :], op=ALU.mult)
                o("dve", "tensor_tensor", [W3.r(), W2.r()], [KE.r()], out=KE[:, :], in0=W3[:, :], in1=W2[:, :], op=ALU.mult)
                yield
                OF = W4
                if dirn == 1:
                    fw.dma("sp", ("dma_start", dict(out=OF[:, :].rearrange("p (h t) -> p h t", h=4), in_=self.sO[:, t0:t0 + 256].rearrange("(h d) t -> d h t", d=64))),
                           reads=[self.sO.r()], writes=[OF.r()])
                corder = list(range(8)) if dirn == 0 else list(range(7, -1, -1))
                OB = W5 if dirn == 1 else None
                bufs = {}
                for it in range(10):
                    cA = corder[it] if it < 8 else None
                    cU = corder[it - 1] if 1 <= it <= 8 else None
                    cB = corder[it - 2] if it >= 2 else None
                    if cA is not None:
                        par3 = self.nxt("hpar3", 3)
                        KTa, SCa = self.HKT3[par3], self.HSC3[par3]
                        bufs[cA] = (KTa, SCa)
                        psK = self.P[self.nxt("pa", 4)]
                        psS = self.P[self.nxt("pa", 4)]
                        for h in range(4):
                            cb = h * 256 + cA * 32
                            o("pe", "matmul", [KE.r(cb, cb + 32), self.identb.r()], [psK.r(h * 64, h * 64 + 64)], out=psK[0:32, h * 64:(h + 1) * 64], lhsT=KE[:, cb:cb + 32], rhs=self.identb[:, :], start=True, stop=True)
                            o("pe", "matmul", [KE.r(cb, cb + 32), QE.r(cb, cb + 32)], [psS.r(h * 32, h * 32 + 32)], out=psS[0:32, h * 32:(h + 1) * 32], lhsT=KE[:, cb:cb + 32], rhs=QE[:, cb:cb + 32], start=True, stop=True)
                    if cU is not None:
                        KTu = bufs[cU][0]
                        psU = self.P[4 + self.nxt("pb", 4)]
                        for h in range(4):
                            o("pe", "matmul", [KTu.r(), Vt.r()], [psU.r(h * 64, h * 64 + 64)], out=psU[0:64, h * 64:(h + 1) * 64], lhsT=KTu[0:32, h * 64:(h + 1) * 64], rhs=Vt[0:32, cU, h * 64:(h + 1) * 64], start=True, stop=True)
                    if cA is not None:
                        o("act", "activation", [psK.r(0, 256)], [KTa.r()], out=KTa[0:32, :], in_=psK[0:32, 0:256], func=AF.Identity)
                    if cB is not None:
                        c = cB
                        SCb = bufs.pop(c)[1]
                        emb = EM[:, :].rearrange("p (h c) -> p h c", c=8)[:, :, c:c + 1].to_broadcast([64, 4, 64])
                        etb = ET[:, :].rearrange("p (h c) -> p h c", c=8)[:, :, c:c + 1].to_broadcast([64, 4, 64])
                        s3 = S[:, :].rearrange("p (h v) -> p h v", h=4)
                        o("dve", "tensor_tensor", [S.r(), EM.r()], [SP.r()], out=SP[:, :].rearrange("p (h v) -> p h v", h=4), in0=s3, in1=emb, op=ALU.mult)
                        psO = self.P[4 + self.nxt("pb", 4)]
                        for h in range(4):
                            cb = h * 256 + c * 32
                            o("pe", "matmul", [Vt.r(), SCb.r()], [psO.r(h * 32, h * 32 + 32)], out=psO[0:64, h * 32:(h + 1) * 32], lhsT=Vt[0:32, c, h * 64:(h + 1) * 64], rhs=SCb[0:32, h * 32:(h + 1) * 32], start=True, stop=False)
                            o("pe", "matmul", [SP.r(), QE.r(cb, cb + 32)], [psO.r(h * 32, h * 32 + 32)], out=psO[0:64, h * 32:(h + 1) * 32], lhsT=SP[:, h * 64:(h + 1) * 64], rhs=QE[:, cb:cb + 32], start=False, stop=True)
                        dst = OF if dirn == 0 else OB
                        ofv = dst[0:64, :].rearrange("p (h t) -> p h t", h=4)[:, :, c * 32:(c + 1) * 32]
                        pov = psO[0:64, 0:128].rearrange("p (h t) -> p h t", h=4)
                        o("act", "activation", [psO.r(0, 128)], [dst.r()], out=ofv, in_=pov, func=AF.Identity)
                        o("dve", "tensor_tensor", [S.r(), ET.r()], [S.r()], out=s3, in0=s3, in1=etb, op=ALU.mult)
                        o("dve", "tensor_tensor", [S.r(), T1.r()], [S.r()], out=S[:, :], in0=S[:, :], in1=T1[:, :], op=ALU.add)
                    if cA is not None:
                        o("dve", "tensor_tensor", [psS.r(0, 128), self.hmask.r()], [SCa.r()], out=SCa[0:32, 0:128].rearrange("p (h t) -> p h t", h=4), in0=psS[0:32, 0:128].rearrange("p (h t) -> p h t", h=4),
                          in1=self.hmask[0:32, dirn, 0:32].unsqueeze(1).to_broadcast([32, 4, 32]), op=ALU.mult)
                    if cU is not None:
                        e2b = E2[:, :].rearrange("p (h c) -> p h c", c=8)[:, :, cU:cU + 1].to_broadcast([64, 4, 64])
                        o("dve", "tensor_tensor", [psU.r(0, 256), E2.r()], [T1.r()], out=T1[:, :].rearrange("p (h v) -> p h v", h=4), in0=psU[0:64, 0:256].rearrange("p (h v) -> p h v", h=4), in1=e2b, op=ALU.mult)
                    yield
                if dirn == 1:
                    o("dve", "tensor_tensor", [OF.r(), OB.r()], [OF.r()], out=OF[:, :], in0=OF[:, :], in1=OB[0:64, :], op=ALU.add)
                if dirn == 0:
                    fw.dma("sp", ("dma_start", dict(out=self.sO[:, t0:t0 + 256].rearrange("(h d) t -> d h t", d=64), in_=OF[:, :].rearrange("p (h t) -> p h t", h=4))),
                           reads=[OF.r()], writes=[self.sO.r(ti, ti + 1)])
                else:
                    o("act", "activation", [OF.r()], [QE.r()], out=QE[:, :], in_=OF[:, :], func=AF.Square)
                    for hf in range(2):
                        psR = self.P[self.nxt("pa", 4)]
                        o("pe", "matmul", [self.onesb.r(), QE.r()], [psR.r()], out=psR[0:64, :], lhsT=self.onesb[0:64, 0:64], rhs=QE[:, hf * 512:(hf + 1) * 512], start=True, stop=True)
                        o("act", "activation", [psR.r(), self.epsc.r()], [W1.r(hf * 512, hf * 512 + 512)], out=W1[:, hf * 512:(hf + 1) * 512], in_=psR[0:64, :], func=AF.Ln, bias=self.epsc[0:64, 0:1], scale=1.0 / 64)
                    o("act", "activation", [W1.r()], [W1.r()], out=W1[:, :], in_=W1[:, :], func=AF.Exp, scale=-0.5)
                    o("dve", "tensor_tensor", [OF.r(), W1.r()], [OF.r()], out=OF[:, :], in0=OF[:, :], in1=W1[:, :], op=ALU.mult)
                    ofh = OF[:, :].rearrange("p (h t) -> p h t", h=4)
                    o("dve", "tensor_tensor", [OF.r(), self.hng.r()], [OF.r()], out=ofh, in0=ofh, in1=self.hng[:, l, :].unsqueeze(2).to_broadcast([64, 4, 256]), op=ALU.mult)
                    fw.dma("sp", ("dma_start", dict(out=W3[:, :].rearrange("p (h t) -> p h t", h=4), in_=self.sA[3, :, t0:t0 + 256].rearrange("(h d) t -> d h t", d=64))),
                           reads=[self.sA.r()], writes=[W3.r()])
                    o("act", "activation", [W3.r()], [W3.r()], out=W3[:, :], in_=W3[:, :], func=AF.Silu)
                    o("dve", "tensor_tensor", [OF.r(), W3.r()], [KE.r()], out=KE[:, :], in0=OF[:, :], in1=W3[:, :], op=ALU.mult)
                    fw.dma("sp", ("dma_start", dict(out=self.mix[0:256, t0:t0 + 256].rearrange("(h d) t -> d h t", d=64), in_=KE[:, :].rearrange("p (h t) -> p h t", h=4))),
                           reads=[KE.r()], writes=[self.mix.r(ti, ti + 1)])
                yield


    def s5(self, l):
        fw = self.fw
        PI = float(np.pi)

        def o(q, name, reads, writes, **kw):
            fw.op(q, (name, kw), reads=reads, writes=writes)

        class Sl:
            def __init__(s_, buf, lo, n=256):
                s_.buf, s_.lo, s_.n = buf, lo, n
                s_.ap = buf[:, lo:lo + n]
                s_.rg = buf.r(lo, lo + n)

            def v4(s_):
                return s_.ap.rearrange("p (a b) -> p a b", b=64)

        pool = []
        for b, n in ((self.s5w[0], 512), (self.s5w[1], 512), (self.s5w[2], 512), (self.s5w[3], 512),
                     (self.rc, 512), (self.rs, 512), (self.rstd, 512)):
            for lo in range(0, n, 256):
                pool.append(Sl(b, lo))
        AR, AI, BQ, AD, ANG, Y, CN, T, U, RDEN, CR, CI, T2, MAG = pool[:14]

        def tt(out, a, b_, op, extra_r=()):
            o("dve", "tensor_tensor", [a.rg, b_.rg] + list(extra_r), [out.rg], out=out.ap, in0=a.ap, in1=b_.ap, op=op)

        def reduce_angle(xap, xregs, shift, out_ap, out_regs, y_ap, y_regs, c_ap, c_regs, t_ap, t_regs):
            o("dve", "tensor_scalar", xregs, y_regs, out=y_ap, in0=xap, scalar1=shift, scalar2=None, op0=ALU.add)
            o("dve", "tensor_scalar", y_regs, c_regs, out=c_ap, in0=y_ap, scalar1=PI, scalar2=None, op0=ALU.is_gt)
            for thr in (3 * PI, 5 * PI, 7 * PI):
                o("dve", "tensor_scalar", y_regs, t_regs, out=t_ap, in0=y_ap, scalar1=thr, scalar2=None, op0=ALU.is_gt)
                o("dve", "tensor_tensor", c_regs + t_regs, c_regs, out=c_ap, in0=c_ap, in1=t_ap, op=ALU.add)
            o("dve", "scalar_tensor_tensor", c_regs + y_regs, out_regs, out=out_ap, in0=c_ap, scalar=-2.0 * PI, in1=y_ap, op0=ALU.mult, op1=ALU.add)

        fw.dma("sp", ("dma_start", dict(out=AR.v4(), in_=self.s5Q[:, l, 0])), reads=[self.s5Q.r()], writes=[AR.rg])
        fw.dma("sp", ("dma_start", dict(out=AI.v4(), in_=self.s5Q[:, l, 1])), reads=[self.s5Q.r()], writes=[AI.rg])
        fw.dma("sp", ("dma_start", dict(out=BQ.v4(), in_=self.s5B[:, l].rearrange("p a h d -> p (a h) d"))), reads=[self.s5B.r()], writes=[BQ.rg])
        fw.dma("sp", ("dma_start", dict(out=self.sAs[:], in_=self.s5A[:, l])), reads=[self.s5A.r()], writes=[self.sAs.r()])
        fw.dma("pool", ("dma_start", dict(out=self.Cblk[:], in_=self.s5C[:, l])), reads=[self.s5C.r()], writes=[self.Cblk.r()])
        fw.dma("pool", ("dma_start", dict(out=self.gluwS[:], in_=self.gluw_d[l].rearrange("(h p) n -> p h n", p=128))), reads=[self.gluw_d.r()], writes=[self.gluwS.r()])
        o("dve", "tensor_scalar", [self.Cblk.r()], [self.Cblk.r()], out=self.Cblk[:, 1], in0=self.Cblk[:, 1], scalar1=-1.0, scalar2=None, op0=ALU.mult)

        dtq = self.dtq
        o("act", "activation", [self.ldq.r()], [dtq.r()], out=dtq[:, :], in_=self.ldq[:, l, :], func=AF.Exp)
        dtb = dtq[:, :].unsqueeze(2).to_broadcast([128, 4, 64])
        o("dve", "tensor_tensor", [AR.rg, dtq.r()], [AD.rg], out=AD.v4(), in0=AR.v4(), in1=dtb, op=ALU.mult)
        o("act", "activation", [AD.rg], [MAG.rg], out=MAG.ap, in_=AD.ap, func=AF.Exp)
        o("dve", "tensor_tensor", [AI.rg, dtq.r()], [ANG.rg], out=ANG.v4(), in0=AI.v4(), in1=dtb, op=ALU.mult)
        SN, CS = AD, U
        reduce_angle(ANG.ap, [ANG.rg], 0.0, SN.ap, [SN.rg], Y.ap, [Y.rg], CN.ap, [CN.rg], T.ap, [T.rg])
        o("act", "activation", [SN.rg], [SN.rg], out=SN.ap, in_=SN.ap, func=AF.Sin)
        reduce_angle(ANG.ap, [ANG.rg], PI / 2, CS.ap, [CS.rg], Y.ap, [Y.rg], CN.ap, [CN.rg], T.ap, [T.rg])
        o("act", "activation", [CS.rg], [CS.rg], out=CS.ap, in_=CS.ap, func=AF.Sin)
        ABR, ABI = CS, SN
        tt(ABR, MAG, CS, ALU.mult)
        tt(ABI, MAG, SN, ALU.mult)
        tt(T, AR, AR, ALU.mult)
        tt(Y, AI, AI, ALU.mult)
        tt(T, T, Y, ALU.add)
        o("dve", "reciprocal", [T.rg], [RDEN.rg], out=RDEN.ap, in_=T.ap)
        o("dve", "tensor_scalar", [ABR.rg], [ABR.rg], out=ABR.ap, in0=ABR.ap, scalar1=-1.0, scalar2=None, op0=ALU.add)
        tt(T, ABR, AR, ALU.mult)
        tt(Y, ABI, AI, ALU.mult)
        tt(T, T, Y, ALU.add)
        tt(CR, T, RDEN, ALU.mult)
        tt(T2, ABI, AR, ALU.mult)
        tt(Y, ABR, AI, ALU.mult)
        tt(T2, T2, Y, ALU.subtract)
        tt(CI, T2, RDEN, ALU.mult)
        BB = self.BB
        cr3 = CR.ap.rearrange("p (d x) -> p d x", d=2)
        ci3 = CI.ap.rearrange("p (d x) -> p d x", d=2)
        bre = BQ.ap[:, 0:128].unsqueeze(1).to_broadcast([128, 2, 128])
        bim = BQ.ap[:, 128:256].unsqueeze(1).to_broadcast([128, 2, 128])
        t3 = T.ap.rearrange("p (d x) -> p d x", d=2)
        y3 = Y.ap.rearrange("p (d x) -> p d x", d=2)
        bbr = BB[:, 0].rearrange("p a b -> p (a b)").rearrange("p (d x) -> p d x", d=2)
        bbi = BB[:, 1].rearrange("p a b -> p (a b)").rearrange("p (d x) -> p d x", d=2)
        o("dve", "tensor_tensor", [CR.rg, BQ.rg], [T.rg], out=t3, in0=cr3, in1=bre, op=ALU.mult)
        o("dve", "tensor_tensor", [CI.rg, BQ.rg], [Y.rg], out=y3, in0=ci3, in1=bim, op=ALU.mult)
        o("dve", "tensor_tensor", [T.rg, Y.rg], [BB.r(0, 256)], out=bbr, in0=t3, in1=y3, op=ALU.subtract)
        o("dve", "tensor_tensor", [CR.rg, BQ.rg], [T.rg], out=t3, in0=cr3, in1=bim, op=ALU.mult)
        o("dve", "tensor_tensor", [CI.rg, BQ.rg], [Y.rg], out=y3, in0=ci3, in1=bre, op=ALU.mult)
        o("dve", "tensor_tensor", [T.rg, Y.rg], [BB.r(256, 512)], out=bbi, in0=t3, in1=y3, op=ALU.add)

        As, DT, R, TH, C1, S1, Y16, N16, T16 = self.sAs, self.sDT, self.sR, self.sTH, self.sC1, self.sS1, self.sY16, self.sN16, self.sT16
        o("act", "activation", [As.r()], [DT.r()], out=DT[:, :], in_=As[:, 2, :], func=AF.Exp)
        o("dve", "tensor_tensor", [As.r(), DT.r()], [TH.r()], out=TH[:, :], in0=As[:, 0, :], in1=DT[:, :], op=ALU.mult)
        o("act", "activation", [TH.r()], [R.r()], out=R[:, :], in_=TH[:, :], func=AF.Exp)
        o("dve", "tensor_tensor", [As.r(), DT.r()], [TH.r()], out=TH[:, :], in0=As[:, 1, :], in1=DT[:, :], op=ALU.mult)
        reduce_angle(TH[:, :], [TH.r()], 0.0, S1[:, :], [S1.r()], Y16[:, :], [Y16.r()], N16[:, :], [N16.r()], T16[:, :], [T16.r()])
        o("act", "activation", [S1.r()], [S1.r()], out=S1[:, :], in_=S1[:, :], func=AF.Sin)
        reduce_angle(TH[:, :], [TH.r()], PI / 2, C1[:, :], [C1.r()], Y16[:, :], [Y16.r()], N16[:, :], [N16.r()], T16[:, :], [T16.r()])
        o("act", "activation", [C1.r()], [C1.r()], out=C1[:, :], in_=C1[:, :], func=AF.Sin)

        uTb, ytF, ytB = self.uTb, self.ytF, self.ytB
        u3 = uTb[:, :].rearrange("p (h t) -> p h t", h=2)
        fw.dma("pool", ("dma_start", dict(out=u3, in_=self.sU[:, :].rearrange("(h p) t -> p h t", p=128))), reads=[self.sU.r()], writes=[uTb.r()])
        Ct, St = self.rc, self.rs
        wsets = [self.s5w, self.s5w2]
        hsets = [(self.sq[0], self.sq[1]), (self.sq2[0], self.sq2[1])]
        tcnt = 0
        pend = None
        yield
        hp = self.hprev
        for dirn in range(2):
            yt = ytF if dirn == 0 else ytB
            yt3 = yt[:, :].rearrange("p (b c) -> p b c", c=256)
            if dirn == 1:
                for bt in range(18):
                    tb = (1 - bt) if bt < 2 else (19 - bt)
                    ub = self.s5ub[self.nxt("s5ub", 2)]
                    fw.dma("sp", ("dma_start", dict(out=ub[:, 0:256], in_=self.sUt[tb * 128:(tb + 1) * 128, :])), reads=[self.sUt.r()], writes=[ub.r(0, 256)])
                    ps = self.P[self.nxt("pa", 4)]
                    for half in range(2):
                        o("pe", "matmul", [ub.r(0, 256), self.Jb.r()], [ps.r(half * 128, half * 128 + 128)], out=ps[:, half * 128:(half + 1) * 128],
                          lhsT=ub[:, half * 128:(half + 1) * 128], rhs=self.Jb[:, :], start=True, stop=True)
                    o("act", "activation", [ps.r(0, 256)], [uTb.r(bt * 128, bt * 128 + 128), uTb.r(2304 + bt * 128, 2304 + bt * 128 + 128)],
                      out=u3[:, :, bt * 128:(bt + 1) * 128], in_=ps[:, 0:256].rearrange("p (h t) -> p h t", h=2), func=AF.Identity)
                    if bt % 3 == 2:
                        yield
            for st in range(8):
                ds = dirn * 8 + st
                half = st // 4
                hd = dirn * 2 + half
                o("act", "activation", [C1.r()], [Ct.r(0, 1)], out=Ct[:, 0:1], in_=C1[:, ds:ds + 1], func=AF.Identity)
                o("act", "activation", [S1.r()], [St.r(0, 1)], out=St[:, 0:1], in_=S1[:, ds:ds + 1], func=AF.Identity)
                gr, gi = wsets[0][2], wsets[0][3]
                m = 1
                while m < 512:
                    cm, sm = Ct[:, m - 1:m], St[:, m - 1:m]
                    o("dve", "tensor_scalar", [St.r(0, m)], [gr.r(0, m)], out=gr[:, 0:m], in0=St[:, 0:m], scalar1=sm, scalar2=None, op0=ALU.mult)
                    o("dve", "scalar_tensor_tensor", [Ct.r(0, m), gr.r(0, m)], [Ct.r(m, 2 * m)], out=Ct[:, m:2 * m], in0=Ct[:, 0:m], scalar=cm, in1=gr[:, 0:m], op0=ALU.mult, op1=ALU.subtract)
                    o("dve", "tensor_scalar", [Ct.r(0, m), St.r(0, m)], [gi.r(0, m)], out=gi[:, 0:m], in0=Ct[:, 0:m], scalar1=sm, scalar2=None, op0=ALU.mult)
                    o("dve", "scalar_tensor_tensor", [St.r(0, m), Ct.r(0, m), gi.r(0, m)], [St.r(m, 2 * m)], out=St[:, m:2 * m], in0=St[:, 0:m], scalar=cm, in1=gi[:, 0:m], op0=ALU.mult, op1=ALU.add)
                    m *= 2
                yield
                for ri in range(2):
                    for gl in range(2):
                        j = 2 * (st % 4) + gl
                        o("dve", "tensor_scalar", [BB.r(), self.rowm.r()], [self.Bblk.r(ri * 128 + gl * 64, ri * 128 + gl * 64 + 64)],
                          out=self.Bblk[:, ri, gl * 64:(gl + 1) * 64], in0=BB[:, ri, hd, :], scalar1=self.rowm[:, j:j + 1], scalar2=None, op0=ALU.mult)
                rb = R[:, ds:ds + 1]
                for ti, (t0, n) in enumerate(TILES):
                    dr, di, gr, gi = wsets[tcnt % 2]
                    hr, hi = hsets[tcnt % 2]
                    tcnt += 1
                    pr = self.P[self.nxt("pa", 4)]
                    pi_ = self.P[self.nxt("pa", 4)]
                    c0 = half * NTOK + t0
                    o("pe", "matmul", [self.Bblk.r(0, 128), uTb.r(c0, c0 + n)], [pr.r(0, n)], out=pr[:, :n], lhsT=self.Bblk[:, 0, :], rhs=uTb[:, c0:c0 + n], start=True, stop=True)
                    o("pe", "matmul", [self.Bblk.r(128, 256), uTb.r(c0, c0 + n)], [pi_.r(0, n)], out=pi_[:, :n], lhsT=self.Bblk[:, 1, :], rhs=uTb[:, c0:c0 + n], start=True, stop=True)
                    o("dve", "tensor_tensor", [pi_.r(0, n), St.r(0, n)], [gr.r(0, n)], out=gr[:, :n], in0=pi_[:, :n], in1=St[:, :n], op=ALU.mult)
                    o("dve", "tensor_tensor", [pr.r(0, n), Ct.r(0, n)], [dr.r(0, n)], out=dr[:, :n], in0=pr[:, :n], in1=Ct[:, :n], op=ALU.mult)
                    o("dve", "tensor_tensor", [dr.r(0, n), gr.r(0, n)], [dr.r(0, n)], out=dr[:, :n], in0=dr[:, :n], in1=gr[:, :n], op=ALU.add)
                    o("dve", "tensor_tensor", [pr.r(0, n), St.r(0, n)], [gi.r(0, n)], out=gi[:, :n], in0=pr[:, :n], in1=St[:, :n], op=ALU.mult)
                    o("dve", "tensor_tensor", [pi_.r(0, n), Ct.r(0, n)], [di.r(0, n)], out=di[:, :n], in0=pi_[:, :n], in1=Ct[:, :n], op=ALU.mult)
                    o("dve", "tensor_tensor", [di.r(0, n), gi.r(0, n)], [di.r(0, n)], out=di[:, :n], in0=di[:, :n], in1=gi[:, :n], op=ALU.subtract)
                    ini_r = 0.0 if ti == 0 else hp[:, 0:1]
                    ini_i = 0.0 if ti == 0 else hp[:, 1:2]
                    o("dve", "tensor_tensor_scan", [R.r(), dr.r(0, n), hp.r()], [gr.r(0, n)], out=gr[:, :n], data0=rb.to_broadcast([128, n]), data1=dr[:, :n], initial=ini_r, op0=ALU.mult, op1=ALU.add)
                    o("dve", "tensor_tensor_scan", [R.r(), di.r(0, n), hp.r()], [gi.r(0, n)], out=gi[:, :n], data0=rb.to_broadcast([128, n]), data1=di[:, :n], initial=ini_i, op0=ALU.mult, op1=ALU.add)
                    if ti < len(TILES) - 1:
                        hpt = self.hpt
                        cl, sl = Ct[:, n - 1:n], St[:, n - 1:n]
                        o("dve", "tensor_scalar", [gi.r(0, n), St.r(0, n)], [hpt.r(0, 1)], out=hpt[:, 0:1], in0=gi[:, n - 1:n], scalar1=sl, scalar2=None, op0=ALU.mult)
                        o("dve", "tensor_scalar", [gr.r(0, n), St.r(0, n)], [hpt.r(1, 2)], out=hpt[:, 1:2], in0=gr[:, n - 1:n], scalar1=sl, scalar2=None, op0=ALU.mult)
                        o("dve", "scalar_tensor_tensor", [gr.r(0, n), Ct.r(0, n), hpt.r(0, 1)], [hp.r(0, 1)], out=hp[:, 0:1], in0=gr[:, n - 1:n], scalar=cl, in1=hpt[:, 0:1], op0=ALU.mult, op1=ALU.subtract)
                        o("dve", "scalar_tensor_tensor", [gi.r(0, n), Ct.r(0, n), hpt.r(1, 2)], [hp.r(1, 2)], out=hp[:, 1:2], in0=gi[:, n - 1:n], scalar=cl, in1=hpt[:, 1:2], op0=ALU.mult, op1=ALU.add)
                    o("pool", "tensor_tensor", [gi.r(0, n), St.r(0, n)], [dr.r(0, n)], out=dr[:, :n], in0=gi[:, :n], in1=St[:, :n], op=ALU.mult)
                    o("pool", "tensor_tensor", [gr.r(0, n), St.r(0, n)], [di.r(0, n)], out=di[:, :n], in0=gr[:, :n], in1=St[:, :n], op=ALU.mult)
                    o("pool", "tensor_tensor", [gr.r(0, n), Ct.r(0, n)], [gr.r(0, n)], out=gr[:, :n], in0=gr[:, :n], in1=Ct[:, :n], op=ALU.mult)
                    o("pool", "tensor_tensor", [gi.r(0, n), Ct.r(0, n)], [gi.r(0, n)], out=gi[:, :n], in0=gi[:, :n], in1=Ct[:, :n], op=ALU.mult)
                    o("pool", "tensor_tensor", [gr.r(0, n), dr.r(0, n)], [hr.r(0, n)], out=hr[:, :n], in0=gr[:, :n], in1=dr[:, :n], op=ALU.subtract)
                    o("pool", "tensor_tensor", [gi.r(0, n), di.r(0, n)], [hi.r(0, n)], out=hi[:, :n], in0=gi[:, :n], in1=di[:, :n], op=ALU.add)
                    def readout(hr=hr, hi=hi, n=n, t0=t0, ds=ds, st=st, yt=yt, yt3=yt3):
                        py = self.P[4 + self.nxt("pb", 4)]
                        nb, b0 = n // 128, t0 // 128
                        for j in range(nb):
                            o("pe", "matmul", [hr.r(j * 128, j * 128 + 128), self.Cblk.r()], [py.r(j * 32, j * 32 + 32)], out=py[:, j * 32:(j + 1) * 32],
                              lhsT=hr[:, j * 128:(j + 1) * 128], rhs=self.Cblk[:, 0, ds, :], start=True, stop=False)
                            o("pe", "matmul", [hi.r(j * 128, j * 128 + 128), self.Cblk.r()], [py.r(j * 32, j * 32 + 32)], out=py[:, j * 32:(j + 1) * 32],
                              lhsT=hi[:, j * 128:(j + 1) * 128], rhs=self.Cblk[:, 1, ds, :], start=False, stop=True)
                        o("act", "activation", [py.r(0, nb * 32)], [yt.r((b0 + j) * 256 + st * 32, (b0 + j) * 256 + st * 32 + 32) for j in range(nb)],
                          out=yt3[:, b0:b0 + nb, st * 32:(st + 1) * 32], in_=py[:, 0:nb * 32].rearrange("p (b c) -> p b c", c=32), func=AF.Identity)

                    if pend is not None:
                        pend()
                    pend = readout
                    yield
        pend()
        ytF3 = ytF[:, :].rearrange("p (b c) -> p b c", c=256)
        ytB3 = ytB[:, :].rearrange("p (b c) -> p b c", c=256)
        for ti, (t0, n) in enumerate(TILES):
            nb, b0 = n // 128, t0 // 128
            gys = []
            for half in range(2):
                pc = self.P[self.nxt("pa", 4)]
                for j in range(nb):
                    tb = b0 + j
                    bt = (1 - tb) if tb < 2 else (19 - tb)
                    o("pe", "matmul", [ytF.r(tb * 256, tb * 256 + 256), self.Ib.r()], [pc.r(j * 128, j * 128 + 128)], out=pc[:, j * 128:(j + 1) * 128],
                      lhsT=ytF3[:, tb, half * 128:(half + 1) * 128], rhs=self.Ib[:, :], start=True, stop=False)
                    o("pe", "matmul", [ytB.r(bt * 256, bt * 256 + 256), self.Jb.r()], [pc.r(j * 128, j * 128 + 128)], out=pc[:, j * 128:(j + 1) * 128],
                      lhsT=ytB3[:, bt, half * 128:(half + 1) * 128], rhs=self.Jb[:, :], start=False, stop=True)
                ut = self.s5w[half]
                fw.dma("sp", ("dma_start", dict(out=ut[:, :n], in_=self.sU[half * 128:(half + 1) * 128, t0:t0 + n])), reads=[self.sU.r()], writes=[ut.r(0, n)])
                yv = self.s5w[2 + half]
                o("dve", "scalar_tensor_tensor", [ut.r(0, n), self.s5dS.r(), pc.r(0, n)], [yv.r(0, n)], out=yv[:, :n], in0=ut[:, :n], scalar=self.s5dS[:, l, half:half + 1], in1=pc[:, :n], op0=ALU.mult, op1=ALU.add)
                o("dve", "tensor_tensor", [yv.r(0, n)], [ut.r(0, n)], out=ut[:, :n], in0=yv[:, :n], in1=yv[:, :n], op=ALU.mult)
                o("dve", "tensor_scalar", [ut.r(0, n)], [ut.r(0, n)], out=ut[:, :n], in0=ut[:, :n], scalar1=0.044715, scalar2=1.0, op0=ALU.mult, op1=ALU.add)
                o("dve", "tensor_tensor", [ut.r(0, n), yv.r(0, n)], [ut.r(0, n)], out=ut[:, :n], in0=ut[:, :n], in1=yv[:, :n], op=ALU.mult)
                o("act", "activation", [ut.r(0, n)], [ut.r(0, n)], out=ut[:, :n], in_=ut[:, :n], func=AF.Sigmoid, scale=1.5957691216057308)
                gy = self.s5gy[half]
                o("dve", "tensor_tensor", [ut.r(0, n), yv.r(0, n)], [gy.r(0, n)], out=gy[:, :n], in0=ut[:, :n], in1=yv[:, :n], op=ALU.mult)
                gys.append(gy)
            pm = [self.P[4 + m_] for m_ in range(4)]
            for m_ in range(4):
                for half in range(2):
                    o("pe", "matmul", [self.gluwS.r(), gys[half].r(0, n)], [pm[m_].r(0, n)], out=pm[m_][:, :n],
                      lhsT=self.gluwS[:, half, m_ * 128:(m_ + 1) * 128], rhs=gys[half][:, :n], start=(half == 0), stop=(half == 1))
            for mm in range(2):
                sgm = self.rstd
                o("act", "activation", [pm[2 + mm].r(0, n), self.glubS.r()], [sgm.r(0, n)], out=sgm[:, :n], in_=pm[2 + mm][:, :n], func=AF.Sigmoid, bias=self.glubS[:, l, 2 + mm:3 + mm], scale=1.0)
                ob = self.s5ob
                o("dve", "scalar_tensor_tensor", [pm[mm].r(0, n), self.glubS.r(), sgm.r(0, n)], [ob.r(0, n)], out=ob[:, :n], in0=pm[mm][:, :n], scalar=self.glubS[:, l, mm:mm + 1], in1=sgm[:, :n], op0=ALU.add, op1=ALU.mult)
                fw.dma("sp", ("dma_start", dict(out=self.mix[768 + mm * 128:768 + (mm + 1) * 128, t0:t0 + n], in_=ob[:, :n])), reads=[ob.r(0, n)], writes=[self.mix.r(16 + ti, 17 + ti)])
            yield

    def zero_mix(self, r0, r1):
        fw = self.fw
        fw.op("dve", ("memset", dict(ap=self.zt[:], constant=0.0)), writes=[self.zt.r()])
        for rr in range(r0, r1, 128):
            for ti, (t0, n) in enumerate(TILES):
                fw.dma("sp", ("dma_start", dict(out=self.mix[rr:rr + 128, t0:t0 + n], in_=self.zt[:, :n])), reads=[self.zt.r()], writes=[self.mix.r(ti, ti + 1)])

    def attn(self, l):
        fw = self.fw
        fw.dma("sp", ("dma_start", dict(out=self.kTall[:], in_=self.sK[:, :])), reads=[self.sK.r()], writes=[self.kTall.r()])
        fw.dma("sp", ("dma_start", dict(out=self.vt[:], in_=self.sV[:, :].rearrange("(b p) c -> p b c", p=128))), reads=[self.sV.r()], writes=[self.vt.r()])
        fw.dma("pool", ("dma_start", dict(out=self.mbnext[:, :], in_=self.mnext_d[:])), reads=[self.mnext_d.r()], writes=[self.mbnext.r()])
        units = [(hk, qb) for hk in range(2) for qb in range(18)]

        def load_q(u):
            hk, qb = units[u]
            qt = self.qt[u % 2]
            fw.dma("sp", ("dma_start", dict(out=qt[hk * 64:hk * 64 + 64, :].rearrange("d (h t) -> d h t", h=4),
                                             in_=self.sQ[hk * 256:(hk + 1) * 256, qb * 128:qb * 128 + 128].rearrange("(h d) t -> d h t", d=64))),
                   reads=[self.sQ.r()], writes=[qt.r()])

        def finish(u):
            hk, qb = units[u]
            ob, den = self.ob[u % 2], self.tmp[u % 2]
            fw.op("dve", ("tensor_tensor", dict(out=ob[0:64, :], in0=ob[0:64, :], in1=den[0:64, :], op=ALU.mult)), reads=[ob.r(), den.r()], writes=[ob.r()])
            ti = 0 if qb < 2 else 1 + (qb - 2) // 4
            fw.dma("sp", ("dma_start", dict(out=self.mix[256 + hk * 256:256 + (hk + 1) * 256, qb * 128:qb * 128 + 128].rearrange("(h d) t -> d h t", d=64),
                                             in_=ob[0:64, :].rearrange("d (h t) -> d h t", h=4))),
                   reads=[ob.r()], writes=[self.mix.r(8 + ti, 9 + ti)])

        load_q(0)
        for u, (hk, qb) in enumerate(units):
            if u + 1 < len(units):
                load_q(u + 1)
            if u > 0:
                finish(u - 1)
            qt = self.qt[u % 2]
            pb_ = hk * 64
            kbs = [(0, None), (1, None)]
            if qb >= 2:
                for dl, mk in ((-1, self.mbprev), (0, None), (1, self.mbnext)):
                    kb = qb + dl
                    if 2 <= kb <= 17:
                        kbs.append((kb, mk))
            pp = self.nxt("pb", 2)
            po, pd = self.P[4 + 2 * pp], self.P[5 + 2 * pp]
            for idx, (kb, mk) in enumerate(kbs):
                ps = self.P[self.nxt("pa", 4)]
                fw.op("pe", ("matmul", dict(out=ps[:, :], lhsT=self.kTall[pb_:pb_ + 64, kb * 128:(kb + 1) * 128], rhs=qt[pb_:pb_ + 64, :], start=True, stop=(mk is None))),
                      reads=[self.kTall.r(kb * 128, (kb + 1) * 128), qt.r()], writes=[ps.r()])
                if mk is not None:
                    fw.op("pe", ("matmul", dict(out=ps[:, :], lhsT=self.Ib[:, :], rhs=mk[:, :], start=False, stop=True)),
                          reads=[self.Ib.r(), mk.r()], writes=[ps.r()])
                pT = self.pT[self.nxt("pT", 3)]
                fw.op("act", ("activation", dict(out=pT[:, :], in_=ps[:, :], func=AF.Exp, scale=0.125)), reads=[ps.r()], writes=[pT.r()])
                st, sp_ = (idx == 0), (idx == len(kbs) - 1)
                fw.op("pe", ("matmul", dict(out=po[0:64, :], lhsT=self.vt[:, kb, hk * 64:(hk + 1) * 64], rhs=pT[:, :], start=st, stop=sp_)),
                      reads=[self.vt.r(), pT.r()], writes=[po.r()])
                fw.op("pe", ("matmul", dict(out=pd[0:64, :], lhsT=self.onesb[:, 0:64], rhs=pT[:, :], start=st, stop=sp_)),
                      reads=[self.onesb.r(), pT.r()], writes=[pd.r()])
            ob, den = self.ob[u % 2], self.tmp[u % 2]
            for h in range(4):
                fw.op("act", ("activation", dict(out=den[0:64, h * 128:(h + 1) * 128], in_=pd[0:64, h * 128:(h + 1) * 128], func=AF.Ln,
                                                 bias=self.esink[:, l, hk * 4 + h:hk * 4 + h + 1], scale=1.0)),
                      reads=[pd.r(), self.esink.r()], writes=[den.r()])
            fw.op("act", ("activation", dict(out=den[0:64, :], in_=den[0:64, :], func=AF.Exp, scale=-1.0)), reads=[den.r()], writes=[den.r()])
            fw.op("act", ("activation", dict(out=ob[0:64, :], in_=po[0:64, :], func=AF.Identity)), reads=[po.r()], writes=[ob.r()])
            yield
        finish(len(units) - 1)
        yield

    def outproj(self, l, tiles):
        fw, xT, hT = self.fw, self.xT, self.hT
        wv = self.wout[l].rearrange("(k p) n -> p k n", p=128)
        for c in range(2):
            fw.dma("pool", ("dma_start", dict(out=self.wA[c][:], in_=wv[:, :, c * 512:(c + 1) * 512])), reads=[self.wout.r()], writes=[self.wA[c].r()])
        for (t0, n) in tiles:
            var = 1 if t0 == 0 else 0
            fw.dma("sp", ("dma_start", dict(out=hT[:, :, t0:t0 + n], in_=self.mix[:, t0:t0 + n].rearrange("(k p) t -> p k t", p=128))),
                   reads=[self.mix.r()], writes=[hT.r(k * NTOK + t0, k * NTOK + t0 + n) for k in range(8)])
            for o in range(8):
                po = self.P[4 + self.nxt("pb", 4)]
                self.mm8(po, n, self.wA[o // 4], (o % 4) * 128, t0)
                fw.op("dve", ("scalar_tensor_tensor", dict(
                    out=xT[:, o, t0:t0 + n], in0=po[:, :n], scalar=self.GT[:, l, 1, o, var:var + 1], in1=xT[:, o, t0:t0 + n], op0=ALU.mult, op1=ALU.add)),
                    reads=[po.r(0, n), self.GT.r(), xT.r(o * NTOK + t0, o * NTOK + t0 + n)], writes=[xT.r(o * NTOK + t0, o * NTOK + t0 + n)])

    def final(self):
        fw, xT = self.fw, self.xT
        for (t0, n) in TILES[1:]:
            self.rms_stats(t0, n)
            for sb4 in range(n // 128):
                ot = self.xin[self.nxt("xin", 2)]
                tb = t0 + sb4 * 128
                for k4 in range(2):
                    ps = self.P[self.nxt("pa", 4)]
                    for kk in range(4):
                        k = k4 * 4 + kk
                        tmp = self.tmp[self.nxt("tmp", 2)]
                        fw.op("dve", ("scalar_tensor_tensor", dict(
                            out=tmp[:, :128], in0=xT[:, k, tb:tb + 128], scalar=self.fngs[:, k:k + 1], in1=self.rstd[:, sb4 * 128:(sb4 + 1) * 128], op0=ALU.mult, op1=ALU.mult)),
                            reads=[xT.r(k * NTOK + tb, k * NTOK + tb + 128), self.fngs.r(), self.rstd.r(0, n)], writes=[tmp.r(0, 128)])
                        fw.op("pe", ("transpose", dict(out=ps[:, kk * 128:(kk + 1) * 128], in_=tmp[:, :128], identity=self.ident[:])),
                              reads=[tmp.r(0, 128), self.ident.r()], writes=[ps.r(kk * 128, (kk + 1) * 128)])
                    fw.op("act", ("activation", dict(out=ot[:, k4 * 512:(k4 + 1) * 512], in_=ps[:], func=AF.Identity)),
                          reads=[ps.r()], writes=[ot.r(k4 * 512, (k4 + 1) * 512)])
                r0 = tb - NCTX
                fw.dma("sp", ("dma_start", dict(out=self.y[r0:r0 + 128, :], in_=ot[:])), reads=[ot.r()], writes=[self.y.r()])

    def build(self):
        self.prologue()
        last = DEPTH - 1
        for l in range(DEPTH):
            if self.stage < 1:
                break
            self.norm_h(l, 0, TILES)
            self.ffn(l, 0, TILES)
            if self.stage < 2:
                break
            tl = TILES if l < last else TILES[1:]
            if self.stage >= 3:
                self.norm_h(l, 1, TILES)
                self.inproj(l)
                gens = [self.attn(l)]
                if self.stage < 4:
                    self.zero_mix(0, 256)
                else:
                    gens.append(self.hgrn(l))
                if self.stage < 5:
                    self.zero_mix(768, 1024)
                else:
                    gens.append(self.s5(l))
                if not INTERLEAVE:
                    for g in gens:
                        for _ in g:
                            pass
                else:
                    while gens:
                        for g in list(gens):
                            try:
                                next(g)
                            except StopIteration:
                                gens.remove(g)
                self.outproj(l, tl)
                if self.stage == 3 and l == 0:
                    break
            self.norm_h(l, 2, tl)
            self.ffn(l, 2, tl)
        self.final()
        self.fw.wait_all("sp")
        self.fw.emit()
        return self.nc


def host_prep(inp, b):
    f32 = np.float32
    d = {}
    d["x_in"] = np.ascontiguousarray(np.concatenate([inp["ctx"][b], inp["x"][b]], axis=0), dtype=f32)
    cv = np.stack([inp["c"][b], inp["c_ctx"]], axis=-1)
    d["cvec"] = np.ascontiguousarray(cv.reshape(8, 128, 2).transpose(1, 0, 2), dtype=f32)
    d["ada_w"] = inp["ada_w"]
    d["ada_b_t"] = np.ascontiguousarray(inp["ada_b"].reshape(DEPTH, 72, 128).transpose(2, 0, 1), dtype=f32)
    d["norm_g_t"] = np.ascontiguousarray(inp["norm_g"].reshape(DEPTH, 3, 8, 128).transpose(3, 0, 1, 2), dtype=f32)
    d["fng_t"] = np.ascontiguousarray(inp["final_norm_g"].reshape(8, 128).T, dtype=f32)
    d["ffn_w1"] = inp["ffn_w1"]
    d["ffn_w2"] = inp["ffn_w2"]
    d["ident"] = np.eye(128, dtype=f32)
    d["win_ext"] = _WIN_EXT(inp)
    d["w_out"] = inp["w_out"]
    rc, rs = _ROPE()
    d["ropeC"], d["ropeS"] = rc, rs
    ii = np.arange(128)[:, None]
    jj = np.arange(128)[None, :]
    d["mprev"] = np.ascontiguousarray(np.tile(np.where(jj <= ii, 0.0, -240000.0).astype(f32), (1, 4)))
    d["mnext"] = np.ascontiguousarray(np.tile(np.where(ii <= jj, 0.0, -240000.0).astype(f32), (1, 4)))
    d["lb_t"] = np.ascontiguousarray(inp["hgrn_lower_bounds"].reshape(DEPTH, 2, 4, 64).transpose(3, 0, 1, 2).reshape(64, DEPTH, 8), dtype=f32)
    d["hng_t"] = np.ascontiguousarray(inp["hgrn_norm_g"].reshape(DEPTH, 4, 64).transpose(2, 0, 1), dtype=f32)
    rm = np.ones((64, 1024), f32)
    rm[:, ::32] = 0.0
    d["rmask"] = rm
    si = np.arange(64)[:, None]
    tj = np.arange(64)[None, :]
    d["hmask"] = np.ascontiguousarray(np.stack([(si <= tj), (si >= tj)], axis=1).astype(f32))
    d.update(_S5(inp))
    d["sink_t"] = np.ascontiguousarray(np.broadcast_to(inp["attn_sink"][None], (64, DEPTH, 8)), dtype=f32)
    return d


_HC = {}


def _WIN_EXT(inp):
    if "win" in _HC:
        return _HC["win"]
    w = inp["w_in"]
    aq, ai, af, ab, ag = (w[:, :, i * 256:(i + 1) * 256] for i in range(5))
    bq = w[:, :, 1280:1792]
    bk = w[:, :, 1792:1920]
    bv = w[:, :, 1920:2048]
    cu = w[:, :, 2048:2304]
    perm = np.arange(64).reshape(2, 2, 16)[:, ::-1, :].reshape(64)

    def partner(m):
        nh = m.shape[-1] // 64
        idx = (np.arange(nh)[:, None] * 64 + perm[None, :]).reshape(-1)
        return m[:, :, idx]

    bqp, bkp = partner(bq), partner(bk)
    cols = [aq, af, ab, ag]
    for j in range(4):
        cols += [bq[:, :, j * 128:(j + 1) * 128], bqp[:, :, j * 128:(j + 1) * 128]]
    cols += [bk, bkp, cu, ai, bv, cu]
    _HC["win"] = np.ascontiguousarray(np.concatenate(cols, axis=-1), dtype=np.float32)
    assert _HC["win"].shape[-1] == 3200
    return _HC["win"]


def _S5(inp):
    if "s5" in _HC:
        return _HC["s5"]
    f32 = np.float32
    L = DEPTH
    are, aim, ldt = inp["s5_a_re"], inp["s5_a_im"], inp["s5_log_dt"]

    def st_layout(a):
        return a.reshape(L, 2, 8, 2, 64).transpose(3, 4, 0, 1, 2).reshape(128, L, 16)

    ld_s = np.broadcast_to(ldt.reshape(L, 2, 8, 2, 1), (L, 2, 8, 2, 64)).transpose(3, 4, 0, 1, 2).reshape(128, L, 16)
    As = np.stack([st_layout(are), st_layout(aim), ld_s], axis=2)

    def q_layout(a):
        t = a.reshape(L, 2, 2, 8, 64).transpose(3, 0, 1, 2, 4)
        t = np.repeat(t[:, None], 16, axis=1)
        return t.reshape(128, L, 4, 64)

    Aq = np.stack([q_layout(are), q_layout(aim)], axis=2)
    ldq = np.repeat(ldt.reshape(L, 2, 2, 8).transpose(3, 0, 1, 2)[:, None], 16, axis=1).reshape(128, L, 4)

    def b_layout(b):
        return b.reshape(L, 2, 8, 64, 16).transpose(2, 4, 0, 1, 3).reshape(128, L, 2, 64)

    Bq = np.stack([b_layout(inp["s5_b_re"]), b_layout(inp["s5_b_im"])], axis=2)

    def c_layout(c):
        t = c.reshape(L, 2, 8, 2, 16, 64)
        out = np.zeros((2, 64, L, 2, 8, 2, 16), f32)
        for gl in range(2):
            out[gl, :, :, :, :, gl, :] = t[:, :, :, gl, :, :].transpose(4, 0, 1, 2, 3)
        return out.reshape(128, L, 16, 32)

    Cb = np.stack([c_layout(inp["s5_c_re"]), c_layout(inp["s5_c_im"])], axis=2)
    c = lambda a: np.ascontiguousarray(a, dtype=f32)
    _HC["s5"] = {
        "s5_As": c(As), "s5_Aq": c(Aq), "s5_LDq": c(ldq), "s5_Bq": c(Bq), "s5_Cblk": c(Cb),
        "s5_d_t": c(inp["s5_d"].reshape(L, 2, 128).transpose(2, 0, 1)),
        "glu_b_t": c(inp["s5_glu_b"].reshape(L, 4, 128).transpose(2, 0, 1)),
        "s5_glu_w": inp["s5_glu_w"],
        "rowmask": c(np.arange(128)[:, None] // 16 == np.arange(8)[None, :]),
        "Jmat": c(np.eye(128)[::-1]),
    }
    return _HC["s5"]


def _ROPE():
    if "rope" in _HC:
        return _HC["rope"]
    f32 = np.float32
    t = np.arange(NLAT)
    row = (t // 64).astype(f32)
    col = (t % 64).astype(f32)
    inv = (f32(10000.0) ** (-np.arange(16, dtype=f32) / f32(16))).astype(f32)
    ang = np.stack([row[:, None] * inv, col[:, None] * inv], axis=1).astype(f32)
    cos, sin = np.cos(ang).astype(f32), np.sin(ang).astype(f32)
    C = np.ones((64, NTOK), f32)
    S = np.zeros((64, NTOK), f32)
    for ax in range(2):
        for two in range(2):
            d0 = ax * 32 + two * 16
            C[d0:d0 + 16, NCTX:] = cos[:, ax, :].T
            S[d0:d0 + 16, NCTX:] = (-sin[:, ax, :].T if two == 0 else sin[:, ax, :].T)
    _HC["rope"] = (np.ascontiguousarray(np.tile(C, (2, 1))), np.ascontiguousarray(np.tile(S, (2, 1))))
    return _HC["rope"]


_NC_CACHE = {}


def kernel(**inputs):
    inp = {k: np.asarray(v) for k, v in inputs.items()}
    stage = int(os.environ.get("KSTAGE", "99"))
    ncores = int(os.environ.get("KCORES", "8"))
    if stage not in _NC_CACHE:
        _NC_CACHE[stage] = K(stage).build()
    nc = _NC_CACHE[stage]
    in_maps = [host_prep(inp, b) for b in range(ncores)]
    res = run_bass_kernel_spmd(nc, in_maps, core_ids=list(range(ncores)))
    out = np.stack([np.asarray(r["y"]) for r in res.results], axis=0)
    return out.astype(np.float32)
```

```python
import os
import numpy as np
import concourse.bass as bass
import concourse.mybir as mybir
from concourse.bass_utils import run_bass_kernel_spmd

F32 = mybir.dt.float32
BF16 = mybir.dt.bfloat16
AF = mybir.ActivationFunctionType
ALU = mybir.AluOpType

NTOK = 2304
NCTX = 256
NLAT = 2048
DM = 1024
DFF = 2816
DEPTH = 4
TILES = [(0, 256), (256, 512), (768, 512), (1280, 512), (1792, 512)]
EPS = 1e-6
INTERLEAVE = True


class Reg:
    __slots__ = ("base", "lo", "hi")

    def __init__(self, base, lo, hi):
        self.base, self.lo, self.hi = base, lo, hi


class Buf:
    _n = 0

    def __init__(self, t, size, base=None, off=0):
        self.t = t
        self.size = size
        if base is None:
            Buf._n += 1
            base = "b%d" % Buf._n
        self.base = base
        self.off = off

    def r(self, lo=0, hi=None):
        if hi is None:
            hi = self.size
        return Reg(self.base, self.off + lo, self.off + hi)

    def __getitem__(self, k):
        return self.t[k]


class Alias(Buf):
    def __init__(self, view, orig):
        self.t = view
        self.size = orig.size
        self.base = orig.base
        self.off = orig.off

    def r(self, lo=0, hi=None):
        return Reg(self.base, self.off, self.off + self.size)


class View(Buf):
    def __init__(self, view, orig, off, size, scale=1):
        self.t = view
        self.size = size
        self.base = orig.base
        self.off = orig.off + off
        self.scale = scale

    def r(self, lo=0, hi=None):
        if hi is None:
            hi = self.size
        return Reg(self.base, self.off + lo * self.scale, self.off + hi * self.scale)


class Q:
    def __init__(self, name, sem):
        self.name, self.sem = name, sem
        self.count = 0
        self.prog = []
        self.waited = {}


class FW:
    def __init__(self, nc):
        self.nc = nc
        self.q = {}
        for n in ("pe", "act", "dve", "pool", "sp"):
            self.q[n] = Q(n, nc.alloc_semaphore("s_" + n))
        self.NDS = 8
        self.dsems = {n: [nc.alloc_semaphore("d_%s%d" % (n, i)) for i in range(self.NDS)] for n in ("sp", "pool")}
        self.dcount = {n: 0 for n in self.dsems}
        self.acc = {}
        self.n_instr = 0

    def _need(self, ev, waits):
        sem, val, key = ev
        cur = waits.get(key)
        if cur is None or cur[1] < val:
            waits[key] = (sem, val)

    def _deps(self, qn, reads, writes, is_dma):
        waits = {}
        for regs, w in ((reads, False), (writes, True)):
            for rg in regs:
                lst = self.acc.get(rg.base)
                if not lst:
                    continue
                for a in lst:
                    if a[1] <= rg.lo or a[0] >= rg.hi:
                        continue
                    if not (a[2] or w):
                        continue
                    if a[3] == qn and not a[5] and not is_dma and not (a[2] and not w):
                        continue
                    self._need(a[4], waits)
        return waits

    def _record(self, qn, reads, writes, ev, is_dma):
        for rg in writes:
            lst = self.acc.setdefault(rg.base, [])
            lst[:] = [a for a in lst if not (a[0] >= rg.lo and a[1] <= rg.hi)]
            lst.append([rg.lo, rg.hi, True, qn, ev, is_dma])
        for rg in reads:
            lst = self.acc.setdefault(rg.base, [])
            lst[:] = [a for a in lst if not (not a[2] and a[3] == qn and a[5] == is_dma and a[0] == rg.lo and a[1] == rg.hi)]
            lst.append([rg.lo, rg.hi, False, qn, ev, is_dma])

    def _emit_waits(self, q, waits):
        for key, (sem, val) in waits.items():
            if q.waited.get(key, 0) >= val:
                continue
            q.waited[key] = val
            q.prog.append(("w", sem, val))

    def op(self, qn, fn, reads=(), writes=()):
        q = self.q[qn]
        waits = self._deps(qn, reads, writes, False)
        self._emit_waits(q, waits)
        q.count += 1
        ev = (q.sem, q.count, qn)
        q.prog.append(("o", fn, q.sem, 1))
        self._record(qn, reads, writes, ev, False)
        self.n_instr += 1
        return ev

    def dma(self, qn, fn, reads=(), writes=()):
        q = self.q[qn]
        waits = self._deps(qn, reads, writes, True)
        j = self.dcount[qn]
        self.dcount[qn] += 1
        s = j % self.NDS
        sem = self.dsems[qn][s]
        key = "d_%s%d" % (qn, s)
        prev = 16 * (j // self.NDS)
        if prev > 0:
            self._need((sem, prev, key), waits)
        self._emit_waits(q, waits)
        ev = (sem, prev + 16, key)
        q.prog.append(("o", fn, sem, 16))
        self._record(qn, reads, writes, ev, True)
        self.n_instr += 1
        return ev

    def wait_all(self, qn):
        q = self.q[qn]
        waits = {}
        for lst in self.acc.values():
            for a in lst:
                self._need(a[4], waits)
        self._emit_waits(q, waits)

    def emit(self):
        nc = self.nc
        me = self

        def run(qn, eng):
            for it in me.q[qn].prog:
                if it[0] == "w":
                    eng.wait_ge(it[1], it[2])
                else:
                    getattr(eng, it[1][0])(**it[1][1]).then_inc(it[2], it[3])

        with nc.Block() as block:
            @block.tensor
            def _(e):
                run("pe", e)

            @block.scalar
            def _(e):
                run("act", e)

            @block.vector
            def _(e):
                run("dve", e)

            @block.gpsimd
            def _(e):
                run("pool", e)

            @block.sync
            def _(e):
                run("sp", e)


class K:
    def __init__(self, stage=99):
        self.stage = stage
        nc = self.nc = bass.Bass("TRN2", target_bir_lowering=False)
        fw = self.fw = FW(nc)
        self.din = {}
        self.rot = {}

        def dram_in(name, shape, dt=F32):
            t = nc.dram_tensor(name, list(shape), dt, kind="ExternalInput").ap()
            n = int(np.prod(shape[1:])) if len(shape) > 1 else 1
            self.din[name] = Buf(t, max(n, 1))
            return self.din[name]

        self.x_in = dram_in("x_in", [NTOK, DM])
        self.cvec = dram_in("cvec", [128, 8, 2])
        self.ada_w = dram_in("ada_w", [DEPTH, DM, 9 * DM])
        self.ada_b = dram_in("ada_b_t", [128, DEPTH, 72])
        self.norm_g = dram_in("norm_g_t", [128, DEPTH, 3, 8])
        self.fng = dram_in("fng_t", [128, 8])
        self.w1 = dram_in("ffn_w1", [DEPTH, 2, DM, 2 * DFF])
        self.w2 = dram_in("ffn_w2", [DEPTH, 2, DFF, DM])
        self.ident_d = dram_in("ident", [128, 128])
        self.win = dram_in("win_ext", [DEPTH, DM, 3200])
        self.wout = dram_in("w_out", [DEPTH, DM, DM])
        self.ropeC = dram_in("ropeC", [128, NTOK])
        self.ropeS = dram_in("ropeS", [128, NTOK])
        self.mprev_d = dram_in("mprev", [128, 512])
        self.mnext_d = dram_in("mnext", [128, 512])
        self.sink_d = dram_in("sink_t", [64, DEPTH, 8])

        self.lb_d = dram_in("lb_t", [64, DEPTH, 8])
        self.hng_d = dram_in("hng_t", [64, DEPTH, 4])
        self.rm_d = dram_in("rmask", [64, 1024])
        self.hm_d = dram_in("hmask", [64, 2, 64])

        self.s5A = dram_in("s5_As", [128, DEPTH, 3, 16])
        self.s5Q = dram_in("s5_Aq", [128, DEPTH, 2, 4, 64])
        self.s5LDq = dram_in("s5_LDq", [128, DEPTH, 4])
        self.s5B = dram_in("s5_Bq", [128, DEPTH, 2, 2, 64])
        self.s5C = dram_in("s5_Cblk", [128, DEPTH, 2, 16, 32])
        self.s5d_d = dram_in("s5_d_t", [128, DEPTH, 2])
        self.glub_d = dram_in("glu_b_t", [128, DEPTH, 4])
        self.gluw_d = dram_in("s5_glu_w", [DEPTH, 256, 512])
        self.rowm_d = dram_in("rowmask", [128, 8])
        self.J_d = dram_in("Jmat", [128, 128])

        def scr(name, shape, dt):
            return Buf(nc.dram_tensor(name, list(shape), dt, kind="Internal").ap(), 24)

        self.mix = scr("mix", [DM, NTOK], BF16)
        self.sQ = scr("sQ", [512, NTOK], BF16)
        self.sK = scr("sK", [128, NTOK], BF16)
        self.sV = scr("sV", [NTOK, 128], BF16)
        self.sA = scr("sA", [4, 256, NTOK], F32)
        self.sAv = scr("sAv", [NTOK, 256], BF16)
        self.sO = scr("sO", [256, NTOK], F32)
        self.sU = scr("sU", [256, NTOK], F32)
        self.sUt = scr("sUt", [NTOK, 256], BF16)
        y = nc.dram_tensor("y", [NLAT, DM], F32, kind="ExternalOutput").ap()
        self.y = Buf(y, DM)

        def sb(name, shape, dt=F32):
            return Buf(nc.alloc_sbuf_tensor(name, list(shape), dt), int(np.prod(shape[1:])))

        self.xT = sb("xT", [128, 8, NTOK])
        self.hT = sb("hT", [128, 8, NTOK], BF16)
        self.wA = [sb("wA%d" % i, [128, 8, 512], BF16) for i in range(2)]
        self.wB = [sb("wB%d" % i, [128, 2, 1024], BF16) for i in range(2)]
        self.sg = [sb("sg%d" % i, [128, 512]) for i in range(2)]
        self.tmp = [sb("tmp%d" % i, [128, 512]) for i in range(2)]
        self.rstd = sb("rstd", [128, 512])
        self.hid = [sb("hid%d" % i, [128, 2, 512], BF16) for i in range(2)]
        self.sq = [sb("sq%d" % i, [128, 512], BF16) for i in range(2)]
        self.xin = [sb("xin%d" % i, [128, 1024]) for i in range(2)]
        self.ident = sb("identf", [128, 128])
        self.onesb = sb("onesb", [128, 128], BF16)
        self.cv = sb("cv", [128, 8, 2])
        self.csb = sb("csb", [128, 8, 2], BF16)
        self.modr = sb("modr", [128, DEPTH, 72, 2])
        self.adab = sb("adab", [128, DEPTH, 72])
        self.ng = sb("ng", [128, DEPTH, 3, 8])
        self.fngs = sb("fngs", [128, 8])
        self.GS = sb("GS", [128, DEPTH, 3, 8, 2])
        self.GT = sb("GT", [128, DEPTH, 3, 8, 2])

        self.esink = sb("esink", [64, DEPTH, 8])
        self.rc = sb("rc", [128, 512])
        self.rs = sb("rs", [128, 512])
        self.kTall = sb("kTall", [128, NTOK], BF16)
        self.vt = sb("vt", [128, 18, 128], BF16)
        self.qt = [sb("qt%d" % i, [128, 512], BF16) for i in range(2)]
        self.sq2 = [sb("sq2_%d" % i, [128, 512], BF16) for i in range(2)]
        self.pT = [sb("pT%d" % i, [128, 512], BF16) for i in range(3)]
        self.ob = [sb("ob%d" % i, [128, 512], BF16) for i in range(2)]
        self.zt = Alias(self.ob[0].t, self.ob[0])
        self.LB = sb("LB", [64, DEPTH, 8])
        self.OML = sb("OML", [64, DEPTH, 8])
        self.lbw = sb("lbw", [64, DEPTH, 8])
        self.lbs = sb("lbs", [64, 8])
        self.hng = sb("hng", [64, DEPTH, 4])
        self.RM = sb("RM", [64, 1024])
        self.hmask = sb("hmask_s", [64, 2, 64], BF16)
        self.identb = sb("identb", [64, 64], BF16)
        self.HW = [sb("HW%d" % i, [64, 1024]) for i in range(2)] + [
            Alias(self.wB[i].t[:].rearrange("p f n -> p (f n)").bitcast(F32)[0:64, :], self.wB[i]) for i in range(2)]
        self.HQE = Alias(self.hid[0].t[:].rearrange("p f n -> p (f n)")[0:64, :], self.hid[0])
        self.HKE = Alias(self.hid[1].t[:].rearrange("p f n -> p (f n)")[0:64, :], self.hid[1])
        self.HV = sb("HV", [32, 8, 256], BF16)
        self.HS = Alias(self.sg[0].t[0:64, 0:256], self.sg[0])
        self.HT1 = Alias(self.sg[1].t[0:64, 0:256], self.sg[1])
        self.HSP = sb("HSP", [64, 256], BF16)
        self.HKT3 = [sb("HKT3_%d" % i, [32, 256], BF16) for i in range(3)]
        self.HSC3 = [sb("HSC3_%d" % i, [32, 128], BF16) for i in range(3)]
        self.HKT = self.HKT3[0]
        self.HSC = self.HSC3[0]
        self.HM1 = sb("HM1", [64, 32])
        self.HEM = sb("HEM", [64, 32])
        self.HET = sb("HET", [64, 32])
        self.HE2 = sb("HE2", [64, 32])
        self.sAs = sb("s5As_s", [128, 3, 16])
        self.sDT = sb("s5DT", [128, 16])
        self.sR = sb("s5R", [128, 16])
        self.sTH = sb("s5TH", [128, 16])
        self.sC1 = sb("s5C1", [128, 16])
        self.sS1 = sb("s5S1", [128, 16])
        self.sY16 = sb("s5Y16", [128, 16])
        self.sN16 = sb("s5N16", [128, 16])
        self.sT16 = sb("s5T16", [128, 16])
        self.ldq = sb("s5ldq", [128, DEPTH, 4])
        self.dtq = sb("s5dtq", [128, 4])
        a0 = self.wA[0].t[:].rearrange("p k n -> p (k n)")
        a1 = self.wA[1].t[:].rearrange("p k n -> p (k n)").bitcast(F32)
        self.Cblk = View(a0[:, 0:1024].rearrange("p (a b c) -> p a b c", a=2, b=16), self.wA[0], 0, 1024)
        self.gluwS = View(a0[:, 1024:2048].rearrange("p (a b) -> p a b", a=2), self.wA[0], 1024, 1024)
        self.Bblk = View(a0[:, 2048:2304].rearrange("p (a b) -> p a b", a=2), self.wA[0], 2048, 256)
        self.s5gy = [View(a0[:, 2304 + i * 512:2816 + i * 512], self.wA[0], 2304 + i * 512, 512) for i in range(2)]
        self.s5ub = [View(a0[:, 3328 + i * 256:3584 + i * 256], self.wA[0], 3328 + i * 256, 256) for i in range(2)]
        self.s5ob = View(a0[:, 3328:3840], self.wA[0], 3328, 512)
        self.s5w = [View(a1[:, i * 512:(i + 1) * 512], self.wA[1], i * 1024, 512, 2) for i in range(4)]
        self.Jb = sb("Jb", [128, 128], BF16)
        self.Ib = sb("Ib", [128, 128], BF16)
        self.rowm = sb("rowm", [128, 8])
        self.s5dS = sb("s5dS", [128, DEPTH, 2])
        self.glubS = sb("glubS", [128, DEPTH, 4])
        self.hprev = sb("hprev", [128, 2])
        hflat = self.hT.t[:].rearrange("p k t -> p (k t)")
        self.ytF = View(hflat[:, 0:4608], self.hT, 0, 4608)
        self.ytB = View(hflat[:, 4608:9216], self.hT, 4608, 4608)
        self.uTb = View(hflat[:, 9216:13824], self.hT, 9216, 4608)
        self.mbnext = View(hflat[:, 17920:18432], self.hT, 17920, 512)
        self.mbprev = Alias(self.adab.t[:].rearrange("p a b -> p (a b)").bitcast(BF16)[:, 0:512], self.adab)
        self.s5w2 = [View(hflat[:, 14848 + i * 1024:15872 + i * 1024].bitcast(F32), self.hT, 14848 + i * 1024, 512, 2) for i in range(3)] + [self.rstd]
        self.hpt = sb("s5hpt", [128, 2])
        self.BB = View(hflat[:, 13824:14848].bitcast(F32).rearrange("p (a b c) -> p a b c", a=2, b=4), self.hT, 13824, 512, 2)
        self.epsc = sb("epsc", [128, 1])
        self.onec = sb("onec", [128, 1])
        self.P = [Buf(nc.alloc_psum_tensor("ps%d" % i, [128, 512], F32), 512) for i in range(8)]

    def nxt(self, key, n):
        v = self.rot.get(key, 0)
        self.rot[key] = v + 1
        return v % n

    def prologue(self):
        fw = self.fw
        ld = lambda dst, src: fw.dma("sp", ("dma_start", dict(out=dst[:], in_=src[:])), reads=[src.r()], writes=[dst.r()])
        ld(self.ident, self.ident_d)
        ld(self.cv, self.cvec)
        ld(self.adab, self.ada_b)
        ld(self.ng, self.norm_g)
        ld(self.fngs, self.fng)
        fw.op("dve", ("memset", dict(ap=self.onesb[:], constant=1.0)), writes=[self.onesb.r()])
        fw.op("dve", ("memset", dict(ap=self.epsc[:], constant=EPS)), writes=[self.epsc.r()])
        fw.op("dve", ("memset", dict(ap=self.onec[:], constant=1.0)), writes=[self.onec.r()])

        ld(self.lbw, self.lb_d)
        ld(self.hng, self.hng_d)
        ld(self.RM, self.rm_d)
        fw.dma("pool", ("dma_start", dict(out=self.hmask[:], in_=self.hm_d[:])), reads=[self.hm_d.r()], writes=[self.hmask.r()])
        fw.dma("pool", ("dma_start", dict(out=self.identb[:], in_=self.ident_d[0:64, 0:64])), reads=[self.ident_d.r()], writes=[self.identb.r()])
        lbw, lbs, LB = self.lbw, self.lbs, self.LB
        fw.op("act", ("activation", dict(out=lbw[:], in_=lbw[:], func=AF.Exp)), reads=[lbw.r()], writes=[lbw.r()])
        fw.op("dve", ("tensor_tensor", dict(out=lbs[:], in0=lbw[:, 0, :], in1=lbw[:, 1, :], op=ALU.add)), reads=[lbw.r()], writes=[lbs.r()])
        fw.op("dve", ("tensor_tensor", dict(out=lbs[:], in0=lbs[:], in1=lbw[:, 2, :], op=ALU.add)), reads=[lbw.r(), lbs.r()], writes=[lbs.r()])
        fw.op("dve", ("tensor_tensor", dict(out=lbs[:], in0=lbs[:], in1=lbw[:, 3, :], op=ALU.add)), reads=[lbw.r(), lbs.r()], writes=[lbs.r()])
        fw.op("dve", ("reciprocal", dict(out=lbs[:], in_=lbs[:])), reads=[lbs.r()], writes=[lbs.r()])
        fw.op("dve", ("tensor_tensor", dict(out=lbw[:], in0=lbw[:], in1=lbs[:].unsqueeze(1).to_broadcast([64, DEPTH, 8]), op=ALU.mult)), reads=[lbw.r(), lbs.r()], writes=[lbw.r()])
        fw.op("dve", ("memset", dict(ap=LB[:, 0, :], constant=0.0)), writes=[LB.r()])
        fw.op("dve", ("tensor_copy", dict(out=LB[:, 1, :], in_=lbw[:, 1, :])), reads=[lbw.r()], writes=[LB.r()])
        fw.op("dve", ("tensor_tensor", dict(out=LB[:, 2, :], in0=LB[:, 1, :], in1=lbw[:, 2, :], op=ALU.add)), reads=[lbw.r(), LB.r()], writes=[LB.r()])
        fw.op("dve", ("tensor_tensor", dict(out=LB[:, 3, :], in0=LB[:, 2, :], in1=lbw[:, 3, :], op=ALU.add)), reads=[lbw.r(), LB.r()], writes=[LB.r()])
        fw.op("dve", ("tensor_scalar", dict(out=self.OML[:], in0=LB[:], scalar1=-1.0, scalar2=1.0, op0=ALU.mult, op1=ALU.add)), reads=[LB.r()], writes=[self.OML.r()])
        ld(self.ldq, self.s5LDq)
        ld(self.rowm, self.rowm_d)
        ld(self.s5dS, self.s5d_d)
        ld(self.glubS, self.glub_d)
        fw.dma("pool", ("dma_start", dict(out=self.Jb[:], in_=self.J_d[:])), reads=[self.J_d.r()], writes=[self.Jb.r()])
        fw.dma("pool", ("dma_start", dict(out=self.Ib[:], in_=self.ident_d[:])), reads=[self.ident_d.r()], writes=[self.Ib.r()])
        ld(self.esink, self.sink_d)
        fw.op("act", ("activation", dict(out=self.esink[:], in_=self.esink[:], func=AF.Exp)), reads=[self.esink.r()], writes=[self.esink.r()])
        xT, ident = self.xT, self.ident
        for blk in range(NTOK // 128):
            xi = self.xin[blk % 2]
            fw.dma("sp", ("dma_start", dict(out=xi[:], in_=self.x_in[blk * 128:(blk + 1) * 128, :])),
                   reads=[self.x_in.r()], writes=[xi.r()])
            for k4 in range(2):
                ps = self.P[self.nxt("pa", 4)]
                for kk in range(4):
                    k = k4 * 4 + kk
                    fw.op("pe", ("transpose", dict(out=ps[:, kk * 128:(kk + 1) * 128], in_=xi[:, k * 128:(k + 1) * 128], identity=ident[:])),
                          reads=[xi.r(k * 128, (k + 1) * 128), ident.r()], writes=[ps.r(kk * 128, (kk + 1) * 128)])
                fw.op("dve", ("tensor_copy", dict(
                    out=xT[:, k4 * 4:(k4 + 1) * 4, blk * 128:(blk + 1) * 128], in_=ps[:].rearrange("p (a b) -> p a b", b=128))),
                    reads=[ps.r()], writes=[xT.r((k4 * 4 + kk) * NTOK + blk * 128, (k4 * 4 + kk) * NTOK + (blk + 1) * 128) for kk in range(4)])
        fw.op("act", ("activation", dict(out=self.csb[:], in_=self.cv[:], func=AF.Silu)), reads=[self.cv.r()], writes=[self.csb.r()])
        csb, modr = self.csb, self.modr
        for l in range(DEPTH):
            awl = self.ada_w[l].rearrange("(k p) n -> p k n", p=128)
            for pc in range(18):
                wa = self.wA[self.nxt("wA", 2)]
                fw.dma("pool", ("dma_start", dict(out=wa[:], in_=awl[:, :, pc * 512:(pc + 1) * 512])),
                       reads=[self.ada_w.r()], writes=[wa.r()])
                ps = self.P[self.nxt("pa", 4)]
                for m4 in range(4):
                    for k in range(8):
                        fw.op("pe", ("matmul", dict(out=ps[:, m4 * 2:(m4 + 1) * 2], lhsT=wa[:, k, m4 * 128:(m4 + 1) * 128], rhs=csb[:, k, :], start=(k == 0), stop=(k == 7))),
                              reads=[wa.r(), csb.r()], writes=[ps.r(m4 * 2, m4 * 2 + 2)])
                lo = (l * 72 + pc * 4) * 2
                fw.op("dve", ("tensor_tensor", dict(
                    out=modr[:, l, pc * 4:(pc + 1) * 4, :], in0=ps[:, 0:8].rearrange("p (a b) -> p a b", b=2),
                    in1=self.adab[:, l, pc * 4:(pc + 1) * 4].unsqueeze(2).to_broadcast([128, 4, 2]), op=ALU.add)),
                    reads=[ps.r(0, 8), self.adab.r()], writes=[modr.r(lo, lo + 8)])
        GS, GT, ng = self.GS, self.GT, self.ng
        for l in range(DEPTH):
            for i in range(3):
                sc = modr[:, l, (i * 3 + 1) * 8:(i * 3 + 2) * 8, :]
                gt = modr[:, l, (i * 3 + 2) * 8:(i * 3 + 3) * 8, :]
                lo = ((l * 3 + i) * 8) * 2
                fw.op("dve", ("scalar_tensor_tensor", dict(
                    out=GS[:, l, i, :, :], in0=sc, scalar=1.0, in1=ng[:, l, i, :].unsqueeze(2).to_broadcast([128, 8, 2]), op0=ALU.add, op1=ALU.mult)),
                    reads=[modr.r(), ng.r()], writes=[GS.r(lo, lo + 16)])
                fw.op("dve", ("tensor_scalar", dict(out=GT[:, l, i, :, :], in0=gt, scalar1=(1.0 if i == 1 else 0.5), scalar2=None, op0=ALU.mult)),
                      reads=[modr.r()], writes=[GT.r(lo, lo + 16)])
        fw.dma("pool", ("dma_start", dict(out=self.mbprev[:, :], in_=self.mprev_d[:])), reads=[self.mprev_d.r()], writes=[self.mbprev.r()])

    def rms_stats(self, t0, n):
        fw, xT = self.fw, self.xT
        ps = self.P[self.nxt("pa", 4)]
        for k in range(8):
            sq = self.sq[self.nxt("sq", 2)]
            fw.op("act", ("activation", dict(out=sq[:, :n], in_=xT[:, k, t0:t0 + n], func=AF.Square)),
                  reads=[xT.r(k * NTOK + t0, k * NTOK + t0 + n)], writes=[sq.r(0, n)])
            fw.op("pe", ("matmul", dict(out=ps[:, :n], lhsT=self.onesb[:], rhs=sq[:, :n], start=(k == 0), stop=(k == 7))),
                  reads=[sq.r(0, n), self.onesb.r()], writes=[ps.r(0, n)])
        rstd = self.rstd
        fw.op("act", ("activation", dict(out=rstd[:, :n], in_=ps[:, :n], func=AF.Ln, bias=self.epsc[:, 0:1], scale=1.0 / DM)),
              reads=[ps.r(0, n), self.epsc.r()], writes=[rstd.r(0, n)])
        fw.op("act", ("activation", dict(out=rstd[:, :n], in_=rstd[:, :n], func=AF.Exp, scale=-0.5)), reads=[rstd.r(0, n)], writes=[rstd.r(0, n)])

    def norm_h(self, l, i, tiles):
        fw, xT, hT = self.fw, self.xT, self.hT
        for (t0, n) in tiles:
            var = 1 if t0 == 0 else 0
            self.rms_stats(t0, n)
            for k in range(8):
                tmp = self.tmp[self.nxt("tmp", 2)]
                fw.op("dve", ("scalar_tensor_tensor", dict(
                    out=tmp[:, :n], in0=xT[:, k, t0:t0 + n], scalar=self.GS[:, l, i, k, var:var + 1], in1=self.rstd[:, :n], op0=ALU.mult, op1=ALU.mult)),
                    reads=[xT.r(k * NTOK + t0, k * NTOK + t0 + n), self.GS.r(), self.rstd.r(0, n)], writes=[tmp.r(0, n)])
                fw.op("act", ("activation", dict(
                    out=hT[:, k, t0:t0 + n], in_=tmp[:, :n], func=AF.Identity, bias=self.modr[:, l, (i * 3) * 8 + k, var:var + 1], scale=1.0)),
                    reads=[tmp.r(0, n), self.modr.r()], writes=[hT.r(k * NTOK + t0, k * NTOK + t0 + n)])

    def ffn(self, l, i, tiles):
        fw, xT, hT = self.fw, self.xT, self.hT
        fi = 0 if i == 0 else 1
        w1v = self.w1[l, fi].rearrange("(k p) n -> p k n", p=128)

        def out_pair(prev, o):
            wb, hid, t0, n, var = prev
            po = self.P[4 + self.nxt("pb", 4)]
            for f in range(2):
                fw.op("pe", ("matmul", dict(out=po[:, :n], lhsT=wb[:, f, o * 128:(o + 1) * 128], rhs=hid[:, f, :n], start=(f == 0), stop=(f == 1))),
                      reads=[wb.r(f * 1024 + o * 128, f * 1024 + (o + 1) * 128), hid.r(f * 512, f * 512 + n)], writes=[po.r(0, n)])
            fw.op("dve", ("scalar_tensor_tensor", dict(
                out=xT[:, o, t0:t0 + n], in0=po[:, :n], scalar=self.GT[:, l, i, o, var:var + 1], in1=xT[:, o, t0:t0 + n], op0=ALU.mult, op1=ALU.add)),
                reads=[po.r(0, n), self.GT.r(), xT.r(o * NTOK + t0, o * NTOK + t0 + n)], writes=[xT.r(o * NTOK + t0, o * NTOK + t0 + n)])

        prev = None
        for g in range(11):
            wa = self.wA[self.nxt("wA", 2)]
            wb = self.wB[self.nxt("wB", 2)]
            fw.dma("pool", ("dma_start", dict(out=wa[:, :, 0:256], in_=w1v[:, :, g * 256:(g + 1) * 256])),
                   reads=[self.w1.r()], writes=[wa.r(kk * 512, kk * 512 + 256) for kk in range(8)])
            fw.dma("pool", ("dma_start", dict(out=wa[:, :, 256:512], in_=w1v[:, :, DFF + g * 256:DFF + (g + 1) * 256])),
                   reads=[self.w1.r()], writes=[wa.r(kk * 512 + 256, kk * 512 + 512) for kk in range(8)])
            fw.dma("pool", ("dma_start", dict(out=wb[:], in_=self.w2[l, fi, g * 256:(g + 1) * 256, :].rearrange("(f p) n -> p f n", p=128))),
                   reads=[self.w2.r()], writes=[wb.r()])
            for (t0, n) in tiles:
                var = 1 if t0 == 0 else 0
                hid = self.hid[self.nxt("hid", 2)]
                q = 0
                for f in range(2):
                    pg = self.P[self.nxt("pa", 4)]
                    pu = self.P[self.nxt("pa", 4)]
                    for (pp, c0, isg) in ((pg, f * 128, True), (pu, 256 + f * 128, False)):
                        for k in range(8):
                            fw.op("pe", ("matmul", dict(out=pp[:, :n], lhsT=wa[:, k, c0:c0 + 128], rhs=hT[:, k, t0:t0 + n], start=(k == 0), stop=(k == 7))),
                                  reads=[wa.r(k * 512 + c0, k * 512 + c0 + 128), hT.r(k * NTOK + t0, k * NTOK + t0 + n)], writes=[pp.r(0, n)])
                            if k % 4 == 3:
                                if prev is not None:
                                    out_pair(prev, q)
                                q += 1
                        if isg:
                            sg = self.sg[self.nxt("sg", 2)]
                            fw.op("act", ("activation", dict(out=sg[:, :n], in_=pg[:, :n], func=AF.Silu)), reads=[pg.r(0, n)], writes=[sg.r(0, n)])
                    fw.op("dve", ("tensor_tensor", dict(out=hid[:, f, :n], in0=sg[:, :n], in1=pu[:, :n], op=ALU.mult)),
                          reads=[sg.r(0, n), pu.r(0, n)], writes=[hid.r(f * 512, f * 512 + n)])
                prev = (wb, hid, t0, n, var)
        for o in range(8):
            out_pair(prev, o)

    def mm8(self, ps, n, wa, c0, t0, ncol=128):
        fw, hT = self.fw, self.hT
        for k in range(8):
            fw.op("pe", ("matmul", dict(out=ps[0:ncol, :n], lhsT=wa[:, k, c0:c0 + ncol], rhs=hT[:, k, t0:t0 + n], start=(k == 0), stop=(k == 7))),
                  reads=[wa.r(k * 512 + c0, k * 512 + c0 + ncol), hT.r(k * NTOK + t0, k * NTOK + t0 + n)], writes=[ps.r(0, n)])

    def inproj(self, l, kinds):
        fw = self.fw
        GR = [(0, 512, "A0"), (512, 512, "A1"), (1024, 512, "B0"), (1536, 512, "B1"), (2048, 512, "BK"), (2560, 384, "TV"), (2944, 256, "TU")]
        GR = [g_ for k_ in kinds for g_ in GR if g_[2] == k_]
        winl = self.win[l].rearrange("(k p) n -> p k n", p=128)
        for (c0, ncl, kind) in GR:
            wa = self.wA[self.nxt("wA", 2)]
            fw.dma("pool", ("dma_start", dict(out=wa[:, :, 0:ncl], in_=winl[:, :, c0:c0 + ncl])), reads=[self.win.r()], writes=[wa.r()])
            for ti, (t0, n) in enumerate(TILES):
                if kind in ("A0", "A1"):
                    for j in range(4):
                        ps = self.P[self.nxt("pa", 4)]
                        self.mm8(ps, n, wa, j * 128, t0)
                        tmp = self.tmp[self.nxt("tmp", 2)]
                        fw.op("act", ("activation", dict(out=tmp[:, :n], in_=ps[:, :n], func=AF.Identity)), reads=[ps.r(0, n)], writes=[tmp.r(0, n)])
                        qi = (0 if kind == "A0" else 2) + j // 2
                        r0 = (j % 2) * 128
                        fw.dma("sp", ("dma_start", dict(out=self.sA[qi, r0:r0 + 128, t0:t0 + n], in_=tmp[:, :n])), reads=[tmp.r(0, n)], writes=[self.sA.r(ti, ti + 1)])
                elif kind in ("B0", "B1", "BK"):
                    fw.dma("sp", ("dma_start", dict(out=self.rc[:, :n], in_=self.ropeC[:, t0:t0 + n])), reads=[self.ropeC.r()], writes=[self.rc.r(0, n)])
                    fw.dma("sp", ("dma_start", dict(out=self.rs[:, :n], in_=self.ropeS[:, t0:t0 + n])), reads=[self.ropeS.r()], writes=[self.rs.r(0, n)])
                    for j in range(2 if kind != "BK" else 1):
                        pq = self.P[self.nxt("pa", 4)]
                        pr = self.P[self.nxt("pa", 4)]
                        self.mm8(pq, n, wa, j * 256, t0)
                        self.mm8(pr, n, wa, j * 256 + 128, t0)
                        t1 = self.tmp[0]
                        t2 = self.tmp[1]
                        fw.op("dve", ("tensor_tensor", dict(out=t1[:, :n], in0=pq[:, :n], in1=self.rc[:, :n], op=ALU.mult)), reads=[pq.r(0, n), self.rc.r(0, n)], writes=[t1.r(0, n)])
                        fw.op("dve", ("tensor_tensor", dict(out=t2[:, :n], in0=pr[:, :n], in1=self.rs[:, :n], op=ALU.mult)), reads=[pr.r(0, n), self.rs.r(0, n)], writes=[t2.r(0, n)])
                        qb = self.ob[self.nxt("ob", 2)]
                        fw.op("dve", ("tensor_tensor", dict(out=qb[:, :n], in0=t1[:, :n], in1=t2[:, :n], op=ALU.add)), reads=[t1.r(0, n), t2.r(0, n)], writes=[qb.r(0, n)])
                        if kind == "BK":
                            fw.dma("sp", ("dma_start", dict(out=self.sK[:, t0:t0 + n], in_=qb[:, :n])), reads=[qb.r(0, n)], writes=[self.sK.r(ti, ti + 1)])
                        else:
                            ch = (0 if kind == "B0" else 2) + j
                            fw.dma("sp", ("dma_start", dict(out=self.sQ[ch * 128:(ch + 1) * 128, t0:t0 + n], in_=qb[:, :n])), reads=[qb.r(0, n)], writes=[self.sQ.r(ti, ti + 1)])
                    if kind == "BK":
                        for j in range(2):
                            ps = self.P[self.nxt("pa", 4)]
                            self.mm8(ps, n, wa, 256 + j * 128, t0)
                            tmp = self.tmp[self.nxt("tmp", 2)]
                            fw.op("act", ("activation", dict(out=tmp[:, :n], in_=ps[:, :n], func=AF.Identity)), reads=[ps.r(0, n)], writes=[tmp.r(0, n)])
                            fw.dma("sp", ("dma_start", dict(out=self.sU[j * 128:(j + 1) * 128, t0:t0 + n], in_=tmp[:, :n])), reads=[tmp.r(0, n)], writes=[self.sU.r(ti, ti + 1)])
                else:
                    hT = self.hT
                    for blk in range(n // 128):
                        tb = t0 + blk * 128
                        ps = self.P[self.nxt("pa", 4)]
                        for k in range(8):
                            fw.op("pe", ("matmul", dict(out=ps[:, :ncl], lhsT=hT[:, k, tb:tb + 128], rhs=wa[:, k, 0:ncl], start=(k == 0), stop=(k == 7))),
                                  reads=[wa.r(k * 512, k * 512 + ncl), hT.r(k * NTOK + tb, k * NTOK + tb + 128)], writes=[ps.r(0, ncl)])
                        ob = self.ob[self.nxt("ob", 2)]
                        fw.op("act", ("activation", dict(out=ob[:, :ncl], in_=ps[:, :ncl], func=AF.Identity)), reads=[ps.r(0, ncl)], writes=[ob.r(0, ncl)])
                        if kind == "TV":
                            fw.dma("sp", ("dma_start", dict(out=self.sAv[tb:tb + 128, :], in_=ob[:, 0:256])), reads=[ob.r(0, 256)], writes=[self.sAv.r(ti, ti + 1)])
                            fw.dma("sp", ("dma_start", dict(out=self.sV[tb:tb + 128, :], in_=ob[:, 256:384])), reads=[ob.r(256, 384)], writes=[self.sV.r(ti, ti + 1)])
                        else:
                            fw.dma("sp", ("dma_start", dict(out=self.sUt[tb:tb + 128, :], in_=ob[:, 0:256])), reads=[ob.r(0, 256)], writes=[self.sUt.r(ti, ti + 1)])
                yield


    def hgrn(self, l):
        fw = self.fw
        W1, W2, W3, W4 = self.HW
        W5, Qb = self.xin[0], self.xin[1]
        QE, KE, Vt, S, T1, SP, KT, SC = self.HQE, self.HKE, self.HV, self.HS, self.HT1, self.HSP, self.HKT, self.HSC
        M1, EM, ET, E2 = self.HM1, self.HEM, self.HET, self.HE2

        def v3(b):
            return b[0:64, :].rearrange("p (a j) -> p a j", j=32)

        def o(q, name, reads, writes, **kw):
            fw.op(q, (name, kw), reads=reads, writes=writes)

        for dirn in range(2):
            o("dve", "memset", [], [S.r()], ap=S[:, :], constant=0.0)
            segs = list(range(9)) if dirn == 0 else [0] + list(range(8, 0, -1))
            lbb = self.LB[:, l, dirn * 4:(dirn + 1) * 4].unsqueeze(2).to_broadcast([64, 4, 256])
            omb = self.OML[:, l, dirn * 4:(dirn + 1) * 4].unsqueeze(2).to_broadcast([64, 4, 256])
            for sgi in segs:
                t0 = sgi * 256
                ti = 0 if sgi == 0 else 1 + (sgi - 1) // 2
                fw.dma("sp", ("dma_start", dict(out=W1[:, :].rearrange("p (h t) -> p h t", h=4), in_=self.sA[1 + dirn, :, t0:t0 + 256].rearrange("(h d) t -> d h t", d=64))),
                       reads=[self.sA.r()], writes=[W1.r()])
                fw.dma("sp", ("dma_start", dict(out=Qb[0:64, :].rearrange("p (h t) -> p h t", h=4), in_=self.sA[0, :, t0:t0 + 256].rearrange("(h d) t -> d h t", d=64))),
                       reads=[self.sA.r()], writes=[Qb.r()])
                fw.dma("sp", ("dma_start", dict(out=Vt[:, :, :], in_=self.sAv[t0:t0 + 256, :].rearrange("(c s) v -> s c v", s=32))),
                       reads=[self.sAv.r()], writes=[Vt.r()])
                w1h = W1[:, :].rearrange("p (h t) -> p h t", h=4)
                o("act", "activation", [W1.r()], [W1.r()], out=W1[:, :], in_=W1[:, :], func=AF.Sigmoid)
                for h in range(4):
                    o("act", "activation", [W1.r(), self.OML.r(), self.LB.r()], [W1.r()], out=W1[:, h * 256:(h + 1) * 256], in_=W1[:, h * 256:(h + 1) * 256], func=AF.Identity,
                      scale=self.OML[:, l, dirn * 4 + h:dirn * 4 + h + 1], bias=self.LB[:, l, dirn * 4 + h:dirn * 4 + h + 1])
                o("act", "activation", [W1.r()], [W2.r()], out=W2[:, :], in_=W1[:, :], func=AF.Ln)
                o("act", "activation", [W1.r(), self.onec.r()], [W3.r()], out=W3[:, :], in_=W1[:, :], func=AF.Identity, scale=-1.0, bias=self.onec[0:64, 0:1])
                o("dve", "tensor_tensor_scan", [self.RM.r(), W2.r()], [W4.r()], out=W4[:, :], data0=self.RM[:, :], data1=W2[:, :], initial=0.0, op0=ALU.mult, op1=ALU.add)
                if dirn == 0:
                    o("dve", "tensor_copy", [W4.r()], [M1.r()], out=M1[:, :].unsqueeze(2), in_=v3(W4)[:, :, 15:16])
                    o("dve", "scalar_tensor_tensor", [M1.r(), W4.r()], [W5.r()], out=v3(W5), in0=v3(W4), scalar=1.0, in1=M1[:, :].unsqueeze(2).to_broadcast([64, 32, 32]), op0=ALU.mult, op1=ALU.subtract)
                    o("act", "activation", [M1.r()], [EM.r()], out=EM[:, :], in_=M1[:, :], func=AF.Exp)
                    o("act", "activation", [W4.r()], [ET.r()], out=ET[:, :].unsqueeze(2), in_=v3(W4)[:, :, 31:32], func=AF.Exp)
                    o("act", "activation", [W5.r()], [E2.r()], out=E2[:, :].unsqueeze(2), in_=v3(W5)[:, :, 31:32], func=AF.Exp)
                else:
                    o("dve", "tensor_tensor", [W4.r(), W2.r()], [W2.r()], out=W2[:, :], in0=W4[:, :], in1=W2[:, :], op=ALU.subtract)
                    o("dve", "tensor_copy", [W2.r()], [M1.r()], out=M1[:, :].unsqueeze(2), in_=v3(W2)[:, :, 16:17])
                    o("dve", "scalar_tensor_tensor", [M1.r(), W2.r()], [W5.r()], out=v3(W5), in0=v3(W2), scalar=-1.0, in1=M1[:, :].unsqueeze(2).to_broadcast([64, 32, 32]), op0=ALU.mult, op1=ALU.add)
                    o("act", "activation", [M1.r()], [E2.r()], out=E2[:, :], in_=M1[:, :], func=AF.Exp)
                    o("act", "activation", [W4.r()], [ET.r()], out=ET[:, :].unsqueeze(2), in_=v3(W4)[:, :, 31:32], func=AF.Exp)
                    o("dve", "tensor_tensor", [W4.r(), M1.r()], [EM.r()], out=EM[:, :].unsqueeze(2), in0=v3(W4)[:, :, 31:32], in1=M1[:, :].unsqueeze(2), op=ALU.subtract)
                    o("act", "activation", [EM.r()], [EM.r()], out=EM[:, :], in_=EM[:, :], func=AF.Exp)
                o("act", "activation", [W5.r()], [W1.r()], out=W1[:, :], in_=W5[0:64, :], func=AF.Exp)
                o("act", "activation", [W5.r()], [W2.r()], out=W2[:, :], in_=W5[0:64, :], func=AF.Exp, scale=-1.0)
                o("act", "activation", [Qb.r()], [Qb.r()], out=Qb[0:64, :], in_=Qb[0:64, :], func=AF.Silu)
                o("dve", "tensor_tensor", [Qb.r(), W1.r()], [QE.r()], out=QE[:, :], in0=Qb[0:64, :], in1=W1[:, :], op=ALU.mult)
                o("dve", "tensor_tensor", [W3.r(), W2.r()], [KE.r()], out=KE[:, :], in0=W3[:, :], in1=W2[:, :], op=ALU.mult)
                yield
                OF = W4
                if dirn == 1:
                    fw.dma("sp", ("dma_start", dict(out=OF[:, :].rearrange("p (h t) -> p h t", h=4), in_=self.sO[:, t0:t0 + 256].rearrange("(h d) t -> d h t", d=64))),
                           reads=[self.sO.r()], writes=[OF.r()])
                corder = list(range(8)) if dirn == 0 else list(range(7, -1, -1))
                OB = W5 if dirn == 1 else None
                bufs = {}
                for it in range(10):
                    cA = corder[it] if it < 8 else None
                    cU = corder[it - 1] if 1 <= it <= 8 else None
                    cB = corder[it - 2] if it >= 2 else None
                    if cA is not None:
                        par3 = self.nxt("hpar3", 3)
                        KTa, SCa = self.HKT3[par3], self.HSC3[par3]
                        bufs[cA] = (KTa, SCa)
                        psK = self.P[self.nxt("pa", 4)]
                        psS = self.P[self.nxt("pa", 4)]
                        for h in range(4):
                            cb = h * 256 + cA * 32
                            o("pe", "matmul", [KE.r(cb, cb + 32), self.identb.r()], [psK.r(h * 64, h * 64 + 64)], out=psK[0:32, h * 64:(h + 1) * 64], lhsT=KE[:, cb:cb + 32], rhs=self.identb[:, :], start=True, stop=True)
                            o("pe", "matmul", [KE.r(cb, cb + 32), QE.r(cb, cb + 32)], [psS.r(h * 32, h * 32 + 32)], out=psS[0:32, h * 32:(h + 1) * 32], lhsT=KE[:, cb:cb + 32], rhs=QE[:, cb:cb + 32], start=True, stop=True)
                    if cU is not None:
                        KTu = bufs[cU][0]
                        psU = self.P[4 + self.nxt("pb", 4)]
                        for h in range(4):
                            o("pe", "matmul", [KTu.r(), Vt.r()], [psU.r(h * 64, h * 64 + 64)], out=psU[0:64, h * 64:(h + 1) * 64], lhsT=KTu[0:32, h * 64:(h + 1) * 64], rhs=Vt[0:32, cU, h * 64:(h + 1) * 64], start=True, stop=True)
                    if cA is not None:
                        o("act", "activation", [psK.r(0, 256)], [KTa.r()], out=KTa[0:32, :], in_=psK[0:32, 0:256], func=AF.Identity)
                    if cB is not None:
                        c = cB
                        SCb = bufs.pop(c)[1]
                        emb = EM[:, :].rearrange("p (h c) -> p h c", c=8)[:, :, c:c + 1].to_broadcast([64, 4, 64])
                        etb = ET[:, :].rearrange("p (h c) -> p h c", c=8)[:, :, c:c + 1].to_broadcast([64, 4, 64])
                        s3 = S[:, :].rearrange("p (h v) -> p h v", h=4)
                        o("dve", "tensor_tensor", [S.r(), EM.r()], [SP.r()], out=SP[:, :].rearrange("p (h v) -> p h v", h=4), in0=s3, in1=emb, op=ALU.mult)
                        psO = self.P[4 + self.nxt("pb", 4)]
                        for h in range(4):
                            cb = h * 256 + c * 32
                            o("pe", "matmul", [Vt.r(), SCb.r()], [psO.r(h * 32, h * 32 + 32)], out=psO[0:64, h * 32:(h + 1) * 32], lhsT=Vt[0:32, c, h * 64:(h + 1) * 64], rhs=SCb[0:32, h * 32:(h + 1) * 32], start=True, stop=False)
                            o("pe", "matmul", [SP.r(), QE.r(cb, cb + 32)], [psO.r(h * 32, h * 32 + 32)], out=psO[0:64, h * 32:(h + 1) * 32], lhsT=SP[:, h * 64:(h + 1) * 64], rhs=QE[:, cb:cb + 32], start=False, stop=True)
                        dst = OF if dirn == 0 else OB
                        ofv = dst[0:64, :].rearrange("p (h t) -> p h t", h=4)[:, :, c * 32:(c + 1) * 32]
                        pov = psO[0:64, 0:128].rearrange("p (h t) -> p h t", h=4)
                        o("act", "activation", [psO.r(0, 128)], [dst.r()], out=ofv, in_=pov, func=AF.Identity)
                        o("dve", "tensor_tensor", [S.r(), ET.r()], [S.r()], out=s3, in0=s3, in1=etb, op=ALU.mult)
                        o("dve", "tensor_tensor", [S.r(), T1.r()], [S.r()], out=S[:, :], in0=S[:, :], in1=T1[:, :], op=ALU.add)
                    if cA is not None:
                        o("dve", "tensor_tensor", [psS.r(0, 128), self.hmask.r()], [SCa.r()], out=SCa[0:32, 0:128].rearrange("p (h t) -> p h t", h=4), in0=psS[0:32, 0:128].rearrange("p (h t) -> p h t", h=4),
                          in1=self.hmask[0:32, dirn, 0:32].unsqueeze(1).to_broadcast([32, 4, 32]), op=ALU.mult)
                    if cU is not None:
                        e2b = E2[:, :].rearrange("p (h c) -> p h c", c=8)[:, :, cU:cU + 1].to_broadcast([64, 4, 64])
                        o("dve", "tensor_tensor", [psU.r(0, 256), E2.r()], [T1.r()], out=T1[:, :].rearrange("p (h v) -> p h v", h=4), in0=psU[0:64, 0:256].rearrange("p (h v) -> p h v", h=4), in1=e2b, op=ALU.mult)
                    yield
                if dirn == 1:
                    o("dve", "tensor_tensor", [OF.r(), OB.r()], [OF.r()], out=OF[:, :], in0=OF[:, :], in1=OB[0:64, :], op=ALU.add)
                if dirn == 0:
                    fw.dma("sp", ("dma_start", dict(out=self.sO[:, t0:t0 + 256].rearrange("(h d) t -> d h t", d=64), in_=OF[:, :].rearrange("p (h t) -> p h t", h=4))),
                           reads=[OF.r()], writes=[self.sO.r(ti, ti + 1)])
                else:
                    o("act", "activation", [OF.r()], [QE.r()], out=QE[:, :], in_=OF[:, :], func=AF.Square)
                    for hf in range(2):
                        psR = self.P[self.nxt("pa", 4)]
                        o("pe", "matmul", [self.onesb.r(), QE.r()], [psR.r()], out=psR[0:64, :], lhsT=self.onesb[0:64, 0:64], rhs=QE[:, hf * 512:(hf + 1) * 512], start=True, stop=True)
                        o("act", "activation", [psR.r(), self.epsc.r()], [W1.r(hf * 512, hf * 512 + 512)], out=W1[:, hf * 512:(hf + 1) * 512], in_=psR[0:64, :], func=AF.Ln, bias=self.epsc[0:64, 0:1], scale=1.0 / 64)
                    o("act", "activation", [W1.r()], [W1.r()], out=W1[:, :], in_=W1[:, :], func=AF.Exp, scale=-0.5)
                    o("dve", "tensor_tensor", [OF.r(), W1.r()], [OF.r()], out=OF[:, :], in0=OF[:, :], in1=W1[:, :], op=ALU.mult)
                    ofh = OF[:, :].rearrange("p (h t) -> p h t", h=4)
                    o("dve", "tensor_tensor", [OF.r(), self.hng.r()], [OF.r()], out=ofh, in0=ofh, in1=self.hng[:, l, :].unsqueeze(2).to_broadcast([64, 4, 256]), op=ALU.mult)
                    fw.dma("sp", ("dma_start", dict(out=W3[:, :].rearrange("p (h t) -> p h t", h=4), in_=self.sA[3, :, t0:t0 + 256].rearrange("(h d) t -> d h t", d=64))),
                           reads=[self.sA.r()], writes=[W3.r()])
                    o("act", "activation", [W3.r()], [W3.r()], out=W3[:, :], in_=W3[:, :], func=AF.Silu)
                    o("dve", "tensor_tensor", [OF.r(), W3.r()], [KE.r()], out=KE[:, :], in0=OF[:, :], in1=W3[:, :], op=ALU.mult)
                    fw.dma("sp", ("dma_start", dict(out=self.mix[0:256, t0:t0 + 256].rearrange("(h d) t -> d h t", d=64), in_=KE[:, :].rearrange("p (h t) -> p h t", h=4))),
                           reads=[KE.r()], writes=[self.mix.r(ti, ti + 1)])
                yield


    def s5(self, l):
        fw = self.fw
        PI = float(np.pi)

        def o(q, name, reads, writes, **kw):
            fw.op(q, (name, kw), reads=reads, writes=writes)

        class Sl:
            def __init__(s_, buf, lo, n=256):
                s_.buf, s_.lo, s_.n = buf, lo, n
                s_.ap = buf[:, lo:lo + n]
                s_.rg = buf.r(lo, lo + n)

            def v4(s_):
                return s_.ap.rearrange("p (a b) -> p a b", b=64)

        pool = []
        for b, n in ((self.s5w[0], 512), (self.s5w[1], 512), (self.s5w[2], 512), (self.s5w[3], 512),
                     (self.rc, 512), (self.rs, 512), (self.rstd, 512)):
            for lo in range(0, n, 256):
                pool.append(Sl(b, lo))
        AR, AI, BQ, AD, ANG, Y, CN, T, U, RDEN, CR, CI, T2, MAG = pool[:14]

        def tt(out, a, b_, op, extra_r=()):
            o("dve", "tensor_tensor", [a.rg, b_.rg] + list(extra_r), [out.rg], out=out.ap, in0=a.ap, in1=b_.ap, op=op)

        def reduce_angle(xap, xregs, shift, out_ap, out_regs, y_ap, y_regs, c_ap, c_regs, t_ap, t_regs):
            o("dve", "tensor_scalar", xregs, y_regs, out=y_ap, in0=xap, scalar1=shift, scalar2=None, op0=ALU.add)
            o("dve", "tensor_scalar", y_regs, c_regs, out=c_ap, in0=y_ap, scalar1=PI, scalar2=None, op0=ALU.is_gt)
            for thr in (3 * PI, 5 * PI, 7 * PI):
                o("dve", "tensor_scalar", y_regs, t_regs, out=t_ap, in0=y_ap, scalar1=thr, scalar2=None, op0=ALU.is_gt)
                o("dve", "tensor_tensor", c_regs + t_regs, c_regs, out=c_ap, in0=c_ap, in1=t_ap, op=ALU.add)
            o("dve", "scalar_tensor_tensor", c_regs + y_regs, out_regs, out=out_ap, in0=c_ap, scalar=-2.0 * PI, in1=y_ap, op0=ALU.mult, op1=ALU.add)

        fw.dma("sp", ("dma_start", dict(out=AR.v4(), in_=self.s5Q[:, l, 0])), reads=[self.s5Q.r()], writes=[AR.rg])
        fw.dma("sp", ("dma_start", dict(out=AI.v4(), in_=self.s5Q[:, l, 1])), reads=[self.s5Q.r()], writes=[AI.rg])
        fw.dma("sp", ("dma_start", dict(out=BQ.v4(), in_=self.s5B[:, l].rearrange("p a h d -> p (a h) d"))), reads=[self.s5B.r()], writes=[BQ.rg])
        fw.dma("sp", ("dma_start", dict(out=self.sAs[:], in_=self.s5A[:, l])), reads=[self.s5A.r()], writes=[self.sAs.r()])
        fw.dma("pool", ("dma_start", dict(out=self.Cblk[:], in_=self.s5C[:, l])), reads=[self.s5C.r()], writes=[self.Cblk.r()])
        fw.dma("pool", ("dma_start", dict(out=self.gluwS[:], in_=self.gluw_d[l].rearrange("(h p) n -> p h n", p=128))), reads=[self.gluw_d.r()], writes=[self.gluwS.r()])
        o("dve", "tensor_scalar", [self.Cblk.r()], [self.Cblk.r()], out=self.Cblk[:, 1], in0=self.Cblk[:, 1], scalar1=-1.0, scalar2=None, op0=ALU.mult)

        dtq = self.dtq
        o("act", "activation", [self.ldq.r()], [dtq.r()], out=dtq[:, :], in_=self.ldq[:, l, :], func=AF.Exp)
        dtb = dtq[:, :].unsqueeze(2).to_broadcast([128, 4, 64])
        o("dve", "tensor_tensor", [AR.rg, dtq.r()], [AD.rg], out=AD.v4(), in0=AR.v4(), in1=dtb, op=ALU.mult)
        o("act", "activation", [AD.rg], [MAG.rg], out=MAG.ap, in_=AD.ap, func=AF.Exp)
        o("dve", "tensor_tensor", [AI.rg, dtq.r()], [ANG.rg], out=ANG.v4(), in0=AI.v4(), in1=dtb, op=ALU.mult)
        SN, CS = AD, U
        reduce_angle(ANG.ap, [ANG.rg], 0.0, SN.ap, [SN.rg], Y.ap, [Y.rg], CN.ap, [CN.rg], T.ap, [T.rg])
        o("act", "activation", [SN.rg], [SN.rg], out=SN.ap, in_=SN.ap, func=AF.Sin)
        reduce_angle(ANG.ap, [ANG.rg], PI / 2, CS.ap, [CS.rg], Y.ap, [Y.rg], CN.ap, [CN.rg], T.ap, [T.rg])
        o("act", "activation", [CS.rg], [CS.rg], out=CS.ap, in_=CS.ap, func=AF.Sin)
        ABR, ABI = CS, SN
        tt(ABR, MAG, CS, ALU.mult)
        tt(ABI, MAG, SN, ALU.mult)
        tt(T, AR, AR, ALU.mult)
        tt(Y, AI, AI, ALU.mult)
        tt(T, T, Y, ALU.add)
        o("dve", "reciprocal", [T.rg], [RDEN.rg], out=RDEN.ap, in_=T.ap)
        o("dve", "tensor_scalar", [ABR.rg], [ABR.rg], out=ABR.ap, in0=ABR.ap, scalar1=-1.0, scalar2=None, op0=ALU.add)
        tt(T, ABR, AR, ALU.mult)
        tt(Y, ABI, AI, ALU.mult)
        tt(T, T, Y, ALU.add)
        tt(CR, T, RDEN, ALU.mult)
        tt(T2, ABI, AR, ALU.mult)
        tt(Y, ABR, AI, ALU.mult)
        tt(T2, T2, Y, ALU.subtract)
        tt(CI, T2, RDEN, ALU.mult)
        BB = self.BB
        cr3 = CR.ap.rearrange("p (d x) -> p d x", d=2)
        ci3 = CI.ap.rearrange("p (d x) -> p d x", d=2)
        bre = BQ.ap[:, 0:128].unsqueeze(1).to_broadcast([128, 2, 128])
        bim = BQ.ap[:, 128:256].unsqueeze(1).to_broadcast([128, 2, 128])
        t3 = T.ap.rearrange("p (d x) -> p d x", d=2)
        y3 = Y.ap.rearrange("p (d x) -> p d x", d=2)
        bbr = BB[:, 0].rearrange("p a b -> p (a b)").rearrange("p (d x) -> p d x", d=2)
        bbi = BB[:, 1].rearrange("p a b -> p (a b)").rearrange("p (d x) -> p d x", d=2)
        o("dve", "tensor_tensor", [CR.rg, BQ.rg], [T.rg], out=t3, in0=cr3, in1=bre, op=ALU.mult)
        o("dve", "tensor_tensor", [CI.rg, BQ.rg], [Y.rg], out=y3, in0=ci3, in1=bim, op=ALU.mult)
        o("dve", "tensor_tensor", [T.rg, Y.rg], [BB.r(0, 256)], out=bbr, in0=t3, in1=y3, op=ALU.subtract)
        o("dve", "tensor_tensor", [CR.rg, BQ.rg], [T.rg], out=t3, in0=cr3, in1=bim, op=ALU.mult)
        o("dve", "tensor_tensor", [CI.rg, BQ.rg], [Y.rg], out=y3, in0=ci3, in1=bre, op=ALU.mult)
        o("dve", "tensor_tensor", [T.rg, Y.rg], [BB.r(256, 512)], out=bbi, in0=t3, in1=y3, op=ALU.add)

        As, DT, R, TH, C1, S1, Y16, N16, T16 = self.sAs, self.sDT, self.sR, self.sTH, self.sC1, self.sS1, self.sY16, self.sN16, self.sT16
        o("act", "activation", [As.r()], [DT.r()], out=DT[:, :], in_=As[:, 2, :], func=AF.Exp)
        o("dve", "tensor_tensor", [As.r(), DT.r()], [TH.r()], out=TH[:, :], in0=As[:, 0, :], in1=DT[:, :], op=ALU.mult)
        o("act", "activation", [TH.r()], [R.r()], out=R[:, :], in_=TH[:, :], func=AF.Exp)
        o("dve", "tensor_tensor", [As.r(), DT.r()], [TH.r()], out=TH[:, :], in0=As[:, 1, :], in1=DT[:, :], op=ALU.mult)
        reduce_angle(TH[:, :], [TH.r()], 0.0, S1[:, :], [S1.r()], Y16[:, :], [Y16.r()], N16[:, :], [N16.r()], T16[:, :], [T16.r()])
        o("act", "activation", [S1.r()], [S1.r()], out=S1[:, :], in_=S1[:, :], func=AF.Sin)
        reduce_angle(TH[:, :], [TH.r()], PI / 2, C1[:, :], [C1.r()], Y16[:, :], [Y16.r()], N16[:, :], [N16.r()], T16[:, :], [T16.r()])
        o("act", "activation", [C1.r()], [C1.r()], out=C1[:, :], in_=C1[:, :], func=AF.Sin)

        uTb, ytF, ytB = self.uTb, self.ytF, self.ytB
        u3 = uTb[:, :].rearrange("p (h t) -> p h t", h=2)
        fw.dma("pool", ("dma_start", dict(out=u3, in_=self.sU[:, :].rearrange("(h p) t -> p h t", p=128))), reads=[self.sU.r()], writes=[uTb.r()])
        Ct, St = self.rc, self.rs
        wsets = [self.s5w, self.s5w2]
        hsets = [(self.sq[0], self.sq[1]), (self.sq2[0], self.sq2[1])]
        tcnt = 0
        pend = None
        yield
        hp = self.hprev
        for dirn in range(2):
            yt = ytF if dirn == 0 else ytB
            yt3 = yt[:, :].rearrange("p (b c) -> p b c", c=256)
            if dirn == 1:
                for bt in range(18):
                    tb = (1 - bt) if bt < 2 else (19 - bt)
                    ub = self.s5ub[self.nxt("s5ub", 2)]
                    fw.dma("sp", ("dma_start", dict(out=ub[:, 0:256], in_=self.sUt[tb * 128:(tb + 1) * 128, :])), reads=[self.sUt.r()], writes=[ub.r(0, 256)])
                    ps = self.P[self.nxt("pa", 4)]
                    for half in range(2):
                        o("pe", "matmul", [ub.r(0, 256), self.Jb.r()], [ps.r(half * 128, half * 128 + 128)], out=ps[:, half * 128:(half + 1) * 128],
                          lhsT=ub[:, half * 128:(half + 1) * 128], rhs=self.Jb[:, :], start=True, stop=True)
                    o("act", "activation", [ps.r(0, 256)], [uTb.r(bt * 128, bt * 128 + 128), uTb.r(2304 + bt * 128, 2304 + bt * 128 + 128)],
                      out=u3[:, :, bt * 128:(bt + 1) * 128], in_=ps[:, 0:256].rearrange("p (h t) -> p h t", h=2), func=AF.Identity)
                    if bt % 3 == 2:
                        yield
            for st in range(8):
                ds = dirn * 8 + st
                half = st // 4
                hd = dirn * 2 + half
                o("act", "activation", [C1.r()], [Ct.r(0, 1)], out=Ct[:, 0:1], in_=C1[:, ds:ds + 1], func=AF.Identity)
                o("act", "activation", [S1.r()], [St.r(0, 1)], out=St[:, 0:1], in_=S1[:, ds:ds + 1], func=AF.Identity)
                gr, gi = wsets[0][2], wsets[0][3]
                m = 1
                while m < 512:
                    cm, sm = Ct[:, m - 1:m], St[:, m - 1:m]
                    o("dve", "tensor_scalar", [St.r(0, m)], [gr.r(0, m)], out=gr[:, 0:m], in0=St[:, 0:m], scalar1=sm, scalar2=None, op0=ALU.mult)
                    o("dve", "scalar_tensor_tensor", [Ct.r(0, m), gr.r(0, m)], [Ct.r(m, 2 * m)], out=Ct[:, m:2 * m], in0=Ct[:, 0:m], scalar=cm, in1=gr[:, 0:m], op0=ALU.mult, op1=ALU.subtract)
                    o("dve", "tensor_scalar", [Ct.r(0, m), St.r(0, m)], [gi.r(0, m)], out=gi[:, 0:m], in0=Ct[:, 0:m], scalar1=sm, scalar2=None, op0=ALU.mult)
                    o("dve", "scalar_tensor_tensor", [St.r(0, m), Ct.r(0, m), gi.r(0, m)], [St.r(m, 2 * m)], out=St[:, m:2 * m], in0=St[:, 0:m], scalar=cm, in1=gi[:, 0:m], op0=ALU.mult, op1=ALU.add)
                    m *= 2
                yield
                for ri in range(2):
                    for gl in range(2):
                        j = 2 * (st % 4) + gl
                        o("dve", "tensor_scalar", [BB.r(), self.rowm.r()], [self.Bblk.r(ri * 128 + gl * 64, ri * 128 + gl * 64 + 64)],
                          out=self.Bblk[:, ri, gl * 64:(gl + 1) * 64], in0=BB[:, ri, hd, :], scalar1=self.rowm[:, j:j + 1], scalar2=None, op0=ALU.mult)
                rb = R[:, ds:ds + 1]
                for ti, (t0, n) in enumerate(TILES):
                    dr, di, gr, gi = wsets[tcnt % 2]
                    hr, hi = hsets[tcnt % 2]
                    tcnt += 1
                    pr = self.P[self.nxt("pa", 4)]
                    pi_ = self.P[self.nxt("pa", 4)]
                    c0 = half * NTOK + t0
                    o("pe", "matmul", [self.Bblk.r(0, 128), uTb.r(c0, c0 + n)], [pr.r(0, n)], out=pr[:, :n], lhsT=self.Bblk[:, 0, :], rhs=uTb[:, c0:c0 + n], start=True, stop=True)
                    o("pe", "matmul", [self.Bblk.r(128, 256), uTb.r(c0, c0 + n)], [pi_.r(0, n)], out=pi_[:, :n], lhsT=self.Bblk[:, 1, :], rhs=uTb[:, c0:c0 + n], start=True, stop=True)
                    o("dve", "tensor_tensor", [pi_.r(0, n), St.r(0, n)], [gr.r(0, n)], out=gr[:, :n], in0=pi_[:, :n], in1=St[:, :n], op=ALU.mult)
                    o("dve", "tensor_tensor", [pr.r(0, n), Ct.r(0, n)], [dr.r(0, n)], out=dr[:, :n], in0=pr[:, :n], in1=Ct[:, :n], op=ALU.mult)
                    o("dve", "tensor_tensor", [dr.r(0, n), gr.r(0, n)], [dr.r(0, n)], out=dr[:, :n], in0=dr[:, :n], in1=gr[:, :n], op=ALU.add)
                    o("dve", "tensor_tensor", [pr.r(0, n), St.r(0, n)], [gi.r(0, n)], out=gi[:, :n], in0=pr[:, :n], in1=St[:, :n], op=ALU.mult)
                    o("dve", "tensor_tensor", [pi_.r(0, n), Ct.r(0, n)], [di.r(0, n)], out=di[:, :n], in0=pi_[:, :n], in1=Ct[:, :n], op=ALU.mult)
                    o("dve", "tensor_tensor", [di.r(0, n), gi.r(0, n)], [di.r(0, n)], out=di[:, :n], in0=di[:, :n], in1=gi[:, :n], op=ALU.subtract)
                    ini_r = 0.0 if ti == 0 else hp[:, 0:1]
                    ini_i = 0.0 if ti == 0 else hp[:, 1:2]
                    o("dve", "tensor_tensor_scan", [R.r(), dr.r(0, n), hp.r()], [gr.r(0, n)], out=gr[:, :n], data0=rb.to_broadcast([128, n]), data1=dr[:, :n], initial=ini_r, op0=ALU.mult, op1=ALU.add)
                    o("dve", "tensor_tensor_scan", [R.r(), di.r(0, n), hp.r()], [gi.r(0, n)], out=gi[:, :n], data0=rb.to_broadcast([128, n]), data1=di[:, :n], initial=ini_i, op0=ALU.mult, op1=ALU.add)
                    if ti < len(TILES) - 1:
                        hpt = self.hpt
                        cl, sl = Ct[:, n - 1:n], St[:, n - 1:n]
                        o("dve", "tensor_scalar", [gi.r(0, n), St.r(0, n)], [hpt.r(0, 1)], out=hpt[:, 0:1], in0=gi[:, n - 1:n], scalar1=sl, scalar2=None, op0=ALU.mult)
                        o("dve", "tensor_scalar", [gr.r(0, n), St.r(0, n)], [hpt.r(1, 2)], out=hpt[:, 1:2], in0=gr[:, n - 1:n], scalar1=sl, scalar2=None, op0=ALU.mult)
                        o("dve", "scalar_tensor_tensor", [gr.r(0, n), Ct.r(0, n), hpt.r(0, 1)], [hp.r(0, 1)], out=hp[:, 0:1], in0=gr[:, n - 1:n], scalar=cl, in1=hpt[:, 0:1], op0=ALU.mult, op1=ALU.subtract)
                        o("dve", "scalar_tensor_tensor", [gi.r(0, n), Ct.r(0, n), hpt.r(1, 2)], [hp.r(1, 2)], out=hp[:, 1:2], in0=gi[:, n - 1:n], scalar=cl, in1=hpt[:, 1:2], op0=ALU.mult, op1=ALU.add)
                    o("pool", "tensor_tensor", [gi.r(0, n), St.r(0, n)], [dr.r(0, n)], out=dr[:, :n], in0=gi[:, :n], in1=St[:, :n], op=ALU.mult)
                    o("pool", "tensor_tensor", [gr.r(0, n), St.r(0, n)], [di.r(0, n)], out=di[:, :n], in0=gr[:, :n], in1=St[:, :n], op=ALU.mult)
                    o("pool", "tensor_tensor", [gr.r(0, n), Ct.r(0, n)], [gr.r(0, n)], out=gr[:, :n], in0=gr[:, :n], in1=Ct[:, :n], op=ALU.mult)
                    o("pool", "tensor_tensor", [gi.r(0, n), Ct.r(0, n)], [gi.r(0, n)], out=gi[:, :n], in0=gi[:, :n], in1=Ct[:, :n], op=ALU.mult)
                    o("pool", "tensor_tensor", [gr.r(0, n), dr.r(0, n)], [hr.r(0, n)], out=hr[:, :n], in0=gr[:, :n], in1=dr[:, :n], op=ALU.subtract)
                    o("pool", "tensor_tensor", [gi.r(0, n), di.r(0, n)], [hi.r(0, n)], out=hi[:, :n], in0=gi[:, :n], in1=di[:, :n], op=ALU.add)
                    def readout(hr=hr, hi=hi, n=n, t0=t0, ds=ds, st=st, yt=yt, yt3=yt3):
                        py = self.P[4 + self.nxt("pb", 4)]
                        nb, b0 = n // 128, t0 // 128
                        for j in range(nb):
                            o("pe", "matmul", [hr.r(j * 128, j * 128 + 128), self.Cblk.r()], [py.r(j * 32, j * 32 + 32)], out=py[:, j * 32:(j + 1) * 32],
                              lhsT=hr[:, j * 128:(j + 1) * 128], rhs=self.Cblk[:, 0, ds, :], start=True, stop=False)
                            o("pe", "matmul", [hi.r(j * 128, j * 128 + 128), self.Cblk.r()], [py.r(j * 32, j * 32 + 32)], out=py[:, j * 32:(j + 1) * 32],
                              lhsT=hi[:, j * 128:(j + 1) * 128], rhs=self.Cblk[:, 1, ds, :], start=False, stop=True)
                        o("act", "activation", [py.r(0, nb * 32)], [yt.r((b0 + j) * 256 + st * 32, (b0 + j) * 256 + st * 32 + 32) for j in range(nb)],
                          out=yt3[:, b0:b0 + nb, st * 32:(st + 1) * 32], in_=py[:, 0:nb * 32].rearrange("p (b c) -> p b c", c=32), func=AF.Identity)

                    if pend is not None:
                        pend()
                    pend = readout
                    yield
        pend()
        ytF3 = ytF[:, :].rearrange("p (b c) -> p b c", c=256)
        ytB3 = ytB[:, :].rearrange("p (b c) -> p b c", c=256)
        for ti, (t0, n) in enumerate(TILES):
            nb, b0 = n // 128, t0 // 128
            gys = []
            for half in range(2):
                pc = self.P[self.nxt("pa", 4)]
                for j in range(nb):
                    tb = b0 + j
                    bt = (1 - tb) if tb < 2 else (19 - tb)
                    o("pe", "matmul", [ytF.r(tb * 256, tb * 256 + 256), self.Ib.r()], [pc.r(j * 128, j * 128 + 128)], out=pc[:, j * 128:(j + 1) * 128],
                      lhsT=ytF3[:, tb, half * 128:(half + 1) * 128], rhs=self.Ib[:, :], start=True, stop=False)
                    o("pe", "matmul", [ytB.r(bt * 256, bt * 256 + 256), self.Jb.r()], [pc.r(j * 128, j * 128 + 128)], out=pc[:, j * 128:(j + 1) * 128],
                      lhsT=ytB3[:, bt, half * 128:(half + 1) * 128], rhs=self.Jb[:, :], start=False, stop=True)
                ut = self.s5w[half]
                fw.dma("sp", ("dma_start", dict(out=ut[:, :n], in_=self.sU[half * 128:(half + 1) * 128, t0:t0 + n])), reads=[self.sU.r()], writes=[ut.r(0, n)])
                yv = self.s5w[2 + half]
                o("dve", "scalar_tensor_tensor", [ut.r(0, n), self.s5dS.r(), pc.r(0, n)], [yv.r(0, n)], out=yv[:, :n], in0=ut[:, :n], scalar=self.s5dS[:, l, half:half + 1], in1=pc[:, :n], op0=ALU.mult, op1=ALU.add)
                o("dve", "tensor_tensor", [yv.r(0, n)], [ut.r(0, n)], out=ut[:, :n], in0=yv[:, :n], in1=yv[:, :n], op=ALU.mult)
                o("dve", "tensor_scalar", [ut.r(0, n)], [ut.r(0, n)], out=ut[:, :n], in0=ut[:, :n], scalar1=0.044715, scalar2=1.0, op0=ALU.mult, op1=ALU.add)
                o("dve", "tensor_tensor", [ut.r(0, n), yv.r(0, n)], [ut.r(0, n)], out=ut[:, :n], in0=ut[:, :n], in1=yv[:, :n], op=ALU.mult)
                o("act", "activation", [ut.r(0, n)], [ut.r(0, n)], out=ut[:, :n], in_=ut[:, :n], func=AF.Sigmoid, scale=1.5957691216057308)
                gy = self.s5gy[half]
                o("dve", "tensor_tensor", [ut.r(0, n), yv.r(0, n)], [gy.r(0, n)], out=gy[:, :n], in0=ut[:, :n], in1=yv[:, :n], op=ALU.mult)
                gys.append(gy)
            pm = [self.P[4 + m_] for m_ in range(4)]
            for m_ in range(4):
                for half in range(2):
                    o("pe", "matmul", [self.gluwS.r(), gys[half].r(0, n)], [pm[m_].r(0, n)], out=pm[m_][:, :n],
                      lhsT=self.gluwS[:, half, m_ * 128:(m_ + 1) * 128], rhs=gys[half][:, :n], start=(half == 0), stop=(half == 1))
            for mm in range(2):
                sgm = self.rstd
                o("act", "activation", [pm[2 + mm].r(0, n), self.glubS.r()], [sgm.r(0, n)], out=sgm[:, :n], in_=pm[2 + mm][:, :n], func=AF.Sigmoid, bias=self.glubS[:, l, 2 + mm:3 + mm], scale=1.0)
                ob = self.s5ob
                o("dve", "scalar_tensor_tensor", [pm[mm].r(0, n), self.glubS.r(), sgm.r(0, n)], [ob.r(0, n)], out=ob[:, :n], in0=pm[mm][:, :n], scalar=self.glubS[:, l, mm:mm + 1], in1=sgm[:, :n], op0=ALU.add, op1=ALU.mult)
                fw.dma("sp", ("dma_start", dict(out=self.mix[768 + mm * 128:768 + (mm + 1) * 128, t0:t0 + n], in_=ob[:, :n])), reads=[ob.r(0, n)], writes=[self.mix.r(16 + ti, 17 + ti)])
            yield

    def zero_mix(self, r0, r1):
        fw = self.fw
        fw.op("dve", ("memset", dict(ap=self.zt[:], constant=0.0)), writes=[self.zt.r()])
        for rr in range(r0, r1, 128):
            for ti, (t0, n) in enumerate(TILES):
                fw.dma("sp", ("dma_start", dict(out=self.mix[rr:rr + 128, t0:t0 + n], in_=self.zt[:, :n])), reads=[self.zt.r()], writes=[self.mix.r(ti, ti + 1)])

    def attn(self, l):
        fw = self.fw
        fw.dma("sp", ("dma_start", dict(out=self.kTall[:], in_=self.sK[:, :])), reads=[self.sK.r()], writes=[self.kTall.r()])
        fw.dma("sp", ("dma_start", dict(out=self.vt[:], in_=self.sV[:, :].rearrange("(b p) c -> p b c", p=128))), reads=[self.sV.r()], writes=[self.vt.r()])
        fw.dma("pool", ("dma_start", dict(out=self.mbnext[:, :], in_=self.mnext_d[:])), reads=[self.mnext_d.r()], writes=[self.mbnext.r()])
        units = [(hk, qb) for hk in range(2) for qb in range(18)]

        def load_q(u):
            hk, qb = units[u]
            qt = self.qt[u % 2]
            fw.dma("sp", ("dma_start", dict(out=qt[hk * 64:hk * 64 + 64, :].rearrange("d (h t) -> d h t", h=4),
                                             in_=self.sQ[hk * 256:(hk + 1) * 256, qb * 128:qb * 128 + 128].rearrange("(h d) t -> d h t", d=64))),
                   reads=[self.sQ.r()], writes=[qt.r()])

        def finish(u):
            hk, qb = units[u]
            ob, den = self.ob[u % 2], self.tmp[u % 2]
            fw.op("dve", ("tensor_tensor", dict(out=ob[0:64, :], in0=ob[0:64, :], in1=den[0:64, :], op=ALU.mult)), reads=[ob.r(), den.r()], writes=[ob.r()])
            ti = 0 if qb < 2 else 1 + (qb - 2) // 4
            fw.dma("sp", ("dma_start", dict(out=self.mix[256 + hk * 256:256 + (hk + 1) * 256, qb * 128:qb * 128 + 128].rearrange("(h d) t -> d h t", d=64),
                                             in_=ob[0:64, :].rearrange("d (h t) -> d h t", h=4))),
                   reads=[ob.r()], writes=[self.mix.r(8 + ti, 9 + ti)])

        load_q(0)
        for u, (hk, qb) in enumerate(units):
            if u + 1 < len(units):
                load_q(u + 1)
            if u > 0:
                finish(u - 1)
            qt = self.qt[u % 2]
            pb_ = hk * 64
            kbs = [(0, None), (1, None)]
            if qb >= 2:
                for dl, mk in ((-1, self.mbprev), (0, None), (1, self.mbnext)):
                    kb = qb + dl
                    if 2 <= kb <= 17:
                        kbs.append((kb, mk))
            pp = self.nxt("pb", 2)
            po, pd = self.P[4 + 2 * pp], self.P[5 + 2 * pp]
            for idx, (kb, mk) in enumerate(kbs):
                ps = self.P[self.nxt("pa", 4)]
                fw.op("pe", ("matmul", dict(out=ps[:, :], lhsT=self.kTall[pb_:pb_ + 64, kb * 128:(kb + 1) * 128], rhs=qt[pb_:pb_ + 64, :], start=True, stop=(mk is None))),
                      reads=[self.kTall.r(kb * 128, (kb + 1) * 128), qt.r()], writes=[ps.r()])
                if mk is not None:
                    fw.op("pe", ("matmul", dict(out=ps[:, :], lhsT=self.Ib[:, :], rhs=mk[:, :], start=False, stop=True)),
                          reads=[self.Ib.r(), mk.r()], writes=[ps.r()])
                pT = self.pT[self.nxt("pT", 3)]
                fw.op("act", ("activation", dict(out=pT[:, :], in_=ps[:, :], func=AF.Exp, scale=0.125)), reads=[ps.r()], writes=[pT.r()])
                st, sp_ = (idx == 0), (idx == len(kbs) - 1)
                fw.op("pe", ("matmul", dict(out=po[0:64, :], lhsT=self.vt[:, kb, hk * 64:(hk + 1) * 64], rhs=pT[:, :], start=st, stop=sp_)),
                      reads=[self.vt.r(), pT.r()], writes=[po.r()])
                fw.op("pe", ("matmul", dict(out=pd[0:64, :], lhsT=self.onesb[:, 0:64], rhs=pT[:, :], start=st, stop=sp_)),
                      reads=[self.onesb.r(), pT.r()], writes=[pd.r()])
            ob, den = self.ob[u % 2], self.tmp[u % 2]
            for h in range(4):
                fw.op("act", ("activation", dict(out=den[0:64, h * 128:(h + 1) * 128], in_=pd[0:64, h * 128:(h + 1) * 128], func=AF.Ln,
                                                 bias=self.esink[:, l, hk * 4 + h:hk * 4 + h + 1], scale=1.0)),
                      reads=[pd.r(), self.esink.r()], writes=[den.r()])
            fw.op("act", ("activation", dict(out=den[0:64, :], in_=den[0:64, :], func=AF.Exp, scale=-1.0)), reads=[den.r()], writes=[den.r()])
            fw.op("act", ("activation", dict(out=ob[0:64, :], in_=po[0:64, :], func=AF.Identity)), reads=[po.r()], writes=[ob.r()])
            yield
        finish(len(units) - 1)
        yield

    def outproj(self, l, tiles):
        fw, xT, hT = self.fw, self.xT, self.hT
        wv = self.wout[l].rearrange("(k p) n -> p k n", p=128)
        for c in range(2):
            fw.dma("pool", ("dma_start", dict(out=self.wA[c][:], in_=wv[:, :, c * 512:(c + 1) * 512])), reads=[self.wout.r()], writes=[self.wA[c].r()])
        for (t0, n) in tiles:
            var = 1 if t0 == 0 else 0
            fw.dma("sp", ("dma_start", dict(out=hT[:, :, t0:t0 + n], in_=self.mix[:, t0:t0 + n].rearrange("(k p) t -> p k t", p=128))),
                   reads=[self.mix.r()], writes=[hT.r(k * NTOK + t0, k * NTOK + t0 + n) for k in range(8)])
            for o in range(8):
                po = self.P[4 + self.nxt("pb", 4)]
                self.mm8(po, n, self.wA[o // 4], (o % 4) * 128, t0)
                fw.op("dve", ("scalar_tensor_tensor", dict(
                    out=xT[:, o, t0:t0 + n], in0=po[:, :n], scalar=self.GT[:, l, 1, o, var:var + 1], in1=xT[:, o, t0:t0 + n], op0=ALU.mult, op1=ALU.add)),
                    reads=[po.r(0, n), self.GT.r(), xT.r(o * NTOK + t0, o * NTOK + t0 + n)], writes=[xT.r(o * NTOK + t0, o * NTOK + t0 + n)])

    def final(self):
        fw, xT = self.fw, self.xT
        for (t0, n) in TILES[1:]:
            self.rms_stats(t0, n)
            for sb4 in range(n // 128):
                ot = self.xin[self.nxt("xin", 2)]
                tb = t0 + sb4 * 128
                for k4 in range(2):
                    ps = self.P[self.nxt("pa", 4)]
                    for kk in range(4):
                        k = k4 * 4 + kk
                        tmp = self.tmp[self.nxt("tmp", 2)]
                        fw.op("dve", ("scalar_tensor_tensor", dict(
                            out=tmp[:, :128], in0=xT[:, k, tb:tb + 128], scalar=self.fngs[:, k:k + 1], in1=self.rstd[:, sb4 * 128:(sb4 + 1) * 128], op0=ALU.mult, op1=ALU.mult)),
                            reads=[xT.r(k * NTOK + tb, k * NTOK + tb + 128), self.fngs.r(), self.rstd.r(0, n)], writes=[tmp.r(0, 128)])
                        fw.op("pe", ("transpose", dict(out=ps[:, kk * 128:(kk + 1) * 128], in_=tmp[:, :128], identity=self.ident[:])),
                              reads=[tmp.r(0, 128), self.ident.r()], writes=[ps.r(kk * 128, (kk + 1) * 128)])
                    fw.op("act", ("activation", dict(out=ot[:, k4 * 512:(k4 + 1) * 512], in_=ps[:], func=AF.Identity)),
                          reads=[ps.r()], writes=[ot.r(k4 * 512, (k4 + 1) * 512)])
                r0 = tb - NCTX
                fw.dma("sp", ("dma_start", dict(out=self.y[r0:r0 + 128, :], in_=ot[:])), reads=[ot.r()], writes=[self.y.r()])

    def build(self):
        self.prologue()
        last = DEPTH - 1
        for l in range(DEPTH):
            if self.stage < 1:
                break
            self.norm_h(l, 0, TILES)
            self.ffn(l, 0, TILES)
            if self.stage < 2:
                break
            tl = TILES if l < last else TILES[1:]
            if self.stage >= 3:
                self.norm_h(l, 1, TILES)
                for _ in self.inproj(l, ["A0", "A1", "TV"]):
                    pass
                rest = self.inproj(l, ["BK", "B0", "B1", "TU"])
                if self.stage >= 4 and INTERLEAVE:
                    hg_early = self.hgrn(l)
                    for _ in rest:
                        next(hg_early)
                else:
                    hg_early = None
                    for _ in rest:
                        pass
                gens = [self.attn(l)]
                if self.stage < 4:
                    self.zero_mix(0, 256)
                else:
                    gens.append(hg_early if hg_early is not None else self.hgrn(l))
                if self.stage < 5:
                    self.zero_mix(768, 1024)
                else:
                    gens.append(self.s5(l))
                if not INTERLEAVE:
                    for g in gens:
                        for _ in g:
                            pass
                else:
                    while gens:
                        for g in list(gens):
                            try:
                                next(g)
                            except StopIteration:
                                gens.remove(g)
                self.outproj(l, tl)
                if self.stage == 3 and l == 0:
                    break
            self.norm_h(l, 2, tl)
            self.ffn(l, 2, tl)
        self.final()
        self.fw.wait_all("sp")
        self.fw.emit()
        return self.nc


def host_prep(inp, b):
    f32 = np.float32
    d = {}
    d["x_in"] = np.ascontiguousarray(np.concatenate([inp["ctx"][b], inp["x"][b]], axis=0), dtype=f32)
    cv = np.stack([inp["c"][b], inp["c_ctx"]], axis=-1)
    d["cvec"] = np.ascontiguousarray(cv.reshape(8, 128, 2).transpose(1, 0, 2), dtype=f32)
    d["ada_w"] = inp["ada_w"]
    d["ada_b_t"] = np.ascontiguousarray(inp["ada_b"].reshape(DEPTH, 72, 128).transpose(2, 0, 1), dtype=f32)
    d["norm_g_t"] = np.ascontiguousarray(inp["norm_g"].reshape(DEPTH, 3, 8, 128).transpose(3, 0, 1, 2), dtype=f32)
    d["fng_t"] = np.ascontiguousarray(inp["final_norm_g"].reshape(8, 128).T, dtype=f32)
    d["ffn_w1"] = inp["ffn_w1"]
    d["ffn_w2"] = inp["ffn_w2"]
    d["ident"] = np.eye(128, dtype=f32)
    d["win_ext"] = _WIN_EXT(inp)
    d["w_out"] = inp["w_out"]
    rc, rs = _ROPE()
    d["ropeC"], d["ropeS"] = rc, rs
    ii = np.arange(128)[:, None]
    jj = np.arange(128)[None, :]
    d["mprev"] = np.ascontiguousarray(np.tile(np.where(jj <= ii, 0.0, -240000.0).astype(f32), (1, 4)))
    d["mnext"] = np.ascontiguousarray(np.tile(np.where(ii <= jj, 0.0, -240000.0).astype(f32), (1, 4)))
    d["lb_t"] = np.ascontiguousarray(inp["hgrn_lower_bounds"].reshape(DEPTH, 2, 4, 64).transpose(3, 0, 1, 2).reshape(64, DEPTH, 8), dtype=f32)
    d["hng_t"] = np.ascontiguousarray(inp["hgrn_norm_g"].reshape(DEPTH, 4, 64).transpose(2, 0, 1), dtype=f32)
    rm = np.ones((64, 1024), f32)
    rm[:, ::32] = 0.0
    d["rmask"] = rm
    si = np.arange(64)[:, None]
    tj = np.arange(64)[None, :]
    d["hmask"] = np.ascontiguousarray(np.stack([(si <= tj), (si >= tj)], axis=1).astype(f32))
    d.update(_S5(inp))
    d["sink_t"] = np.ascontiguousarray(np.broadcast_to(inp["attn_sink"][None], (64, DEPTH, 8)), dtype=f32)
    return d


_HC = {}


def _WIN_EXT(inp):
    if "win" in _HC:
        return _HC["win"]
    w = inp["w_in"]
    aq, ai, af, ab, ag = (w[:, :, i * 256:(i + 1) * 256] for i in range(5))
    bq = w[:, :, 1280:1792]
    bk = w[:, :, 1792:1920]
    bv = w[:, :, 1920:2048]
    cu = w[:, :, 2048:2304]
    perm = np.arange(64).reshape(2, 2, 16)[:, ::-1, :].reshape(64)

    def partner(m):
        nh = m.shape[-1] // 64
        idx = (np.arange(nh)[:, None] * 64 + perm[None, :]).reshape(-1)
        return m[:, :, idx]

    bqp, bkp = partner(bq), partner(bk)
    cols = [aq, af, ab, ag]
    for j in range(4):
        cols += [bq[:, :, j * 128:(j + 1) * 128], bqp[:, :, j * 128:(j + 1) * 128]]
    cols += [bk, bkp, cu, ai, bv, cu]
    _HC["win"] = np.ascontiguousarray(np.concatenate(cols, axis=-1), dtype=np.float32)
    assert _HC["win"].shape[-1] == 3200
    return _HC["win"]


def _S5(inp):
    if "s5" in _HC:
        return _HC["s5"]
    f32 = np.float32
    L = DEPTH
    are, aim, ldt = inp["s5_a_re"], inp["s5_a_im"], inp["s5_log_dt"]

    def st_layout(a):
        return a.reshape(L, 2, 8, 2, 64).transpose(3, 4, 0, 1, 2).reshape(128, L, 16)

    ld_s = np.broadcast_to(ldt.reshape(L, 2, 8, 2, 1), (L, 2, 8, 2, 64)).transpose(3, 4, 0, 1, 2).reshape(128, L, 16)
    As = np.stack([st_layout(are), st_layout(aim), ld_s], axis=2)

    def q_layout(a):
        t = a.reshape(L, 2, 2, 8, 64).transpose(3, 0, 1, 2, 4)
        t = np.repeat(t[:, None], 16, axis=1)
        return t.reshape(128, L, 4, 64)

    Aq = np.stack([q_layout(are), q_layout(aim)], axis=2)
    ldq = np.repeat(ldt.reshape(L, 2, 2, 8).transpose(3, 0, 1, 2)[:, None], 16, axis=1).reshape(128, L, 4)

    def b_layout(b):
        return b.reshape(L, 2, 8, 64, 16).transpose(2, 4, 0, 1, 3).reshape(128, L, 2, 64)

    Bq = np.stack([b_layout(inp["s5_b_re"]), b_layout(inp["s5_b_im"])], axis=2)

    def c_layout(c):
        t = c.reshape(L, 2, 8, 2, 16, 64)
        out = np.zeros((2, 64, L, 2, 8, 2, 16), f32)
        for gl in range(2):
            out[gl, :, :, :, :, gl, :] = t[:, :, :, gl, :, :].transpose(4, 0, 1, 2, 3)
        return out.reshape(128, L, 16, 32)

    Cb = np.stack([c_layout(inp["s5_c_re"]), c_layout(inp["s5_c_im"])], axis=2)
    c = lambda a: np.ascontiguousarray(a, dtype=f32)
    _HC["s5"] = {
        "s5_As": c(As), "s5_Aq": c(Aq), "s5_LDq": c(ldq), "s5_Bq": c(Bq), "s5_Cblk": c(Cb),
        "s5_d_t": c(inp["s5_d"].reshape(L, 2, 128).transpose(2, 0, 1)),
        "glu_b_t": c(inp["s5_glu_b"].reshape(L, 4, 128).transpose(2, 0, 1)),
        "s5_glu_w": inp["s5_glu_w"],
        "rowmask": c(np.arange(128)[:, None] // 16 == np.arange(8)[None, :]),
        "Jmat": c(np.eye(128)[::-1]),
    }
    return _HC["s5"]


def _ROPE():
    if "rope" in _HC:
        return _HC["rope"]
    f32 = np.float32
    t = np.arange(NLAT)
    row = (t // 64).astype(f32)
    col = (t % 64).astype(f32)
    inv = (f32(10000.0) ** (-np.arange(16, dtype=f32) / f32(16))).astype(f32)
    ang = np.stack([row[:, None] * inv, col[:, None] * inv], axis=1).astype(f32)
    cos, sin = np.cos(ang).astype(f32), np.sin(ang).astype(f32)
    C = np.ones((64, NTOK), f32)
    S = np.zeros((64, NTOK), f32)
    for ax in range(2):
        for two in range(2):
            d0 = ax * 32 + two * 16
            C[d0:d0 + 16, NCTX:] = cos[:, ax, :].T
            S[d0:d0 + 16, NCTX:] = (-sin[:, ax, :].T if two == 0 else sin[:, ax, :].T)
    _HC["rope"] = (np.ascontiguousarray(np.tile(C, (2, 1))), np.ascontiguousarray(np.tile(S, (2, 1))))
    return _HC["rope"]


_NC_CACHE = {}


def kernel(**inputs):
    inp = {k: np.asarray(v) for k, v in inputs.items()}
    stage = int(os.environ.get("KSTAGE", "99"))
    ncores = int(os.environ.get("KCORES", "8"))
    if stage not in _NC_CACHE:
        _NC_CACHE[stage] = K(stage).build()
    nc = _NC_CACHE[stage]
    in_maps = [host_prep(inp, b) for b in range(ncores)]
    res = run_bass_kernel_spmd(nc, in_maps, core_ids=list(range(ncores)))
    out = np.stack([np.asarray(r["y"]) for r in res.results], axis=0)
    return out.astype(np.float32)
```

```python
import os
import numpy as np
import concourse.bass as bass
import concourse.mybir as mybir
from concourse.bass_utils import run_bass_kernel_spmd

F32 = mybir.dt.float32
BF16 = mybir.dt.bfloat16
AF = mybir.ActivationFunctionType
ALU = mybir.AluOpType

NTOK = 2304
NCTX = 256
NLAT = 2048
DM = 1024
DFF = 2816
DEPTH = 4
TILES = [(0, 256), (256, 512), (768, 512), (1280, 512), (1792, 512)]
EPS = 1e-6
INTERLEAVE = True
PIPE_NORM = True


class Reg:
    __slots__ = ("base", "lo", "hi")

    def __init__(self, base, lo, hi):
        self.base, self.lo, self.hi = base, lo, hi


class Buf:
    _n = 0

    def __init__(self, t, size, base=None, off=0):
        self.t = t
        self.size = size
        if base is None:
            Buf._n += 1
            base = "b%d" % Buf._n
        self.base = base
        self.off = off

    def r(self, lo=0, hi=None):
        if hi is None:
            hi = self.size
        return Reg(self.base, self.off + lo, self.off + hi)

    def __getitem__(self, k):
        return self.t[k]


class Alias(Buf):
    def __init__(self, view, orig):
        self.t = view
        self.size = orig.size
        self.base = orig.base
        self.off = orig.off

    def r(self, lo=0, hi=None):
        return Reg(self.base, self.off, self.off + self.size)


class View(Buf):
    def __init__(self, view, orig, off, size, scale=1):
        self.t = view
        self.size = size
        self.base = orig.base
        self.off = orig.off + off
        self.scale = scale

    def r(self, lo=0, hi=None):
        if hi is None:
            hi = self.size
        return Reg(self.base, self.off + lo * self.scale, self.off + hi * self.scale)


class Q:
    def __init__(self, name, sem):
        self.name, self.sem = name, sem
        self.count = 0
        self.prog = []
        self.waited = {}


class FW:
    def __init__(self, nc):
        self.nc = nc
        self.q = {}
        for n in ("pe", "act", "dve", "pool", "sp"):
            self.q[n] = Q(n, nc.alloc_semaphore("s_" + n))
        self.NDS = 8
        self.dsems = {n: [nc.alloc_semaphore("d_%s%d" % (n, i)) for i in range(self.NDS)] for n in ("sp", "pool")}
        self.dcount = {n: 0 for n in self.dsems}
        self.acc = {}
        self.n_instr = 0

    def _need(self, ev, waits):
        sem, val, key = ev
        cur = waits.get(key)
        if cur is None or cur[1] < val:
            waits[key] = (sem, val)

    def _deps(self, qn, reads, writes, is_dma):
        waits = {}
        for regs, w in ((reads, False), (writes, True)):
            for rg in regs:
                lst = self.acc.get(rg.base)
                if not lst:
                    continue
                for a in lst:
                    if a[1] <= rg.lo or a[0] >= rg.hi:
                        continue
                    if not (a[2] or w):
                        continue
                    if a[3] == qn and not a[5] and not is_dma and not (a[2] and not w):
                        continue
                    self._need(a[4], waits)
        return waits

    def _record(self, qn, reads, writes, ev, is_dma):
        for rg in writes:
            lst = self.acc.setdefault(rg.base, [])
            lst[:] = [a for a in lst if not (a[0] >= rg.lo and a[1] <= rg.hi)]
            lst.append([rg.lo, rg.hi, True, qn, ev, is_dma])
        for rg in reads:
            lst = self.acc.setdefault(rg.base, [])
            lst[:] = [a for a in lst if not (not a[2] and a[3] == qn and a[5] == is_dma and a[0] == rg.lo and a[1] == rg.hi)]
            lst.append([rg.lo, rg.hi, False, qn, ev, is_dma])

    def _emit_waits(self, q, waits):
        for key, (sem, val) in waits.items():
            if q.waited.get(key, 0) >= val:
                continue
            q.waited[key] = val
            q.prog.append(("w", sem, val))

    def op(self, qn, fn, reads=(), writes=()):
        q = self.q[qn]
        waits = self._deps(qn, reads, writes, False)
        self._emit_waits(q, waits)
        q.count += 1
        ev = (q.sem, q.count, qn)
        q.prog.append(("o", fn, q.sem, 1))
        self._record(qn, reads, writes, ev, False)
        self.n_instr += 1
        return ev

    def dma(self, qn, fn, reads=(), writes=()):
        q = self.q[qn]
        waits = self._deps(qn, reads, writes, True)
        j = self.dcount[qn]
        self.dcount[qn] += 1
        s = j % self.NDS
        sem = self.dsems[qn][s]
        key = "d_%s%d" % (qn, s)
        prev = 16 * (j // self.NDS)
        if prev > 0:
            self._need((sem, prev, key), waits)
        self._emit_waits(q, waits)
        ev = (sem, prev + 16, key)
        q.prog.append(("o", fn, sem, 16))
        self._record(qn, reads, writes, ev, True)
        self.n_instr += 1
        return ev

    def wait_all(self, qn):
        q = self.q[qn]
        waits = {}
        for lst in self.acc.values():
            for a in lst:
                self._need(a[4], waits)
        self._emit_waits(q, waits)

    def emit(self):
        nc = self.nc
        me = self

        def run(qn, eng):
            for it in me.q[qn].prog:
                if it[0] == "w":
                    eng.wait_ge(it[1], it[2])
                else:
                    getattr(eng, it[1][0])(**it[1][1]).then_inc(it[2], it[3])

        with nc.Block() as block:
            @block.tensor
            def _(e):
                run("pe", e)

            @block.scalar
            def _(e):
                run("act", e)

            @block.vector
            def _(e):
                run("dve", e)

            @block.gpsimd
            def _(e):
                run("pool", e)

            @block.sync
            def _(e):
                run("sp", e)


class K:
    def __init__(self, stage=99):
        self.stage = stage
        nc = self.nc = bass.Bass("TRN2", target_bir_lowering=False)
        fw = self.fw = FW(nc)
        self.din = {}
        self.rot = {}

        def dram_in(name, shape, dt=F32):
            t = nc.dram_tensor(name, list(shape), dt, kind="ExternalInput").ap()
            n = int(np.prod(shape[1:])) if len(shape) > 1 else 1
            self.din[name] = Buf(t, max(n, 1))
            return self.din[name]

        self.x_in = dram_in("x_in", [NTOK, DM])
        self.cvec = dram_in("cvec", [128, 8, 2])
        self.ada_w = dram_in("ada_w", [DEPTH, DM, 9 * DM])
        self.ada_b = dram_in("ada_b_t", [128, DEPTH, 72])
        self.norm_g = dram_in("norm_g_t", [128, DEPTH, 3, 8])
        self.fng = dram_in("fng_t", [128, 8])
        self.w1 = dram_in("ffn_w1", [DEPTH, 2, DM, 2 * DFF])
        self.w2 = dram_in("ffn_w2", [DEPTH, 2, DFF, DM])
        self.ident_d = dram_in("ident", [128, 128])
        self.win = dram_in("win_ext", [DEPTH, DM, 3200])
        self.wout = dram_in("w_out", [DEPTH, DM, DM])
        self.ropeC = dram_in("ropeC", [128, NTOK])
        self.ropeS = dram_in("ropeS", [128, NTOK])
        self.mprev_d = dram_in("mprev", [128, 512])
        self.mnext_d = dram_in("mnext", [128, 512])
        self.sink_d = dram_in("sink_t", [64, DEPTH, 8])

        self.lb_d = dram_in("lb_t", [64, DEPTH, 8])
        self.hng_d = dram_in("hng_t", [64, DEPTH, 4])
        self.rm_d = dram_in("rmask", [64, 1024])
        self.hm_d = dram_in("hmask", [64, 2, 64])

        self.s5A = dram_in("s5_As", [128, DEPTH, 3, 16])
        self.s5Q = dram_in("s5_Aq", [128, DEPTH, 2, 4, 64])
        self.s5LDq = dram_in("s5_LDq", [128, DEPTH, 4])
        self.s5B = dram_in("s5_Bq", [128, DEPTH, 2, 2, 64])
        self.s5C = dram_in("s5_Cblk", [128, DEPTH, 2, 16, 32])
        self.s5d_d = dram_in("s5_d_t", [128, DEPTH, 2])
        self.glub_d = dram_in("glu_b_t", [128, DEPTH, 4])
        self.gluw_d = dram_in("s5_glu_w", [DEPTH, 256, 512])
        self.rowm_d = dram_in("rowmask", [128, 8])
        self.J_d = dram_in("Jmat", [128, 128])

        def scr(name, shape, dt):
            return Buf(nc.dram_tensor(name, list(shape), dt, kind="Internal").ap(), 24)

        self.mix = scr("mix", [DM, NTOK], BF16)
        self.sQ = scr("sQ", [512, NTOK], BF16)
        self.sK = scr("sK", [128, NTOK], BF16)
        self.sV = scr("sV", [NTOK, 128], BF16)
        self.sA = scr("sA", [4, 256, NTOK], F32)
        self.sAv = scr("sAv", [NTOK, 256], BF16)
        self.sO = scr("sO", [256, NTOK], F32)
        self.sU = scr("sU", [256, NTOK], F32)
        self.sUt = scr("sUt", [NTOK, 256], BF16)
        y = nc.dram_tensor("y", [NLAT, DM], F32, kind="ExternalOutput").ap()
        self.y = Buf(y, DM)

        def sb(name, shape, dt=F32):
            return Buf(nc.alloc_sbuf_tensor(name, list(shape), dt), int(np.prod(shape[1:])))

        self.xT = sb("xT", [128, 8, NTOK])
        self.hT = sb("hT", [128, 8, NTOK], BF16)
        self.wA = [sb("wA%d" % i, [128, 8, 512], BF16) for i in range(2)]
        self.wB = [sb("wB%d" % i, [128, 2, 1024], BF16) for i in range(2)]
        self.sg = [sb("sg%d" % i, [128, 512]) for i in range(2)]
        self.tmp = [sb("tmp%d" % i, [128, 512]) for i in range(2)]
        self.rstd = sb("rstd", [128, 512])
        self.hid = [sb("hid%d" % i, [128, 2, 512], BF16) for i in range(2)]
        self.sq = [sb("sq%d" % i, [128, 512], BF16) for i in range(2)]
        self.xin = [sb("xin%d" % i, [128, 1024]) for i in range(2)]
        self.ident = sb("identf", [128, 128])
        self.onesb = sb("onesb", [128, 128], BF16)
        self.cv = sb("cv", [128, 8, 2])
        self.csb = sb("csb", [128, 8, 2], BF16)
        self.modr = sb("modr", [128, DEPTH, 72, 2])
        self.adab = sb("adab", [128, DEPTH, 72])
        self.ng = sb("ng", [128, DEPTH, 3, 8])
        self.fngs = sb("fngs", [128, 8])
        self.GS = sb("GS", [128, DEPTH, 3, 8, 2])
        self.GT = sb("GT", [128, DEPTH, 3, 8, 2])

        self.esink = sb("esink", [64, DEPTH, 8])
        self.rc = sb("rc", [128, 512])
        self.rs = sb("rs", [128, 512])
        self.kTall = sb("kTall", [128, NTOK], BF16)
        self.vt = sb("vt", [128, 18, 128], BF16)
        self.qt = [sb("qt%d" % i, [128, 512], BF16) for i in range(2)]
        self.sq2 = [sb("sq2_%d" % i, [128, 512], BF16) for i in range(2)]
        self.pT = [sb("pT%d" % i, [128, 512], BF16) for i in range(3)]
        self.ob = [sb("ob%d" % i, [128, 512], BF16) for i in range(2)]
        self.zt = Alias(self.ob[0].t, self.ob[0])
        self.LB = sb("LB", [64, DEPTH, 8])
        self.OML = sb("OML", [64, DEPTH, 8])
        self.lbw = sb("lbw", [64, DEPTH, 8])
        self.lbs = sb("lbs", [64, 8])
        self.hng = sb("hng", [64, DEPTH, 4])
        self.RM = sb("RM", [64, 1024])
        self.hmask = sb("hmask_s", [64, 2, 64], BF16)
        self.identb = sb("identb", [64, 64], BF16)
        self.HW = [sb("HW%d" % i, [64, 1024]) for i in range(2)] + [
            Alias(self.wB[i].t[:].rearrange("p f n -> p (f n)").bitcast(F32)[0:64, :], self.wB[i]) for i in range(2)]
        self.HQE = Alias(self.hid[0].t[:].rearrange("p f n -> p (f n)")[0:64, :], self.hid[0])
        self.HKE = Alias(self.hid[1].t[:].rearrange("p f n -> p (f n)")[0:64, :], self.hid[1])
        self.HV = sb("HV", [32, 8, 256], BF16)
        self.HS = Alias(self.sg[0].t[0:64, 0:256], self.sg[0])
        self.HT1 = Alias(self.sg[1].t[0:64, 0:256], self.sg[1])
        self.HSP = sb("HSP", [64, 256], BF16)
        self.HKT3 = [sb("HKT3_%d" % i, [32, 256], BF16) for i in range(3)]
        self.HSC3 = [sb("HSC3_%d" % i, [32, 128], BF16) for i in range(3)]
        self.HKT = self.HKT3[0]
        self.HSC = self.HSC3[0]
        self.HM1 = sb("HM1", [64, 32])
        self.HEM = sb("HEM", [64, 32])
        self.HET = sb("HET", [64, 32])
        self.HE2 = sb("HE2", [64, 32])
        self.sAs = sb("s5As_s", [128, 3, 16])
        self.sDT = sb("s5DT", [128, 16])
        self.sR = sb("s5R", [128, 16])
        self.sTH = sb("s5TH", [128, 16])
        self.sC1 = sb("s5C1", [128, 16])
        self.sS1 = sb("s5S1", [128, 16])
        self.sY16 = sb("s5Y16", [128, 16])
        self.sN16 = sb("s5N16", [128, 16])
        self.sT16 = sb("s5T16", [128, 16])
        self.ldq = sb("s5ldq", [128, DEPTH, 4])
        self.dtq = sb("s5dtq", [128, 4])
        a0 = self.wA[0].t[:].rearrange("p k n -> p (k n)")
        a1 = self.wA[1].t[:].rearrange("p k n -> p (k n)").bitcast(F32)
        self.Cblk = View(a0[:, 0:1024].rearrange("p (a b c) -> p a b c", a=2, b=16), self.wA[0], 0, 1024)
        self.gluwS = View(a0[:, 1024:2048].rearrange("p (a b) -> p a b", a=2), self.wA[0], 1024, 1024)
        self.Bblk = View(a0[:, 2048:2304].rearrange("p (a b) -> p a b", a=2), self.wA[0], 2048, 256)
        self.s5gy = [View(a0[:, 2304 + i * 512:2816 + i * 512], self.wA[0], 2304 + i * 512, 512) for i in range(2)]
        self.s5ub = [View(a0[:, 3328 + i * 256:3584 + i * 256], self.wA[0], 3328 + i * 256, 256) for i in range(2)]
        self.s5ob = View(a0[:, 3328:3840], self.wA[0], 3328, 512)
        self.s5w = [View(a1[:, i * 512:(i + 1) * 512], self.wA[1], i * 1024, 512, 2) for i in range(4)]
        self.Jb = sb("Jb", [128, 128], BF16)
        self.Ib = sb("Ib", [128, 128], BF16)
        self.rowm = sb("rowm", [128, 8])
        self.s5dS = sb("s5dS", [128, DEPTH, 2])
        self.glubS = sb("glubS", [128, DEPTH, 4])
        self.hprev = sb("hprev", [128, 2])
        hflat = self.hT.t[:].rearrange("p k t -> p (k t)")
        self.ytF = View(hflat[:, 0:4608], self.hT, 0, 4608)
        self.ytB = View(hflat[:, 4608:9216], self.hT, 4608, 4608)
        self.uTb = View(hflat[:, 9216:13824], self.hT, 9216, 4608)
        self.mbnext = View(hflat[:, 17920:18432], self.hT, 17920, 512)
        self.mbprev = Alias(self.adab.t[:].rearrange("p a b -> p (a b)").bitcast(BF16)[:, 0:512], self.adab)
        self.s5w2 = [View(hflat[:, 14848 + i * 1024:15872 + i * 1024].bitcast(F32), self.hT, 14848 + i * 1024, 512, 2) for i in range(3)] + [self.rstd]
        self.hpt = sb("s5hpt", [128, 2])
        self.BB = View(hflat[:, 13824:14848].bitcast(F32).rearrange("p (a b c) -> p a b c", a=2, b=4), self.hT, 13824, 512, 2)
        self.epsc = sb("epsc", [128, 1])
        self.onec = sb("onec", [128, 1])
        self.P = [Buf(nc.alloc_psum_tensor("ps%d" % i, [128, 512], F32), 512) for i in range(8)]

    def nxt(self, key, n):
        v = self.rot.get(key, 0)
        self.rot[key] = v + 1
        return v % n

    def prologue(self):
        fw = self.fw
        ld = lambda dst, src: fw.dma("sp", ("dma_start", dict(out=dst[:], in_=src[:])), reads=[src.r()], writes=[dst.r()])
        ld(self.ident, self.ident_d)
        ld(self.cv, self.cvec)
        ld(self.adab, self.ada_b)
        ld(self.ng, self.norm_g)
        ld(self.fngs, self.fng)
        fw.op("dve", ("memset", dict(ap=self.onesb[:], constant=1.0)), writes=[self.onesb.r()])
        fw.op("dve", ("memset", dict(ap=self.epsc[:], constant=EPS)), writes=[self.epsc.r()])
        fw.op("dve", ("memset", dict(ap=self.onec[:], constant=1.0)), writes=[self.onec.r()])

        ld(self.lbw, self.lb_d)
        ld(self.hng, self.hng_d)
        ld(self.RM, self.rm_d)
        fw.dma("pool", ("dma_start", dict(out=self.hmask[:], in_=self.hm_d[:])), reads=[self.hm_d.r()], writes=[self.hmask.r()])
        fw.dma("pool", ("dma_start", dict(out=self.identb[:], in_=self.ident_d[0:64, 0:64])), reads=[self.ident_d.r()], writes=[self.identb.r()])
        lbw, lbs, LB = self.lbw, self.lbs, self.LB
        fw.op("act", ("activation", dict(out=lbw[:], in_=lbw[:], func=AF.Exp)), reads=[lbw.r()], writes=[lbw.r()])
        fw.op("dve", ("tensor_tensor", dict(out=lbs[:], in0=lbw[:, 0, :], in1=lbw[:, 1, :], op=ALU.add)), reads=[lbw.r()], writes=[lbs.r()])
        fw.op("dve", ("tensor_tensor", dict(out=lbs[:], in0=lbs[:], in1=lbw[:, 2, :], op=ALU.add)), reads=[lbw.r(), lbs.r()], writes=[lbs.r()])
        fw.op("dve", ("tensor_tensor", dict(out=lbs[:], in0=lbs[:], in1=lbw[:, 3, :], op=ALU.add)), reads=[lbw.r(), lbs.r()], writes=[lbs.r()])
        fw.op("dve", ("reciprocal", dict(out=lbs[:], in_=lbs[:])), reads=[lbs.r()], writes=[lbs.r()])
        fw.op("dve", ("tensor_tensor", dict(out=lbw[:], in0=lbw[:], in1=lbs[:].unsqueeze(1).to_broadcast([64, DEPTH, 8]), op=ALU.mult)), reads=[lbw.r(), lbs.r()], writes=[lbw.r()])
        fw.op("dve", ("memset", dict(ap=LB[:, 0, :], constant=0.0)), writes=[LB.r()])
        fw.op("dve", ("tensor_copy", dict(out=LB[:, 1, :], in_=lbw[:, 1, :])), reads=[lbw.r()], writes=[LB.r()])
        fw.op("dve", ("tensor_tensor", dict(out=LB[:, 2, :], in0=LB[:, 1, :], in1=lbw[:, 2, :], op=ALU.add)), reads=[lbw.r(), LB.r()], writes=[LB.r()])
        fw.op("dve", ("tensor_tensor", dict(out=LB[:, 3, :], in0=LB[:, 2, :], in1=lbw[:, 3, :], op=ALU.add)), reads=[lbw.r(), LB.r()], writes=[LB.r()])
        fw.op("dve", ("tensor_scalar", dict(out=self.OML[:], in0=LB[:], scalar1=-1.0, scalar2=1.0, op0=ALU.mult, op1=ALU.add)), reads=[LB.r()], writes=[self.OML.r()])
        ld(self.ldq, self.s5LDq)
        ld(self.rowm, self.rowm_d)
        ld(self.s5dS, self.s5d_d)
        ld(self.glubS, self.glub_d)
        fw.dma("pool", ("dma_start", dict(out=self.Jb[:], in_=self.J_d[:])), reads=[self.J_d.r()], writes=[self.Jb.r()])
        fw.dma("pool", ("dma_start", dict(out=self.Ib[:], in_=self.ident_d[:])), reads=[self.ident_d.r()], writes=[self.Ib.r()])
        ld(self.esink, self.sink_d)
        fw.op("act", ("activation", dict(out=self.esink[:], in_=self.esink[:], func=AF.Exp)), reads=[self.esink.r()], writes=[self.esink.r()])
        xT, ident = self.xT, self.ident
        for blk in range(NTOK // 128):
            xi = self.xin[blk % 2]
            fw.dma("sp", ("dma_start", dict(out=xi[:], in_=self.x_in[blk * 128:(blk + 1) * 128, :])),
                   reads=[self.x_in.r()], writes=[xi.r()])
            for k4 in range(2):
                ps = self.P[self.nxt("pa", 4)]
                for kk in range(4):
                    k = k4 * 4 + kk
                    fw.op("pe", ("transpose", dict(out=ps[:, kk * 128:(kk + 1) * 128], in_=xi[:, k * 128:(k + 1) * 128], identity=ident[:])),
                          reads=[xi.r(k * 128, (k + 1) * 128), ident.r()], writes=[ps.r(kk * 128, (kk + 1) * 128)])
                fw.op("dve", ("tensor_copy", dict(
                    out=xT[:, k4 * 4:(k4 + 1) * 4, blk * 128:(blk + 1) * 128], in_=ps[:].rearrange("p (a b) -> p a b", b=128))),
                    reads=[ps.r()], writes=[xT.r((k4 * 4 + kk) * NTOK + blk * 128, (k4 * 4 + kk) * NTOK + (blk + 1) * 128) for kk in range(4)])
        fw.op("act", ("activation", dict(out=self.csb[:], in_=self.cv[:], func=AF.Silu)), reads=[self.cv.r()], writes=[self.csb.r()])
        csb, modr = self.csb, self.modr
        for l in range(DEPTH):
            awl = self.ada_w[l].rearrange("(k p) n -> p k n", p=128)
            for pc in range(18):
                wa = self.wA[self.nxt("wA", 2)]
                fw.dma("pool", ("dma_start", dict(out=wa[:], in_=awl[:, :, pc * 512:(pc + 1) * 512])),
                       reads=[self.ada_w.r()], writes=[wa.r()])
                ps = self.P[self.nxt("pa", 4)]
                for m4 in range(4):
                    for k in range(8):
                        fw.op("pe", ("matmul", dict(out=ps[:, m4 * 2:(m4 + 1) * 2], lhsT=wa[:, k, m4 * 128:(m4 + 1) * 128], rhs=csb[:, k, :], start=(k == 0), stop=(k == 7))),
                              reads=[wa.r(), csb.r()], writes=[ps.r(m4 * 2, m4 * 2 + 2)])
                lo = (l * 72 + pc * 4) * 2
                fw.op("dve", ("tensor_tensor", dict(
                    out=modr[:, l, pc * 4:(pc + 1) * 4, :], in0=ps[:, 0:8].rearrange("p (a b) -> p a b", b=2),
                    in1=self.adab[:, l, pc * 4:(pc + 1) * 4].unsqueeze(2).to_broadcast([128, 4, 2]), op=ALU.add)),
                    reads=[ps.r(0, 8), self.adab.r()], writes=[modr.r(lo, lo + 8)])
        GS, GT, ng = self.GS, self.GT, self.ng
        for l in range(DEPTH):
            for i in range(3):
                sc = modr[:, l, (i * 3 + 1) * 8:(i * 3 + 2) * 8, :]
                gt = modr[:, l, (i * 3 + 2) * 8:(i * 3 + 3) * 8, :]
                lo = ((l * 3 + i) * 8) * 2
                fw.op("dve", ("scalar_tensor_tensor", dict(
                    out=GS[:, l, i, :, :], in0=sc, scalar=1.0, in1=ng[:, l, i, :].unsqueeze(2).to_broadcast([128, 8, 2]), op0=ALU.add, op1=ALU.mult)),
                    reads=[modr.r(), ng.r()], writes=[GS.r(lo, lo + 16)])
                fw.op("dve", ("tensor_scalar", dict(out=GT[:, l, i, :, :], in0=gt, scalar1=(1.0 if i == 1 else 0.5), scalar2=None, op0=ALU.mult)),
                      reads=[modr.r()], writes=[GT.r(lo, lo + 16)])
        fw.dma("pool", ("dma_start", dict(out=self.mbprev[:, :], in_=self.mprev_d[:])), reads=[self.mprev_d.r()], writes=[self.mbprev.r()])

    def rms_stats(self, t0, n):
        fw, xT = self.fw, self.xT
        ps = self.P[self.nxt("pa", 4)]
        for k in range(8):
            sq = self.sq[self.nxt("sq", 2)]
            fw.op("act", ("activation", dict(out=sq[:, :n], in_=xT[:, k, t0:t0 + n], func=AF.Square)),
                  reads=[xT.r(k * NTOK + t0, k * NTOK + t0 + n)], writes=[sq.r(0, n)])
            fw.op("pe", ("matmul", dict(out=ps[:, :n], lhsT=self.onesb[:], rhs=sq[:, :n], start=(k == 0), stop=(k == 7))),
                  reads=[sq.r(0, n), self.onesb.r()], writes=[ps.r(0, n)])
        rstd = self.rstd
        fw.op("act", ("activation", dict(out=rstd[:, :n], in_=ps[:, :n], func=AF.Ln, bias=self.epsc[:, 0:1], scale=1.0 / DM)),
              reads=[ps.r(0, n), self.epsc.r()], writes=[rstd.r(0, n)])
        fw.op("act", ("activation", dict(out=rstd[:, :n], in_=rstd[:, :n], func=AF.Exp, scale=-0.5)), reads=[rstd.r(0, n)], writes=[rstd.r(0, n)])

    def norm_h(self, l, i, tiles):
        fw, xT, hT = self.fw, self.xT, self.hT
        for (t0, n) in tiles:
            var = 1 if t0 == 0 else 0
            self.rms_stats(t0, n)
            for k in range(8):
                tmp = self.tmp[self.nxt("tmp", 2)]
                fw.op("dve", ("scalar_tensor_tensor", dict(
                    out=tmp[:, :n], in0=xT[:, k, t0:t0 + n], scalar=self.GS[:, l, i, k, var:var + 1], in1=self.rstd[:, :n], op0=ALU.mult, op1=ALU.mult)),
                    reads=[xT.r(k * NTOK + t0, k * NTOK + t0 + n), self.GS.r(), self.rstd.r(0, n)], writes=[tmp.r(0, n)])
                fw.op("act", ("activation", dict(
                    out=hT[:, k, t0:t0 + n], in_=tmp[:, :n], func=AF.Identity, bias=self.modr[:, l, (i * 3) * 8 + k, var:var + 1], scale=1.0)),
                    reads=[tmp.r(0, n), self.modr.r()], writes=[hT.r(k * NTOK + t0, k * NTOK + t0 + n)])

    def ffn(self, l, i, tiles, nxt_norm=None):
        fw, xT, hT = self.fw, self.xT, self.hT
        fi = 0 if i == 0 else 1
        w1v = self.w1[l, fi].rearrange("(k p) n -> p k n", p=128)

        def out_pair(prev, o):
            wb, hid, t0, n, var = prev[:5]
            po = self.P[4 + self.nxt("pb", 4)]
            for f in range(2):
                fw.op("pe", ("matmul", dict(out=po[:, :n], lhsT=wb[:, f, o * 128:(o + 1) * 128], rhs=hid[:, f, :n], start=(f == 0), stop=(f == 1))),
                      reads=[wb.r(f * 1024 + o * 128, f * 1024 + (o + 1) * 128), hid.r(f * 512, f * 512 + n)], writes=[po.r(0, n)])
            fw.op("dve", ("scalar_tensor_tensor", dict(
                out=xT[:, o, t0:t0 + n], in0=po[:, :n], scalar=self.GT[:, l, i, o, var:var + 1], in1=xT[:, o, t0:t0 + n], op0=ALU.mult, op1=ALU.add)),
                reads=[po.r(0, n), self.GT.r(), xT.r(o * NTOK + t0, o * NTOK + t0 + n)], writes=[xT.r(o * NTOK + t0, o * NTOK + t0 + n)])

        prev = None
        for g in range(11):
            wa = self.wA[self.nxt("wA", 2)]
            wb = self.wB[self.nxt("wB", 2)]
            fw.dma("pool", ("dma_start", dict(out=wa[:, :, 0:256], in_=w1v[:, :, g * 256:(g + 1) * 256])),
                   reads=[self.w1.r()], writes=[wa.r(kk * 512, kk * 512 + 256) for kk in range(8)])
            fw.dma("pool", ("dma_start", dict(out=wa[:, :, 256:512], in_=w1v[:, :, DFF + g * 256:DFF + (g + 1) * 256])),
                   reads=[self.w1.r()], writes=[wa.r(kk * 512 + 256, kk * 512 + 512) for kk in range(8)])
            fw.dma("pool", ("dma_start", dict(out=wb[:], in_=self.w2[l, fi, g * 256:(g + 1) * 256, :].rearrange("(f p) n -> p f n", p=128))),
                   reads=[self.w2.r()], writes=[wb.r()])
            for (t0, n) in tiles:
                var = 1 if t0 == 0 else 0
                hid = self.hid[self.nxt("hid", 2)]
                q = 0
                for f in range(2):
                    pg = self.P[self.nxt("pa", 4)]
                    pu = self.P[self.nxt("pa", 4)]
                    for (pp, c0, isg) in ((pg, f * 128, True), (pu, 256 + f * 128, False)):
                        for k in range(8):
                            fw.op("pe", ("matmul", dict(out=pp[:, :n], lhsT=wa[:, k, c0:c0 + 128], rhs=hT[:, k, t0:t0 + n], start=(k == 0), stop=(k == 7))),
                                  reads=[wa.r(k * 512 + c0, k * 512 + c0 + 128), hT.r(k * NTOK + t0, k * NTOK + t0 + n)], writes=[pp.r(0, n)])
                            if k % 4 == 3:
                                if prev is not None:
                                    out_pair(prev, q)
                                    if q == 7 and prev[5] and nxt_norm is not None:
                                        self.norm_h(nxt_norm[0], nxt_norm[1], [(prev[2], prev[3])])
                                q += 1
                        if isg:
                            sg = self.sg[self.nxt("sg", 2)]
                            fw.op("act", ("activation", dict(out=sg[:, :n], in_=pg[:, :n], func=AF.Silu)), reads=[pg.r(0, n)], writes=[sg.r(0, n)])
                    fw.op("dve", ("tensor_tensor", dict(out=hid[:, f, :n], in0=sg[:, :n], in1=pu[:, :n], op=ALU.mult)),
                          reads=[sg.r(0, n), pu.r(0, n)], writes=[hid.r(f * 512, f * 512 + n)])
                prev = (wb, hid, t0, n, var, g == 10)
        for o in range(8):
            out_pair(prev, o)
        if nxt_norm is not None:
            self.norm_h(nxt_norm[0], nxt_norm[1], [(prev[2], prev[3])])

    def mm8(self, ps, n, wa, c0, t0, ncol=128):
        fw, hT = self.fw, self.hT
        for k in range(8):
            fw.op("pe", ("matmul", dict(out=ps[0:ncol, :n], lhsT=wa[:, k, c0:c0 + ncol], rhs=hT[:, k, t0:t0 + n], start=(k == 0), stop=(k == 7))),
                  reads=[wa.r(k * 512 + c0, k * 512 + c0 + ncol), hT.r(k * NTOK + t0, k * NTOK + t0 + n)], writes=[ps.r(0, n)])

    def inproj(self, l, kinds):
        fw = self.fw
        GR = [(0, 512, "A0"), (512, 512, "A1"), (1024, 512, "B0"), (1536, 512, "B1"), (2048, 512, "BK"), (2560, 384, "TV"), (2944, 256, "TU")]
        GR = [g_ for k_ in kinds for g_ in GR if g_[2] == k_]
        winl = self.win[l].rearrange("(k p) n -> p k n", p=128)
        for (c0, ncl, kind) in GR:
            wa = self.wA[self.nxt("wA", 2)]
            fw.dma("pool", ("dma_start", dict(out=wa[:, :, 0:ncl], in_=winl[:, :, c0:c0 + ncl])), reads=[self.win.r()], writes=[wa.r()])
            for ti, (t0, n) in enumerate(TILES):
                if kind in ("A0", "A1"):
                    for j in range(4):
                        ps = self.P[self.nxt("pa", 4)]
                        self.mm8(ps, n, wa, j * 128, t0)
                        tmp = self.tmp[self.nxt("tmp", 2)]
                        fw.op("act", ("activation", dict(out=tmp[:, :n], in_=ps[:, :n], func=AF.Identity)), reads=[ps.r(0, n)], writes=[tmp.r(0, n)])
                        qi = (0 if kind == "A0" else 2) + j // 2
                        r0 = (j % 2) * 128
                        fw.dma("sp", ("dma_start", dict(out=self.sA[qi, r0:r0 + 128, t0:t0 + n], in_=tmp[:, :n])), reads=[tmp.r(0, n)], writes=[self.sA.r(ti, ti + 1)])
                elif kind in ("B0", "B1", "BK"):
                    fw.dma("sp", ("dma_start", dict(out=self.rc[:, :n], in_=self.ropeC[:, t0:t0 + n])), reads=[self.ropeC.r()], writes=[self.rc.r(0, n)])
                    fw.dma("sp", ("dma_start", dict(out=self.rs[:, :n], in_=self.ropeS[:, t0:t0 + n])), reads=[self.ropeS.r()], writes=[self.rs.r(0, n)])
                    for j in range(2 if kind != "BK" else 1):
                        pq = self.P[self.nxt("pa", 4)]
                        pr = self.P[self.nxt("pa", 4)]
                        self.mm8(pq, n, wa, j * 256, t0)
                        self.mm8(pr, n, wa, j * 256 + 128, t0)
                        t1 = self.tmp[0]
                        t2 = self.tmp[1]
                        fw.op("dve", ("tensor_tensor", dict(out=t1[:, :n], in0=pq[:, :n], in1=self.rc[:, :n], op=ALU.mult)), reads=[pq.r(0, n), self.rc.r(0, n)], writes=[t1.r(0, n)])
                        fw.op("dve", ("tensor_tensor", dict(out=t2[:, :n], in0=pr[:, :n], in1=self.rs[:, :n], op=ALU.mult)), reads=[pr.r(0, n), self.rs.r(0, n)], writes=[t2.r(0, n)])
                        qb = self.ob[self.nxt("ob", 2)]
                        fw.op("dve", ("tensor_tensor", dict(out=qb[:, :n], in0=t1[:, :n], in1=t2[:, :n], op=ALU.add)), reads=[t1.r(0, n), t2.r(0, n)], writes=[qb.r(0, n)])
                        if kind == "BK":
                            fw.dma("sp", ("dma_start", dict(out=self.sK[:, t0:t0 + n], in_=qb[:, :n])), reads=[qb.r(0, n)], writes=[self.sK.r(ti, ti + 1)])
                        else:
                            ch = (0 if kind == "B0" else 2) + j
                            fw.dma("sp", ("dma_start", dict(out=self.sQ[ch * 128:(ch + 1) * 128, t0:t0 + n], in_=qb[:, :n])), reads=[qb.r(0, n)], writes=[self.sQ.r(ti, ti + 1)])
                    if kind == "BK":
                        for j in range(2):
                            ps = self.P[self.nxt("pa", 4)]
                            self.mm8(ps, n, wa, 256 + j * 128, t0)
                            tmp = self.tmp[self.nxt("tmp", 2)]
                            fw.op("act", ("activation", dict(out=tmp[:, :n], in_=ps[:, :n], func=AF.Identity)), reads=[ps.r(0, n)], writes=[tmp.r(0, n)])
                            fw.dma("sp", ("dma_start", dict(out=self.sU[j * 128:(j + 1) * 128, t0:t0 + n], in_=tmp[:, :n])), reads=[tmp.r(0, n)], writes=[self.sU.r(ti, ti + 1)])
                else:
                    hT = self.hT
                    for blk in range(n // 128):
                        tb = t0 + blk * 128
                        ps = self.P[self.nxt("pa", 4)]
                        for k in range(8):
                            fw.op("pe", ("matmul", dict(out=ps[:, :ncl], lhsT=hT[:, k, tb:tb + 128], rhs=wa[:, k, 0:ncl], start=(k == 0), stop=(k == 7))),
                                  reads=[wa.r(k * 512, k * 512 + ncl), hT.r(k * NTOK + tb, k * NTOK + tb + 128)], writes=[ps.r(0, ncl)])
                        ob = self.ob[self.nxt("ob", 2)]
                        fw.op("act", ("activation", dict(out=ob[:, :ncl], in_=ps[:, :ncl], func=AF.Identity)), reads=[ps.r(0, ncl)], writes=[ob.r(0, ncl)])
                        if kind == "TV":
                            fw.dma("sp", ("dma_start", dict(out=self.sAv[tb:tb + 128, :], in_=ob[:, 0:256])), reads=[ob.r(0, 256)], writes=[self.sAv.r(ti, ti + 1)])
                            fw.dma("sp", ("dma_start", dict(out=self.sV[tb:tb + 128, :], in_=ob[:, 256:384])), reads=[ob.r(256, 384)], writes=[self.sV.r(ti, ti + 1)])
                        else:
                            fw.dma("sp", ("dma_start", dict(out=self.sUt[tb:tb + 128, :], in_=ob[:, 0:256])), reads=[ob.r(0, 256)], writes=[self.sUt.r(ti, ti + 1)])
                yield


    def hgrn(self, l):
        fw = self.fw
        W1, W2, W3, W4 = self.HW
        W5, Qb = self.xin[0], self.xin[1]
        QE, KE, Vt, S, T1, SP, KT, SC = self.HQE, self.HKE, self.HV, self.HS, self.HT1, self.HSP, self.HKT, self.HSC
        M1, EM, ET, E2 = self.HM1, self.HEM, self.HET, self.HE2

        def v3(b):
            return b[0:64, :].rearrange("p (a j) -> p a j", j=32)

        def o(q, name, reads, writes, **kw):
            fw.op(q, (name, kw), reads=reads, writes=writes)

        for dirn in range(2):
            o("dve", "memset", [], [S.r()], ap=S[:, :], constant=0.0)
            segs = list(range(9)) if dirn == 0 else [0] + list(range(8, 0, -1))
            lbb = self.LB[:, l, dirn * 4:(dirn + 1) * 4].unsqueeze(2).to_broadcast([64, 4, 256])
            omb = self.OML[:, l, dirn * 4:(dirn + 1) * 4].unsqueeze(2).to_broadcast([64, 4, 256])
            for sgi in segs:
                t0 = sgi * 256
                ti = 0 if sgi == 0 else 1 + (sgi - 1) // 2
                fw.dma("sp", ("dma_start", dict(out=W1[:, :].rearrange("p (h t) -> p h t", h=4), in_=self.sA[1 + dirn, :, t0:t0 + 256].rearrange("(h d) t -> d h t", d=64))),
                       reads=[self.sA.r()], writes=[W1.r()])
                fw.dma("sp", ("dma_start", dict(out=Qb[0:64, :].rearrange("p (h t) -> p h t", h=4), in_=self.sA[0, :, t0:t0 + 256].rearrange("(h d) t -> d h t", d=64))),
                       reads=[self.sA.r()], writes=[Qb.r()])
                fw.dma("sp", ("dma_start", dict(out=Vt[:, :, :], in_=self.sAv[t0:t0 + 256, :].rearrange("(c s) v -> s c v", s=32))),
                       reads=[self.sAv.r()], writes=[Vt.r()])
                w1h = W1[:, :].rearrange("p (h t) -> p h t", h=4)
                o("act", "activation", [W1.r()], [W1.r()], out=W1[:, :], in_=W1[:, :], func=AF.Sigmoid)
                for h in range(4):
                    o("act", "activation", [W1.r(), self.OML.r(), self.LB.r()], [W1.r()], out=W1[:, h * 256:(h + 1) * 256], in_=W1[:, h * 256:(h + 1) * 256], func=AF.Identity,
                      scale=self.OML[:, l, dirn * 4 + h:dirn * 4 + h + 1], bias=self.LB[:, l, dirn * 4 + h:dirn * 4 + h + 1])
                o("act", "activation", [W1.r()], [W2.r()], out=W2[:, :], in_=W1[:, :], func=AF.Ln)
                o("act", "activation", [W1.r(), self.onec.r()], [W3.r()], out=W3[:, :], in_=W1[:, :], func=AF.Identity, scale=-1.0, bias=self.onec[0:64, 0:1])
                o("dve", "tensor_tensor_scan", [self.RM.r(), W2.r()], [W4.r()], out=W4[:, :], data0=self.RM[:, :], data1=W2[:, :], initial=0.0, op0=ALU.mult, op1=ALU.add)
                if dirn == 0:
                    o("dve", "tensor_copy", [W4.r()], [M1.r()], out=M1[:, :].unsqueeze(2), in_=v3(W4)[:, :, 15:16])
                    o("dve", "scalar_tensor_tensor", [M1.r(), W4.r()], [W5.r()], out=v3(W5), in0=v3(W4), scalar=1.0, in1=M1[:, :].unsqueeze(2).to_broadcast([64, 32, 32]), op0=ALU.mult, op1=ALU.subtract)
                    o("act", "activation", [M1.r()], [EM.r()], out=EM[:, :], in_=M1[:, :], func=AF.Exp)
                    o("act", "activation", [W4.r()], [ET.r()], out=ET[:, :].unsqueeze(2), in_=v3(W4)[:, :, 31:32], func=AF.Exp)
                    o("act", "activation", [W5.r()], [E2.r()], out=E2[:, :].unsqueeze(2), in_=v3(W5)[:, :, 31:32], func=AF.Exp)
                else:
                    o("dve", "tensor_tensor", [W4.r(), W2.r()], [W2.r()], out=W2[:, :], in0=W4[:, :], in1=W2[:, :], op=ALU.subtract)
                    o("dve", "tensor_copy", [W2.r()], [M1.r()], out=M1[:, :].unsqueeze(2), in_=v3(W2)[:, :, 16:17])
                    o("dve", "scalar_tensor_tensor", [M1.r(), W2.r()], [W5.r()], out=v3(W5), in0=v3(W2), scalar=-1.0, in1=M1[:, :].unsqueeze(2).to_broadcast([64, 32, 32]), op0=ALU.mult, op1=ALU.add)
                    o("act", "activation", [M1.r()], [E2.r()], out=E2[:, :], in_=M1[:, :], func=AF.Exp)
                    o("act", "activation", [W4.r()], [ET.r()], out=ET[:, :].unsqueeze(2), in_=v3(W4)[:, :, 31:32], func=AF.Exp)
                    o("dve", "tensor_tensor", [W4.r(), M1.r()], [EM.r()], out=EM[:, :].unsqueeze(2), in0=v3(W4)[:, :, 31:32], in1=M1[:, :].unsqueeze(2), op=ALU.subtract)
                    o("act", "activation", [EM.r()], [EM.r()], out=EM[:, :], in_=EM[:, :], func=AF.Exp)
                o("act", "activation", [W5.r()], [W1.r()], out=W1[:, :], in_=W5[0:64, :], func=AF.Exp)
                o("act", "activation", [W5.r()], [W2.r()], out=W2[:, :], in_=W5[0:64, :], func=AF.Exp, scale=-1.0)
                o("act", "activation", [Qb.r()], [Qb.r()], out=Qb[0:64, :], in_=Qb[0:64, :], func=AF.Silu)
                o("dve", "tensor_tensor", [Qb.r(), W1.r()], [QE.r()], out=QE[:, :], in0=Qb[0:64, :], in1=W1[:, :], op=ALU.mult)
                o("dve", "tensor_tensor", [W3.r(), W2.r()], [KE.r()], out=KE[:, :], in0=W3[:, :], in1=W2[:, :], op=ALU.mult)
                yield
                OF = W4
                if dirn == 1:
                    fw.dma("sp", ("dma_start", dict(out=OF[:, :].rearrange("p (h t) -> p h t", h=4), in_=self.sO[:, t0:t0 + 256].rearrange("(h d) t -> d h t", d=64))),
                           reads=[self.sO.r()], writes=[OF.r()])
                corder = list(range(8)) if dirn == 0 else list(range(7, -1, -1))
                OB = W5 if dirn == 1 else None
                bufs = {}
                for it in range(10):
                    cA = corder[it] if it < 8 else None
                    cU = corder[it - 1] if 1 <= it <= 8 else None
                    cB = corder[it - 2] if it >= 2 else None
                    if cA is not None:
                        par3 = self.nxt("hpar3", 3)
                        KTa, SCa = self.HKT3[par3], self.HSC3[par3]
                        bufs[cA] = (KTa, SCa)
                        psK = self.P[self.nxt("pa", 4)]
                        psS = self.P[self.nxt("pa", 4)]
                        for h in range(4):
                            cb = h * 256 + cA * 32
                            o("pe", "matmul", [KE.r(cb, cb + 32), self.identb.r()], [psK.r(h * 64, h * 64 + 64)], out=psK[0:32, h * 64:(h + 1) * 64], lhsT=KE[:, cb:cb + 32], rhs=self.identb[:, :], start=True, stop=True)
                            o("pe", "matmul", [KE.r(cb, cb + 32), QE.r(cb, cb + 32)], [psS.r(h * 32, h * 32 + 32)], out=psS[0:32, h * 32:(h + 1) * 32], lhsT=KE[:, cb:cb + 32], rhs=QE[:, cb:cb + 32], start=True, stop=True)
                    if cU is not None:
                        KTu = bufs[cU][0]
                        psU = self.P[4 + self.nxt("pb", 4)]
                        for h in range(4):
                            o("pe", "matmul", [KTu.r(), Vt.r()], [psU.r(h * 64, h * 64 + 64)], out=psU[0:64, h * 64:(h + 1) * 64], lhsT=KTu[0:32, h * 64:(h + 1) * 64], rhs=Vt[0:32, cU, h * 64:(h + 1) * 64], start=True, stop=True)
                    if cA is not None:
                        o("act", "activation", [psK.r(0, 256)], [KTa.r()], out=KTa[0:32, :], in_=psK[0:32, 0:256], func=AF.Identity)
                    if cB is not None:
                        c = cB
                        SCb = bufs.pop(c)[1]
                        emb = EM[:, :].rearrange("p (h c) -> p h c", c=8)[:, :, c:c + 1].to_broadcast([64, 4, 64])
                        etb = ET[:, :].rearrange("p (h c) -> p h c", c=8)[:, :, c:c + 1].to_broadcast([64, 4, 64])
                        s3 = S[:, :].rearrange("p (h v) -> p h v", h=4)
                        o("dve", "tensor_tensor", [S.r(), EM.r()], [SP.r()], out=SP[:, :].rearrange("p (h v) -> p h v", h=4), in0=s3, in1=emb, op=ALU.mult)
                        psO = self.P[4 + self.nxt("pb", 4)]
                        for h in range(4):
                            cb = h * 256 + c * 32
                            o("pe", "matmul", [Vt.r(), SCb.r()], [psO.r(h * 32, h * 32 + 32)], out=psO[0:64, h * 32:(h + 1) * 32], lhsT=Vt[0:32, c, h * 64:(h + 1) * 64], rhs=SCb[0:32, h * 32:(h + 1) * 32], start=True, stop=False)
                            o("pe", "matmul", [SP.r(), QE.r(cb, cb + 32)], [psO.r(h * 32, h * 32 + 32)], out=psO[0:64, h * 32:(h + 1) * 32], lhsT=SP[:, h * 64:(h + 1) * 64], rhs=QE[:, cb:cb + 32], start=False, stop=True)
                        dst = OF if dirn == 0 else OB
                        ofv = dst[0:64, :].rearrange("p (h t) -> p h t", h=4)[:, :, c * 32:(c + 1) * 32]
                        pov = psO[0:64, 0:128].rearrange("p (h t) -> p h t", h=4)
                        o("act", "activation", [psO.r(0, 128)], [dst.r()], out=ofv, in_=pov, func=AF.Identity)
                        o("dve", "tensor_tensor", [S.r(), ET.r()], [S.r()], out=s3, in0=s3, in1=etb, op=ALU.mult)
                        o("dve", "tensor_tensor", [S.r(), T1.r()], [S.r()], out=S[:, :], in0=S[:, :], in1=T1[:, :], op=ALU.add)
                    if cA is not None:
                        o("dve", "tensor_tensor", [psS.r(0, 128), self.hmask.r()], [SCa.r()], out=SCa[0:32, 0:128].rearrange("p (h t) -> p h t", h=4), in0=psS[0:32, 0:128].rearrange("p (h t) -> p h t", h=4),
                          in1=self.hmask[0:32, dirn, 0:32].unsqueeze(1).to_broadcast([32, 4, 32]), op=ALU.mult)
                    if cU is not None:
                        e2b = E2[:, :].rearrange("p (h c) -> p h c", c=8)[:, :, cU:cU + 1].to_broadcast([64, 4, 64])
                        o("dve", "tensor_tensor", [psU.r(0, 256), E2.r()], [T1.r()], out=T1[:, :].rearrange("p (h v) -> p h v", h=4), in0=psU[0:64, 0:256].rearrange("p (h v) -> p h v", h=4), in1=e2b, op=ALU.mult)
                    yield
                if dirn == 1:
                    o("dve", "tensor_tensor", [OF.r(), OB.r()], [OF.r()], out=OF[:, :], in0=OF[:, :], in1=OB[0:64, :], op=ALU.add)
                if dirn == 0:
                    fw.dma("sp", ("dma_start", dict(out=self.sO[:, t0:t0 + 256].rearrange("(h d) t -> d h t", d=64), in_=OF[:, :].rearrange("p (h t) -> p h t", h=4))),
                           reads=[OF.r()], writes=[self.sO.r(ti, ti + 1)])
                else:
                    o("act", "activation", [OF.r()], [QE.r()], out=QE[:, :], in_=OF[:, :], func=AF.Square)
                    for hf in range(2):
                        psR = self.P[self.nxt("pa", 4)]
                        o("pe", "matmul", [self.onesb.r(), QE.r()], [psR.r()], out=psR[0:64, :], lhsT=self.onesb[0:64, 0:64], rhs=QE[:, hf * 512:(hf + 1) * 512], start=True, stop=True)
                        o("act", "activation", [psR.r(), self.epsc.r()], [W1.r(hf * 512, hf * 512 + 512)], out=W1[:, hf * 512:(hf + 1) * 512], in_=psR[0:64, :], func=AF.Ln, bias=self.epsc[0:64, 0:1], scale=1.0 / 64)
                    o("act", "activation", [W1.r()], [W1.r()], out=W1[:, :], in_=W1[:, :], func=AF.Exp, scale=-0.5)
                    o("dve", "tensor_tensor", [OF.r(), W1.r()], [OF.r()], out=OF[:, :], in0=OF[:, :], in1=W1[:, :], op=ALU.mult)
                    ofh = OF[:, :].rearrange("p (h t) -> p h t", h=4)
                    o("dve", "tensor_tensor", [OF.r(), self.hng.r()], [OF.r()], out=ofh, in0=ofh, in1=self.hng[:, l, :].unsqueeze(2).to_broadcast([64, 4, 256]), op=ALU.mult)
                    fw.dma("sp", ("dma_start", dict(out=W3[:, :].rearrange("p (h t) -> p h t", h=4), in_=self.sA[3, :, t0:t0 + 256].rearrange("(h d) t -> d h t", d=64))),
                           reads=[self.sA.r()], writes=[W3.r()])
                    o("act", "activation", [W3.r()], [W3.r()], out=W3[:, :], in_=W3[:, :], func=AF.Silu)
                    o("dve", "tensor_tensor", [OF.r(), W3.r()], [KE.r()], out=KE[:, :], in0=OF[:, :], in1=W3[:, :], op=ALU.mult)
                    fw.dma("sp", ("dma_start", dict(out=self.mix[0:256, t0:t0 + 256].rearrange("(h d) t -> d h t", d=64), in_=KE[:, :].rearrange("p (h t) -> p h t", h=4))),
                           reads=[KE.r()], writes=[self.mix.r(ti, ti + 1)])
                yield


    def s5(self, l):
        fw = self.fw
        PI = float(np.pi)

        def o(q, name, reads, writes, **kw):
            fw.op(q, (name, kw), reads=reads, writes=writes)

        class Sl:
            def __init__(s_, buf, lo, n=256):
                s_.buf, s_.lo, s_.n = buf, lo, n
                s_.ap = buf[:, lo:lo + n]
                s_.rg = buf.r(lo, lo + n)

            def v4(s_):
                return s_.ap.rearrange("p (a b) -> p a b", b=64)

        pool = []
        for b, n in ((self.s5w[0], 512), (self.s5w[1], 512), (self.s5w[2], 512), (self.s5w[3], 512),
                     (self.rc, 512), (self.rs, 512), (self.rstd, 512)):
            for lo in range(0, n, 256):
                pool.append(Sl(b, lo))
        AR, AI, BQ, AD, ANG, Y, CN, T, U, RDEN, CR, CI, T2, MAG = pool[:14]

        def tt(out, a, b_, op, extra_r=()):
            o("dve", "tensor_tensor", [a.rg, b_.rg] + list(extra_r), [out.rg], out=out.ap, in0=a.ap, in1=b_.ap, op=op)

        def reduce_angle(xap, xregs, shift, out_ap, out_regs, y_ap, y_regs, c_ap, c_regs, t_ap, t_regs):
            o("dve", "tensor_scalar", xregs, y_regs, out=y_ap, in0=xap, scalar1=shift, scalar2=None, op0=ALU.add)
            o("dve", "tensor_scalar", y_regs, c_regs, out=c_ap, in0=y_ap, scalar1=PI, scalar2=None, op0=ALU.is_gt)
            for thr in (3 * PI, 5 * PI, 7 * PI):
                o("dve", "tensor_scalar", y_regs, t_regs, out=t_ap, in0=y_ap, scalar1=thr, scalar2=None, op0=ALU.is_gt)
                o("dve", "tensor_tensor", c_regs + t_regs, c_regs, out=c_ap, in0=c_ap, in1=t_ap, op=ALU.add)
            o("dve", "scalar_tensor_tensor", c_regs + y_regs, out_regs, out=out_ap, in0=c_ap, scalar=-2.0 * PI, in1=y_ap, op0=ALU.mult, op1=ALU.add)

        fw.dma("sp", ("dma_start", dict(out=AR.v4(), in_=self.s5Q[:, l, 0])), reads=[self.s5Q.r()], writes=[AR.rg])
        fw.dma("sp", ("dma_start", dict(out=AI.v4(), in_=self.s5Q[:, l, 1])), reads=[self.s5Q.r()], writes=[AI.rg])
        fw.dma("sp", ("dma_start", dict(out=BQ.v4(), in_=self.s5B[:, l].rearrange("p a h d -> p (a h) d"))), reads=[self.s5B.r()], writes=[BQ.rg])
        fw.dma("sp", ("dma_start", dict(out=self.sAs[:], in_=self.s5A[:, l])), reads=[self.s5A.r()], writes=[self.sAs.r()])
        fw.dma("pool", ("dma_start", dict(out=self.Cblk[:], in_=self.s5C[:, l])), reads=[self.s5C.r()], writes=[self.Cblk.r()])
        fw.dma("pool", ("dma_start", dict(out=self.gluwS[:], in_=self.gluw_d[l].rearrange("(h p) n -> p h n", p=128))), reads=[self.gluw_d.r()], writes=[self.gluwS.r()])
        o("dve", "tensor_scalar", [self.Cblk.r()], [self.Cblk.r()], out=self.Cblk[:, 1], in0=self.Cblk[:, 1], scalar1=-1.0, scalar2=None, op0=ALU.mult)

        dtq = self.dtq
        o("act", "activation", [self.ldq.r()], [dtq.r()], out=dtq[:, :], in_=self.ldq[:, l, :], func=AF.Exp)
        dtb = dtq[:, :].unsqueeze(2).to_broadcast([128, 4, 64])
        o("dve", "tensor_tensor", [AR.rg, dtq.r()], [AD.rg], out=AD.v4(), in0=AR.v4(), in1=dtb, op=ALU.mult)
        o("act", "activation", [AD.rg], [MAG.rg], out=MAG.ap, in_=AD.ap, func=AF.Exp)
        o("dve", "tensor_tensor", [AI.rg, dtq.r()], [ANG.rg], out=ANG.v4(), in0=AI.v4(), in1=dtb, op=ALU.mult)
        SN, CS = AD, U
        reduce_angle(ANG.ap, [ANG.rg], 0.0, SN.ap, [SN.rg], Y.ap, [Y.rg], CN.ap, [CN.rg], T.ap, [T.rg])
        o("act", "activation", [SN.rg], [SN.rg], out=SN.ap, in_=SN.ap, func=AF.Sin)
        reduce_angle(ANG.ap, [ANG.rg], PI / 2, CS.ap, [CS.rg], Y.ap, [Y.rg], CN.ap, [CN.rg], T.ap, [T.rg])
        o("act", "activation", [CS.rg], [CS.rg], out=CS.ap, in_=CS.ap, func=AF.Sin)
        ABR, ABI = CS, SN
        tt(ABR, MAG, CS, ALU.mult)
        tt(ABI, MAG, SN, ALU.mult)
        tt(T, AR, AR, ALU.mult)
        tt(Y, AI, AI, ALU.mult)
        tt(T, T, Y, ALU.add)
        o("dve", "reciprocal", [T.rg], [RDEN.rg], out=RDEN.ap, in_=T.ap)
        o("dve", "tensor_scalar", [ABR.rg], [ABR.rg], out=ABR.ap, in0=ABR.ap, scalar1=-1.0, scalar2=None, op0=ALU.add)
        tt(T, ABR, AR, ALU.mult)
        tt(Y, ABI, AI, ALU.mult)
        tt(T, T, Y, ALU.add)
        tt(CR, T, RDEN, ALU.mult)
        tt(T2, ABI, AR, ALU.mult)
        tt(Y, ABR, AI, ALU.mult)
        tt(T2, T2, Y, ALU.subtract)
        tt(CI, T2, RDEN, ALU.mult)
        BB = self.BB
        cr3 = CR.ap.rearrange("p (d x) -> p d x", d=2)
        ci3 = CI.ap.rearrange("p (d x) -> p d x", d=2)
        bre = BQ.ap[:, 0:128].unsqueeze(1).to_broadcast([128, 2, 128])
        bim = BQ.ap[:, 128:256].unsqueeze(1).to_broadcast([128, 2, 128])
        t3 = T.ap.rearrange("p (d x) -> p d x", d=2)
        y3 = Y.ap.rearrange("p (d x) -> p d x", d=2)
        bbr = BB[:, 0].rearrange("p a b -> p (a b)").rearrange("p (d x) -> p d x", d=2)
        bbi = BB[:, 1].rearrange("p a b -> p (a b)").rearrange("p (d x) -> p d x", d=2)
        o("dve", "tensor_tensor", [CR.rg, BQ.rg], [T.rg], out=t3, in0=cr3, in1=bre, op=ALU.mult)
        o("dve", "tensor_tensor", [CI.rg, BQ.rg], [Y.rg], out=y3, in0=ci3, in1=bim, op=ALU.mult)
        o("dve", "tensor_tensor", [T.rg, Y.rg], [BB.r(0, 256)], out=bbr, in0=t3, in1=y3, op=ALU.subtract)
        o("dve", "tensor_tensor", [CR.rg, BQ.rg], [T.rg], out=t3, in0=cr3, in1=bim, op=ALU.mult)
        o("dve", "tensor_tensor", [CI.rg, BQ.rg], [Y.rg], out=y3, in0=ci3, in1=bre, op=ALU.mult)
        o("dve", "tensor_tensor", [T.rg, Y.rg], [BB.r(256, 512)], out=bbi, in0=t3, in1=y3, op=ALU.add)

        As, DT, R, TH, C1, S1, Y16, N16, T16 = self.sAs, self.sDT, self.sR, self.sTH, self.sC1, self.sS1, self.sY16, self.sN16, self.sT16
        o("act", "activation", [As.r()], [DT.r()], out=DT[:, :], in_=As[:, 2, :], func=AF.Exp)
        o("dve", "tensor_tensor", [As.r(), DT.r()], [TH.r()], out=TH[:, :], in0=As[:, 0, :], in1=DT[:, :], op=ALU.mult)
        o("act", "activation", [TH.r()], [R.r()], out=R[:, :], in_=TH[:, :], func=AF.Exp)
        o("dve", "tensor_tensor", [As.r(), DT.r()], [TH.r()], out=TH[:, :], in0=As[:, 1, :], in1=DT[:, :], op=ALU.mult)
        reduce_angle(TH[:, :], [TH.r()], 0.0, S1[:, :], [S1.r()], Y16[:, :], [Y16.r()], N16[:, :], [N16.r()], T16[:, :], [T16.r()])
        o("act", "activation", [S1.r()], [S1.r()], out=S1[:, :], in_=S1[:, :], func=AF.Sin)
        reduce_angle(TH[:, :], [TH.r()], PI / 2, C1[:, :], [C1.r()], Y16[:, :], [Y16.r()], N16[:, :], [N16.r()], T16[:, :], [T16.r()])
        o("act", "activation", [C1.r()], [C1.r()], out=C1[:, :], in_=C1[:, :], func=AF.Sin)

        uTb, ytF, ytB = self.uTb, self.ytF, self.ytB
        u3 = uTb[:, :].rearrange("p (h t) -> p h t", h=2)
        fw.dma("pool", ("dma_start", dict(out=u3, in_=self.sU[:, :].rearrange("(h p) t -> p h t", p=128))), reads=[self.sU.r()], writes=[uTb.r()])
        Ct, St = self.rc, self.rs
        wsets = [self.s5w, self.s5w2]
        hsets = [(self.sq[0], self.sq[1]), (self.sq2[0], self.sq2[1])]
        tcnt = 0
        pend = None
        yield
        hp = self.hprev
        for dirn in range(2):
            yt = ytF if dirn == 0 else ytB
            yt3 = yt[:, :].rearrange("p (b c) -> p b c", c=256)
            if dirn == 1:
                for bt in range(18):
                    tb = (1 - bt) if bt < 2 else (19 - bt)
                    ub = self.s5ub[self.nxt("s5ub", 2)]
                    fw.dma("sp", ("dma_start", dict(out=ub[:, 0:256], in_=self.sUt[tb * 128:(tb + 1) * 128, :])), reads=[self.sUt.r()], writes=[ub.r(0, 256)])
                    ps = self.P[self.nxt("pa", 4)]
                    for half in range(2):
                        o("pe", "matmul", [ub.r(0, 256), self.Jb.r()], [ps.r(half * 128, half * 128 + 128)], out=ps[:, half * 128:(half + 1) * 128],
                          lhsT=ub[:, half * 128:(half + 1) * 128], rhs=self.Jb[:, :], start=True, stop=True)
                    o("act", "activation", [ps.r(0, 256)], [uTb.r(bt * 128, bt * 128 + 128), uTb.r(2304 + bt * 128, 2304 + bt * 128 + 128)],
                      out=u3[:, :, bt * 128:(bt + 1) * 128], in_=ps[:, 0:256].rearrange("p (h t) -> p h t", h=2), func=AF.Identity)
                    if bt % 3 == 2:
                        yield
            for st in range(8):
                ds = dirn * 8 + st
                half = st // 4
                hd = dirn * 2 + half
                o("act", "activation", [C1.r()], [Ct.r(0, 1)], out=Ct[:, 0:1], in_=C1[:, ds:ds + 1], func=AF.Identity)
                o("act", "activation", [S1.r()], [St.r(0, 1)], out=St[:, 0:1], in_=S1[:, ds:ds + 1], func=AF.Identity)
                gr, gi = wsets[0][2], wsets[0][3]
                m = 1
                while m < 512:
                    cm, sm = Ct[:, m - 1:m], St[:, m - 1:m]
                    o("dve", "tensor_scalar", [St.r(0, m)], [gr.r(0, m)], out=gr[:, 0:m], in0=St[:, 0:m], scalar1=sm, scalar2=None, op0=ALU.mult)
                    o("dve", "scalar_tensor_tensor", [Ct.r(0, m), gr.r(0, m)], [Ct.r(m, 2 * m)], out=Ct[:, m:2 * m], in0=Ct[:, 0:m], scalar=cm, in1=gr[:, 0:m], op0=ALU.mult, op1=ALU.subtract)
                    o("dve", "tensor_scalar", [Ct.r(0, m), St.r(0, m)], [gi.r(0, m)], out=gi[:, 0:m], in0=Ct[:, 0:m], scalar1=sm, scalar2=None, op0=ALU.mult)
                    o("dve", "scalar_tensor_tensor", [St.r(0, m), Ct.r(0, m), gi.r(0, m)], [St.r(m, 2 * m)], out=St[:, m:2 * m], in0=St[:, 0:m], scalar=cm, in1=gi[:, 0:m], op0=ALU.mult, op1=ALU.add)
                    m *= 2
                yield
                for ri in range(2):
                    for gl in range(2):
                        j = 2 * (st % 4) + gl
                        o("dve", "tensor_scalar", [BB.r(), self.rowm.r()], [self.Bblk.r(ri * 128 + gl * 64, ri * 128 + gl * 64 + 64)],
                          out=self.Bblk[:, ri, gl * 64:(gl + 1) * 64], in0=BB[:, ri, hd, :], scalar1=self.rowm[:, j:j + 1], scalar2=None, op0=ALU.mult)
                rb = R[:, ds:ds + 1]
                for ti, (t0, n) in enumerate(TILES):
                    dr, di, gr, gi = wsets[tcnt % 2]
                    hr, hi = hsets[tcnt % 2]
                    tcnt += 1
                    pr = self.P[self.nxt("pa", 4)]
                    pi_ = self.P[self.nxt("pa", 4)]
                    c0 = half * NTOK + t0
                    o("pe", "matmul", [self.Bblk.r(0, 128), uTb.r(c0, c0 + n)], [pr.r(0, n)], out=pr[:, :n], lhsT=self.Bblk[:, 0, :], rhs=uTb[:, c0:c0 + n], start=True, stop=True)
                    o("pe", "matmul", [self.Bblk.r(128, 256), uTb.r(c0, c0 + n)], [pi_.r(0, n)], out=pi_[:, :n], lhsT=self.Bblk[:, 1, :], rhs=uTb[:, c0:c0 + n], start=True, stop=True)
                    o("dve", "tensor_tensor", [pi_.r(0, n), St.r(0, n)], [gr.r(0, n)], out=gr[:, :n], in0=pi_[:, :n], in1=St[:, :n], op=ALU.mult)
                    o("dve", "tensor_tensor", [pr.r(0, n), Ct.r(0, n)], [dr.r(0, n)], out=dr[:, :n], in0=pr[:, :n], in1=Ct[:, :n], op=ALU.mult)
                    o("dve", "tensor_tensor", [dr.r(0, n), gr.r(0, n)], [dr.r(0, n)], out=dr[:, :n], in0=dr[:, :n], in1=gr[:, :n], op=ALU.add)
                    o("dve", "tensor_tensor", [pr.r(0, n), St.r(0, n)], [gi.r(0, n)], out=gi[:, :n], in0=pr[:, :n], in1=St[:, :n], op=ALU.mult)
                    o("dve", "tensor_tensor", [pi_.r(0, n), Ct.r(0, n)], [di.r(0, n)], out=di[:, :n], in0=pi_[:, :n], in1=Ct[:, :n], op=ALU.mult)
                    o("dve", "tensor_tensor", [di.r(0, n), gi.r(0, n)], [di.r(0, n)], out=di[:, :n], in0=di[:, :n], in1=gi[:, :n], op=ALU.subtract)
                    ini_r = 0.0 if ti == 0 else hp[:, 0:1]
                    ini_i = 0.0 if ti == 0 else hp[:, 1:2]
                    o("dve", "tensor_tensor_scan", [R.r(), dr.r(0, n), hp.r()], [gr.r(0, n)], out=gr[:, :n], data0=rb.to_broadcast([128, n]), data1=dr[:, :n], initial=ini_r, op0=ALU.mult, op1=ALU.add)
                    o("dve", "tensor_tensor_scan", [R.r(), di.r(0, n), hp.r()], [gi.r(0, n)], out=gi[:, :n], data0=rb.to_broadcast([128, n]), data1=di[:, :n], initial=ini_i, op0=ALU.mult, op1=ALU.add)
                    if ti < len(TILES) - 1:
                        hpt = self.hpt
                        cl, sl = Ct[:, n - 1:n], St[:, n - 1:n]
                        o("dve", "tensor_scalar", [gi.r(0, n), St.r(0, n)], [hpt.r(0, 1)], out=hpt[:, 0:1], in0=gi[:, n - 1:n], scalar1=sl, scalar2=None, op0=ALU.mult)
                        o("dve", "tensor_scalar", [gr.r(0, n), St.r(0, n)], [hpt.r(1, 2)], out=hpt[:, 1:2], in0=gr[:, n - 1:n], scalar1=sl, scalar2=None, op0=ALU.mult)
                        o("dve", "scalar_tensor_tensor", [gr.r(0, n), Ct.r(0, n), hpt.r(0, 1)], [hp.r(0, 1)], out=hp[:, 0:1], in0=gr[:, n - 1:n], scalar=cl, in1=hpt[:, 0:1], op0=ALU.mult, op1=ALU.subtract)
                        o("dve", "scalar_tensor_tensor", [gi.r(0, n), Ct.r(0, n), hpt.r(1, 2)], [hp.r(1, 2)], out=hp[:, 1:2], in0=gi[:, n - 1:n], scalar=cl, in1=hpt[:, 1:2], op0=ALU.mult, op1=ALU.add)
                    o("pool", "tensor_tensor", [gi.r(0, n), St.r(0, n)], [dr.r(0, n)], out=dr[:, :n], in0=gi[:, :n], in1=St[:, :n], op=ALU.mult)
                    o("pool", "tensor_tensor", [gr.r(0, n), St.r(0, n)], [di.r(0, n)], out=di[:, :n], in0=gr[:, :n], in1=St[:, :n], op=ALU.mult)
                    o("pool", "tensor_tensor", [gr.r(0, n), Ct.r(0, n)], [gr.r(0, n)], out=gr[:, :n], in0=gr[:, :n], in1=Ct[:, :n], op=ALU.mult)
                    o("pool", "tensor_tensor", [gi.r(0, n), Ct.r(0, n)], [gi.r(0, n)], out=gi[:, :n], in0=gi[:, :n], in1=Ct[:, :n], op=ALU.mult)
                    o("pool", "tensor_tensor", [gr.r(0, n), dr.r(0, n)], [hr.r(0, n)], out=hr[:, :n], in0=gr[:, :n], in1=dr[:, :n], op=ALU.subtract)
                    o("pool", "tensor_tensor", [gi.r(0, n), di.r(0, n)], [hi.r(0, n)], out=hi[:, :n], in0=gi[:, :n], in1=di[:, :n], op=ALU.add)
                    def readout(hr=hr, hi=hi, n=n, t0=t0, ds=ds, st=st, yt=yt, yt3=yt3):
                        py = self.P[4 + self.nxt("pb", 4)]
                        nb, b0 = n // 128, t0 // 128
                        for j in range(nb):
                            o("pe", "matmul", [hr.r(j * 128, j * 128 + 128), self.Cblk.r()], [py.r(j * 32, j * 32 + 32)], out=py[:, j * 32:(j + 1) * 32],
                              lhsT=hr[:, j * 128:(j + 1) * 128], rhs=self.Cblk[:, 0, ds, :], start=True, stop=False)
                            o("pe", "matmul", [hi.r(j * 128, j * 128 + 128), self.Cblk.r()], [py.r(j * 32, j * 32 + 32)], out=py[:, j * 32:(j + 1) * 32],
                              lhsT=hi[:, j * 128:(j + 1) * 128], rhs=self.Cblk[:, 1, ds, :], start=False, stop=True)
                        o("act", "activation", [py.r(0, nb * 32)], [yt.r((b0 + j) * 256 + st * 32, (b0 + j) * 256 + st * 32 + 32) for j in range(nb)],
                          out=yt3[:, b0:b0 + nb, st * 32:(st + 1) * 32], in_=py[:, 0:nb * 32].rearrange("p (b c) -> p b c", c=32), func=AF.Identity)

                    if pend is not None:
                        pend()
                    pend = readout
                    yield
        pend()
        ytF3 = ytF[:, :].rearrange("p (b c) -> p b c", c=256)
        ytB3 = ytB[:, :].rearrange("p (b c) -> p b c", c=256)
        for ti, (t0, n) in enumerate(TILES):
            nb, b0 = n // 128, t0 // 128
            gys = []
            for half in range(2):
                pc = self.P[self.nxt("pa", 4)]
                for j in range(nb):
                    tb = b0 + j
                    bt = (1 - tb) if tb < 2 else (19 - tb)
                    o("pe", "matmul", [ytF.r(tb * 256, tb * 256 + 256), self.Ib.r()], [pc.r(j * 128, j * 128 + 128)], out=pc[:, j * 128:(j + 1) * 128],
                      lhsT=ytF3[:, tb, half * 128:(half + 1) * 128], rhs=self.Ib[:, :], start=True, stop=False)
                    o("pe", "matmul", [ytB.r(bt * 256, bt * 256 + 256), self.Jb.r()], [pc.r(j * 128, j * 128 + 128)], out=pc[:, j * 128:(j + 1) * 128],
                      lhsT=ytB3[:, bt, half * 128:(half + 1) * 128], rhs=self.Jb[:, :], start=False, stop=True)
                ut = self.s5w[half]
                fw.dma("sp", ("dma_start", dict(out=ut[:, :n], in_=self.sU[half * 128:(half + 1) * 128, t0:t0 + n])), reads=[self.sU.r()], writes=[ut.r(0, n)])
                yv = self.s5w[2 + half]
                o("dve", "scalar_tensor_tensor", [ut.r(0, n), self.s5dS.r(), pc.r(0, n)], [yv.r(0, n)], out=yv[:, :n], in0=ut[:, :n], scalar=self.s5dS[:, l, half:half + 1], in1=pc[:, :n], op0=ALU.mult, op1=ALU.add)
                o("dve", "tensor_tensor", [yv.r(0, n)], [ut.r(0, n)], out=ut[:, :n], in0=yv[:, :n], in1=yv[:, :n], op=ALU.mult)
                o("dve", "tensor_scalar", [ut.r(0, n)], [ut.r(0, n)], out=ut[:, :n], in0=ut[:, :n], scalar1=0.044715, scalar2=1.0, op0=ALU.mult, op1=ALU.add)
                o("dve", "tensor_tensor", [ut.r(0, n), yv.r(0, n)], [ut.r(0, n)], out=ut[:, :n], in0=ut[:, :n], in1=yv[:, :n], op=ALU.mult)
                o("act", "activation", [ut.r(0, n)], [ut.r(0, n)], out=ut[:, :n], in_=ut[:, :n], func=AF.Sigmoid, scale=1.5957691216057308)
                gy = self.s5gy[half]
                o("dve", "tensor_tensor", [ut.r(0, n), yv.r(0, n)], [gy.r(0, n)], out=gy[:, :n], in0=ut[:, :n], in1=yv[:, :n], op=ALU.mult)
                gys.append(gy)
            pm = [self.P[4 + m_] for m_ in range(4)]
            for m_ in range(4):
                for half in range(2):
                    o("pe", "matmul", [self.gluwS.r(), gys[half].r(0, n)], [pm[m_].r(0, n)], out=pm[m_][:, :n],
                      lhsT=self.gluwS[:, half, m_ * 128:(m_ + 1) * 128], rhs=gys[half][:, :n], start=(half == 0), stop=(half == 1))
            for mm in range(2):
                sgm = self.rstd
                o("act", "activation", [pm[2 + mm].r(0, n), self.glubS.r()], [sgm.r(0, n)], out=sgm[:, :n], in_=pm[2 + mm][:, :n], func=AF.Sigmoid, bias=self.glubS[:, l, 2 + mm:3 + mm], scale=1.0)
                ob = self.s5ob
                o("dve", "scalar_tensor_tensor", [pm[mm].r(0, n), self.glubS.r(), sgm.r(0, n)], [ob.r(0, n)], out=ob[:, :n], in0=pm[mm][:, :n], scalar=self.glubS[:, l, mm:mm + 1], in1=sgm[:, :n], op0=ALU.add, op1=ALU.mult)
                fw.dma("sp", ("dma_start", dict(out=self.mix[768 + mm * 128:768 + (mm + 1) * 128, t0:t0 + n], in_=ob[:, :n])), reads=[ob.r(0, n)], writes=[self.mix.r(16 + ti, 17 + ti)])
            yield

    def zero_mix(self, r0, r1):
        fw = self.fw
        fw.op("dve", ("memset", dict(ap=self.zt[:], constant=0.0)), writes=[self.zt.r()])
        for rr in range(r0, r1, 128):
            for ti, (t0, n) in enumerate(TILES):
                fw.dma("sp", ("dma_start", dict(out=self.mix[rr:rr + 128, t0:t0 + n], in_=self.zt[:, :n])), reads=[self.zt.r()], writes=[self.mix.r(ti, ti + 1)])

    def attn(self, l):
        fw = self.fw
        fw.dma("sp", ("dma_start", dict(out=self.kTall[:], in_=self.sK[:, :])), reads=[self.sK.r()], writes=[self.kTall.r()])
        fw.dma("sp", ("dma_start", dict(out=self.vt[:], in_=self.sV[:, :].rearrange("(b p) c -> p b c", p=128))), reads=[self.sV.r()], writes=[self.vt.r()])
        fw.dma("pool", ("dma_start", dict(out=self.mbnext[:, :], in_=self.mnext_d[:])), reads=[self.mnext_d.r()], writes=[self.mbnext.r()])
        units = [(hk, qb) for hk in range(2) for qb in range(18)]

        def load_q(u):
            hk, qb = units[u]
            qt = self.qt[u % 2]
            fw.dma("sp", ("dma_start", dict(out=qt[hk * 64:hk * 64 + 64, :].rearrange("d (h t) -> d h t", h=4),
                                             in_=self.sQ[hk * 256:(hk + 1) * 256, qb * 128:qb * 128 + 128].rearrange("(h d) t -> d h t", d=64))),
                   reads=[self.sQ.r()], writes=[qt.r()])

        def finish(u):
            hk, qb = units[u]
            ob, den = self.ob[u % 2], self.tmp[u % 2]
            fw.op("dve", ("tensor_tensor", dict(out=ob[0:64, :], in0=ob[0:64, :], in1=den[0:64, :], op=ALU.mult)), reads=[ob.r(), den.r()], writes=[ob.r()])
            ti = 0 if qb < 2 else 1 + (qb - 2) // 4
            fw.dma("sp", ("dma_start", dict(out=self.mix[256 + hk * 256:256 + (hk + 1) * 256, qb * 128:qb * 128 + 128].rearrange("(h d) t -> d h t", d=64),
                                             in_=ob[0:64, :].rearrange("d (h t) -> d h t", h=4))),
                   reads=[ob.r()], writes=[self.mix.r(8 + ti, 9 + ti)])

        load_q(0)
        for u, (hk, qb) in enumerate(units):
            if u + 1 < len(units):
                load_q(u + 1)
            if u > 0:
                finish(u - 1)
            qt = self.qt[u % 2]
            pb_ = hk * 64
            kbs = [(0, None), (1, None)]
            if qb >= 2:
                for dl, mk in ((-1, self.mbprev), (0, None), (1, self.mbnext)):
                    kb = qb + dl
                    if 2 <= kb <= 17:
                        kbs.append((kb, mk))
            pp = self.nxt("pb", 2)
            po, pd = self.P[4 + 2 * pp], self.P[5 + 2 * pp]
            for idx, (kb, mk) in enumerate(kbs):
                ps = self.P[self.nxt("pa", 4)]
                fw.op("pe", ("matmul", dict(out=ps[:, :], lhsT=self.kTall[pb_:pb_ + 64, kb * 128:(kb + 1) * 128], rhs=qt[pb_:pb_ + 64, :], start=True, stop=(mk is None))),
                      reads=[self.kTall.r(kb * 128, (kb + 1) * 128), qt.r()], writes=[ps.r()])
                if mk is not None:
                    fw.op("pe", ("matmul", dict(out=ps[:, :], lhsT=self.Ib[:, :], rhs=mk[:, :], start=False, stop=True)),
                          reads=[self.Ib.r(), mk.r()], writes=[ps.r()])
                pT = self.pT[self.nxt("pT", 3)]
                fw.op("act", ("activation", dict(out=pT[:, :], in_=ps[:, :], func=AF.Exp, scale=0.125)), reads=[ps.r()], writes=[pT.r()])
                st, sp_ = (idx == 0), (idx == len(kbs) - 1)
                fw.op("pe", ("matmul", dict(out=po[0:64, :], lhsT=self.vt[:, kb, hk * 64:(hk + 1) * 64], rhs=pT[:, :], start=st, stop=sp_)),
                      reads=[self.vt.r(), pT.r()], writes=[po.r()])
                fw.op("pe", ("matmul", dict(out=pd[0:64, :], lhsT=self.onesb[:, 0:64], rhs=pT[:, :], start=st, stop=sp_)),
                      reads=[self.onesb.r(), pT.r()], writes=[pd.r()])
            ob, den = self.ob[u % 2], self.tmp[u % 2]
            for h in range(4):
                fw.op("act", ("activation", dict(out=den[0:64, h * 128:(h + 1) * 128], in_=pd[0:64, h * 128:(h + 1) * 128], func=AF.Ln,
                                                 bias=self.esink[:, l, hk * 4 + h:hk * 4 + h + 1], scale=1.0)),
                      reads=[pd.r(), self.esink.r()], writes=[den.r()])
            fw.op("act", ("activation", dict(out=den[0:64, :], in_=den[0:64, :], func=AF.Exp, scale=-1.0)), reads=[den.r()], writes=[den.r()])
            fw.op("act", ("activation", dict(out=ob[0:64, :], in_=po[0:64, :], func=AF.Identity)), reads=[po.r()], writes=[ob.r()])
            yield
        finish(len(units) - 1)
        yield

    def outproj(self, l, tiles, load=True):
        fw, xT, hT = self.fw, self.xT, self.hT
        wv = self.wout[l].rearrange("(k p) n -> p k n", p=128)
        for c in (range(2) if load else ()):
            fw.dma("pool", ("dma_start", dict(out=self.wA[c][:], in_=wv[:, :, c * 512:(c + 1) * 512])), reads=[self.wout.r()], writes=[self.wA[c].r()])
        for (t0, n) in tiles:
            var = 1 if t0 == 0 else 0
            fw.dma("sp", ("dma_start", dict(out=hT[:, :, t0:t0 + n], in_=self.mix[:, t0:t0 + n].rearrange("(k p) t -> p k t", p=128))),
                   reads=[self.mix.r()], writes=[hT.r(k * NTOK + t0, k * NTOK + t0 + n) for k in range(8)])
            for o in range(8):
                po = self.P[4 + self.nxt("pb", 4)]
                self.mm8(po, n, self.wA[o // 4], (o % 4) * 128, t0)
                fw.op("dve", ("scalar_tensor_tensor", dict(
                    out=xT[:, o, t0:t0 + n], in0=po[:, :n], scalar=self.GT[:, l, 1, o, var:var + 1], in1=xT[:, o, t0:t0 + n], op0=ALU.mult, op1=ALU.add)),
                    reads=[po.r(0, n), self.GT.r(), xT.r(o * NTOK + t0, o * NTOK + t0 + n)], writes=[xT.r(o * NTOK + t0, o * NTOK + t0 + n)])

    def final(self):
        fw, xT = self.fw, self.xT
        for (t0, n) in TILES[1:]:
            self.rms_stats(t0, n)
            for sb4 in range(n // 128):
                ot = self.xin[self.nxt("xin", 2)]
                tb = t0 + sb4 * 128
                for k4 in range(2):
                    ps = self.P[self.nxt("pa", 4)]
                    for kk in range(4):
                        k = k4 * 4 + kk
                        tmp = self.tmp[self.nxt("tmp", 2)]
                        fw.op("dve", ("scalar_tensor_tensor", dict(
                            out=tmp[:, :128], in0=xT[:, k, tb:tb + 128], scalar=self.fngs[:, k:k + 1], in1=self.rstd[:, sb4 * 128:(sb4 + 1) * 128], op0=ALU.mult, op1=ALU.mult)),
                            reads=[xT.r(k * NTOK + tb, k * NTOK + tb + 128), self.fngs.r(), self.rstd.r(0, n)], writes=[tmp.r(0, 128)])
                        fw.op("pe", ("transpose", dict(out=ps[:, kk * 128:(kk + 1) * 128], in_=tmp[:, :128], identity=self.ident[:])),
                              reads=[tmp.r(0, 128), self.ident.r()], writes=[ps.r(kk * 128, (kk + 1) * 128)])
                    fw.op("act", ("activation", dict(out=ot[:, k4 * 512:(k4 + 1) * 512], in_=ps[:], func=AF.Identity)),
                          reads=[ps.r()], writes=[ot.r(k4 * 512, (k4 + 1) * 512)])
                r0 = tb - NCTX
                fw.dma("sp", ("dma_start", dict(out=self.y[r0:r0 + 128, :], in_=ot[:])), reads=[ot.r()], writes=[self.y.r()])

    def build(self):
        self.prologue()
        last = DEPTH - 1
        for l in range(DEPTH):
            if self.stage < 1:
                break
            if l == 0 or not PIPE_NORM:
                self.norm_h(l, 0, TILES)
            pn = PIPE_NORM and self.stage >= 5
            self.ffn(l, 0, TILES, nxt_norm=(l, 1) if pn else None)
            if self.stage < 2:
                break
            tl = TILES if l < last else TILES[1:]
            if self.stage >= 3:
                if not pn:
                    self.norm_h(l, 1, TILES)
                for _ in self.inproj(l, ["A0", "A1", "TV"]):
                    pass
                rest = self.inproj(l, ["BK", "B0", "B1", "TU"])
                if self.stage >= 4 and INTERLEAVE:
                    hg_early = self.hgrn(l)
                    for _ in rest:
                        next(hg_early)
                else:
                    hg_early = None
                    for _ in rest:
                        pass
                gens = [self.attn(l)]
                if self.stage < 4:
                    self.zero_mix(0, 256)
                else:
                    gens.append(hg_early if hg_early is not None else self.hgrn(l))
                if self.stage < 5:
                    self.zero_mix(768, 1024)
                else:
                    gens.append(self.s5(l))
                if not INTERLEAVE:
                    for g in gens:
                        for _ in g:
                            pass
                else:
                    while gens:
                        for g in list(gens):
                            try:
                                next(g)
                            except StopIteration:
                                gens.remove(g)
                if pn:
                    for j_, tile_ in enumerate(tl):
                        self.outproj(l, [tile_], load=(j_ == 0))
                        self.norm_h(l, 2, [tile_])
                else:
                    self.outproj(l, tl)
                if self.stage == 3 and l == 0:
                    break
            if not pn:
                self.norm_h(l, 2, tl)
            self.ffn(l, 2, tl, nxt_norm=(l + 1, 0) if (pn and l < last) else None)
        self.final()
        self.fw.wait_all("sp")
        self.fw.emit()
        return self.nc


def host_prep(inp, b):
    f32 = np.float32
    d = {}
    d["x_in"] = np.ascontiguousarray(np.concatenate([inp["ctx"][b], inp["x"][b]], axis=0), dtype=f32)
    cv = np.stack([inp["c"][b], inp["c_ctx"]], axis=-1)
    d["cvec"] = np.ascontiguousarray(cv.reshape(8, 128, 2).transpose(1, 0, 2), dtype=f32)
    d["ada_w"] = inp["ada_w"]
    d["ada_b_t"] = np.ascontiguousarray(inp["ada_b"].reshape(DEPTH, 72, 128).transpose(2, 0, 1), dtype=f32)
    d["norm_g_t"] = np.ascontiguousarray(inp["norm_g"].reshape(DEPTH, 3, 8, 128).transpose(3, 0, 1, 2), dtype=f32)
    d["fng_t"] = np.ascontiguousarray(inp["final_norm_g"].reshape(8, 128).T, dtype=f32)
    d["ffn_w1"] = inp["ffn_w1"]
    d["ffn_w2"] = inp["ffn_w2"]
    d["ident"] = np.eye(128, dtype=f32)
    d["win_ext"] = _WIN_EXT(inp)
    d["w_out"] = inp["w_out"]
    rc, rs = _ROPE()
    d["ropeC"], d["ropeS"] = rc, rs
    ii = np.arange(128)[:, None]
    jj = np.arange(128)[None, :]
    d["mprev"] = np.ascontiguousarray(np.tile(np.where(jj <= ii, 0.0, -240000.0).astype(f32), (1, 4)))
    d["mnext"] = np.ascontiguousarray(np.tile(np.where(ii <= jj, 0.0, -240000.0).astype(f32), (1, 4)))
    d["lb_t"] = np.ascontiguousarray(inp["hgrn_lower_bounds"].reshape(DEPTH, 2, 4, 64).transpose(3, 0, 1, 2).reshape(64, DEPTH, 8), dtype=f32)
    d["hng_t"] = np.ascontiguousarray(inp["hgrn_norm_g"].reshape(DEPTH, 4, 64).transpose(2, 0, 1), dtype=f32)
    rm = np.ones((64, 1024), f32)
    rm[:, ::32] = 0.0
    d["rmask"] = rm
    si = np.arange(64)[:, None]
    tj = np.arange(64)[None, :]
    d["hmask"] = np.ascontiguousarray(np.stack([(si <= tj), (si >= tj)], axis=1).astype(f32))
    d.update(_S5(inp))
    d["sink_t"] = np.ascontiguousarray(np.broadcast_to(inp["attn_sink"][None], (64, DEPTH, 8)), dtype=f32)
    return d


_HC = {}


def _WIN_EXT(inp):
    if "win" in _HC:
        return _HC["win"]
    w = inp["w_in"]
    aq, ai, af, ab, ag = (w[:, :, i * 256:(i + 1) * 256] for i in range(5))
    bq = w[:, :, 1280:1792]
    bk = w[:, :, 1792:1920]
    bv = w[:, :, 1920:2048]
    cu = w[:, :, 2048:2304]
    perm = np.arange(64).reshape(2, 2, 16)[:, ::-1, :].reshape(64)

    def partner(m):
        nh = m.shape[-1] // 64
        idx = (np.arange(nh)[:, None] * 64 + perm[None, :]).reshape(-1)
        return m[:, :, idx]

    bqp, bkp = partner(bq), partner(bk)
    cols = [aq, af, ab, ag]
    for j in range(4):
        cols += [bq[:, :, j * 128:(j + 1) * 128], bqp[:, :, j * 128:(j + 1) * 128]]
    cols += [bk, bkp, cu, ai, bv, cu]
    _HC["win"] = np.ascontiguousarray(np.concatenate(cols, axis=-1), dtype=np.float32)
    assert _HC["win"].shape[-1] == 3200
    return _HC["win"]


def _S5(inp):
    if "s5" in _HC:
        return _HC["s5"]
    f32 = np.float32
    L = DEPTH
    are, aim, ldt = inp["s5_a_re"], inp["s5_a_im"], inp["s5_log_dt"]

    def st_layout(a):
        return a.reshape(L, 2, 8, 2, 64).transpose(3, 4, 0, 1, 2).reshape(128, L, 16)

    ld_s = np.broadcast_to(ldt.reshape(L, 2, 8, 2, 1), (L, 2, 8, 2, 64)).transpose(3, 4, 0, 1, 2).reshape(128, L, 16)
    As = np.stack([st_layout(are), st_layout(aim), ld_s], axis=2)

    def q_layout(a):
        t = a.reshape(L, 2, 2, 8, 64).transpose(3, 0, 1, 2, 4)
        t = np.repeat(t[:, None], 16, axis=1)
        return t.reshape(128, L, 4, 64)

    Aq = np.stack([q_layout(are), q_layout(aim)], axis=2)
    ldq = np.repeat(ldt.reshape(L, 2, 2, 8).transpose(3, 0, 1, 2)[:, None], 16, axis=1).reshape(128, L, 4)

    def b_layout(b):
        return b.reshape(L, 2, 8, 64, 16).transpose(2, 4, 0, 1, 3).reshape(128, L, 2, 64)

    Bq = np.stack([b_layout(inp["s5_b_re"]), b_layout(inp["s5_b_im"])], axis=2)

    def c_layout(c):
        t = c.reshape(L, 2, 8, 2, 16, 64)
        out = np.zeros((2, 64, L, 2, 8, 2, 16), f32)
        for gl in range(2):
            out[gl, :, :, :, :, gl, :] = t[:, :, :, gl, :, :].transpose(4, 0, 1, 2, 3)
        return out.reshape(128, L, 16, 32)

    Cb = np.stack([c_layout(inp["s5_c_re"]), c_layout(inp["s5_c_im"])], axis=2)
    c = lambda a: np.ascontiguousarray(a, dtype=f32)
    _HC["s5"] = {
        "s5_As": c(As), "s5_Aq": c(Aq), "s5_LDq": c(ldq), "s5_Bq": c(Bq), "s5_Cblk": c(Cb),
        "s5_d_t": c(inp["s5_d"].reshape(L, 2, 128).transpose(2, 0, 1)),
        "glu_b_t": c(inp["s5_glu_b"].reshape(L, 4, 128).transpose(2, 0, 1)),
        "s5_glu_w": inp["s5_glu_w"],
        "rowmask": c(np.arange(128)[:, None] // 16 == np.arange(8)[None, :]),
        "Jmat": c(np.eye(128)[::-1]),
    }
    return _HC["s5"]


def _ROPE():
    if "rope" in _HC:
        return _HC["rope"]
    f32 = np.float32
    t = np.arange(NLAT)
    row = (t // 64).astype(f32)
    col = (t % 64).astype(f32)
    inv = (f32(10000.0) ** (-np.arange(16, dtype=f32) / f32(16))).astype(f32)
    ang = np.stack([row[:, None] * inv, col[:, None] * inv], axis=1).astype(f32)
    cos, sin = np.cos(ang).astype(f32), np.sin(ang).astype(f32)
    C = np.ones((64, NTOK), f32)
    S = np.zeros((64, NTOK), f32)
    for ax in range(2):
        for two in range(2):
            d0 = ax * 32 + two * 16
            C[d0:d0 + 16, NCTX:] = cos[:, ax, :].T
            S[d0:d0 + 16, NCTX:] = (-sin[:, ax, :].T if two == 0 else sin[:, ax, :].T)
    _HC["rope"] = (np.ascontiguousarray(np.tile(C, (2, 1))), np.ascontiguousarray(np.tile(S, (2, 1))))
    return _HC["rope"]


_NC_CACHE = {}


def kernel(**inputs):
    inp = {k: np.asarray(v) for k, v in inputs.items()}
    stage = int(os.environ.get("KSTAGE", "99"))
    ncores = int(os.environ.get("KCORES", "8"))
    if stage not in _NC_CACHE:
        _NC_CACHE[stage] = K(stage).build()
    nc = _NC_CACHE[stage]
    in_maps = [host_prep(inp, b) for b in range(ncores)]
    res = run_bass_kernel_spmd(nc, in_maps, core_ids=list(range(ncores)))
    out = np.stack([np.asarray(r["y"]) for r in res.results], axis=0)
    return out.astype(np.float32)
```
